# Optimizing a Trainium2 kernel written in Bass

```python
import math
import jax, jax.numpy as jnp
from jax import lax
import numpy as np

D_MODEL = 1024
BATCH = 32
SEQ = 256
DEPTH = 2
DEC_BATCH = 8
DEC_SEQ = 4096
PAST_LEN = 256

GRID_W = 64
HEAD_DIM = 64
A_HEADS = 8
A_KV_HEADS = 2
A_GROUP = A_HEADS // A_KV_HEADS
A_WIDTH = A_HEADS * HEAD_DIM
A_KV_WIDTH = A_KV_HEADS * HEAD_DIM
WINDOW = 128
ATTN_BLOCK = 128
ROPE_BASE = 10000.0
B_GROUPS = 4
B_GROUP_DIM = 64
B_WIDTH = B_GROUPS * B_GROUP_DIM
SGU_CHUNK = 128
C_HEADS = 4
C_HEAD_DIM = 64
C_WIDTH = C_HEADS * C_HEAD_DIM
CONV_K = 5
GDN_CHUNK = 64
N_DIRS = 2
D_MIX = A_WIDTH + B_WIDTH + C_WIDTH
IN_SPLITS = (A_WIDTH, A_KV_WIDTH, A_KV_WIDTH, A_WIDTH,
             B_WIDTH, B_WIDTH, B_WIDTH,
             C_WIDTH, C_WIDTH, C_WIDTH, N_DIRS * C_HEADS, N_DIRS * C_HEADS, C_WIDTH)
IN_COLS = sum(IN_SPLITS)
EPS = 1e-6
NEG_INF = -1e30

kernel_name = 'hybrid_flow_trunk_step'


def _rmsnorm(x, g):
    xf = x.astype(jnp.float32)
    y = xf * lax.rsqrt(jnp.mean(xf * xf, axis=-1, keepdims=True) + EPS)
    return (y * g.astype(jnp.float32)).astype(x.dtype)


def _layernorm(x, g, b):
    xf = x.astype(jnp.float32)
    xc = xf - jnp.mean(xf, axis=-1, keepdims=True)
    y = xc * lax.rsqrt(jnp.mean(xc * xc, axis=-1, keepdims=True) + EPS)
    return (y * g.astype(jnp.float32) + b.astype(jnp.float32)).astype(x.dtype)


def _l2norm(x):
    return x * lax.rsqrt(jnp.sum(x * x, axis=-1, keepdims=True) + EPS)


def _rope_2d(x):
    n = x.shape[1]
    rows = n // GRID_W
    row = jnp.repeat(jnp.arange(rows), GRID_W)
    col = jnp.tile(jnp.arange(GRID_W), rows)
    half = HEAD_DIM // 2
    nf = half // 2
    inv_freq = ROPE_BASE ** (-jnp.arange(nf, dtype=jnp.float32) / nf)
    xf = x.astype(jnp.float32)

    def rot(xa, pos):
        ang = pos.astype(jnp.float32)[:, None] * inv_freq[None, :]
        cos = jnp.cos(ang)[None, :, None, :]
        sin = jnp.sin(ang)[None, :, None, :]
        x1, x2 = xa[..., :nf], xa[..., nf:]
        return jnp.concatenate([x1 * cos - x2 * sin, x2 * cos + x1 * sin], axis=-1)

    out = jnp.concatenate([rot(xf[..., :half], row), rot(xf[..., half:], col)], axis=-1)
    return out.astype(x.dtype)


def _attend(q, k, v, mask, sink):
    s = jnp.einsum('bqhgd,bkhd->bhgqk', q, k).astype(jnp.float32) * (HEAD_DIM ** -0.5)
    s = jnp.where(mask, s, NEG_INF)
    sk = sink.astype(jnp.float32)[None, :, :, None, None]
    m = jnp.maximum(jnp.max(s, axis=-1, keepdims=True), sk)
    p = jnp.exp(s - m)
    p = p / (jnp.sum(p, axis=-1, keepdims=True) + jnp.exp(sk - m))
    return jnp.einsum('bhgqk,bkhd->bqhgd', p.astype(v.dtype), v)


def _context_attention(q, k, v, sink):
    b, s = q.shape[:2]
    nb = s // ATTN_BLOCK
    qb = jnp.moveaxis(q.reshape(b, nb, ATTN_BLOCK, A_KV_HEADS, A_GROUP, HEAD_DIM), 1, 0)
    mask = jnp.ones((ATTN_BLOCK, s), dtype=bool)
    o = lax.map(lambda qi: _attend(qi, k, v, mask, sink), qb)
    return jnp.moveaxis(o, 0, 1).reshape(b, s, A_WIDTH)


def _latent_attention(q, k, v, k_ctx, v_ctx, sink):
    b, n = q.shape[:2]
    nb = n // ATTN_BLOCK
    n_ctx = k_ctx.shape[1]
    pad = ((0, 0), (ATTN_BLOCK, ATTN_BLOCK), (0, 0), (0, 0))
    k_pad = jnp.pad(k, pad)
    v_pad = jnp.pad(v, pad)
    q_rel = jnp.arange(ATTN_BLOCK)
    k_rel = jnp.arange(3 * ATTN_BLOCK) - ATTN_BLOCK
    band = jnp.abs(q_rel[:, None] - k_rel[None, :]) <= WINDOW
    ctx_ok = jnp.ones((ATTN_BLOCK, n_ctx), dtype=bool)

    def block(i):
        start = i * ATTN_BLOCK
        qi = lax.dynamic_slice_in_dim(q, start, ATTN_BLOCK, axis=1)
        ki = lax.dynamic_slice_in_dim(k_pad, start, 3 * ATTN_BLOCK, axis=1)
        vi = lax.dynamic_slice_in_dim(v_pad, start, 3 * ATTN_BLOCK, axis=1)
        kpos = start + k_rel
        mask = band & ((kpos >= 0) & (kpos < n))[None, :]
        mask = jnp.concatenate([mask, ctx_ok], axis=1)
        keys = jnp.concatenate([ki, k_ctx.astype(ki.dtype)], axis=1)
        vals = jnp.concatenate([vi, v_ctx.astype(vi.dtype)], axis=1)
        return _attend(qi, keys, vals, mask, sink)

    o = lax.map(block, jnp.arange(nb))
    return jnp.moveaxis(o, 0, 1).reshape(b, n, A_WIDTH)


def _sgu(u, v, ln_g, ln_b, w_s, b_s):
    b, n = u.shape[:2]
    nc = n // SGU_CHUNK
    vg = _layernorm(v.reshape(b, n, B_GROUPS, B_GROUP_DIM),
                    ln_g.reshape(B_GROUPS, B_GROUP_DIM), ln_b.reshape(B_GROUPS, B_GROUP_DIM))
    vc = vg.reshape(b, nc, SGU_CHUNK, B_GROUPS, B_GROUP_DIM)
    mixed = jnp.einsum('gts,bnsgc->bntgc', w_s, vc) + jnp.swapaxes(b_s, 0, 1)[:, :, None]
    return u * mixed.reshape(b, n, B_WIDTH)


def _short_conv(x, w):
    ch = x.shape[-1]
    y = lax.conv_general_dilated(x, w.astype(x.dtype)[:, None, :], window_strides=(1,),
                                 padding=[(CONV_K // 2, CONV_K // 2)],
                                 dimension_numbers=('NWC', 'WIO', 'NWC'),
                                 feature_group_count=ch)
    return jax.nn.silu(y)


def _gdn_features(cq, ck, cv, ca, cb, conv_w, a_log, dt_bias):
    b, t = cq.shape[:2]
    qkv = _short_conv(jnp.concatenate([cq, ck, cv], axis=-1), conv_w).astype(jnp.float32)
    q, k, v = jnp.split(qkv, 3, axis=-1)
    q = _l2norm(q.reshape(b, t, C_HEADS, C_HEAD_DIM)) * (C_HEAD_DIM ** -0.5)
    k = _l2norm(k.reshape(b, t, C_HEADS, C_HEAD_DIM))
    v = v.reshape(b, t, C_HEADS, C_HEAD_DIM)
    a = ca.astype(jnp.float32).reshape(b, t, N_DIRS, C_HEADS)
    g = -jnp.exp(a_log.astype(jnp.float32)) * jax.nn.softplus(a + dt_bias.astype(jnp.float32))
    beta = jax.nn.sigmoid(cb.astype(jnp.float32).reshape(b, t, N_DIRS, C_HEADS))
    return q, k, v, g, beta


def _gdn_chunked(q, k, v, g, beta, s0):
    b, t, h, dk = q.shape
    dv = v.shape[-1]
    c = GDN_CHUNK
    n = t // c
    q = q.reshape(b, n, c, h, dk)
    k = k.reshape(b, n, c, h, dk)
    v = v.reshape(b, n, c, h, dv)
    beta = beta.reshape(b, n, c, h)
    gc = jnp.cumsum(g.reshape(b, n, c, h), axis=2)
    gch = jnp.moveaxis(gc, 3, 2)
    idx = jnp.arange(c)
    lower = idx[:, None] >= idx[None, :]
    strict = idx[:, None] > idx[None, :]
    decay = jnp.exp(jnp.where(lower, gch[..., :, None] - gch[..., None, :], NEG_INF))
    kb = k * beta[..., None]
    a_mat = jnp.where(strict, jnp.einsum('bnihd,bnjhd->bnhij', kb, k) * decay, 0.0)
    eye = jnp.eye(c, dtype=jnp.float32)
    t_mat = lax.linalg.triangular_solve(a_mat + eye, jnp.broadcast_to(eye, a_mat.shape),
                                        left_side=True, lower=True, unit_diagonal=True)
    u = jnp.einsum('bnhij,bnjhd->bnihd', t_mat, v * beta[..., None])
    w = jnp.einsum('bnhij,bnjhd->bnihd', t_mat, kb * jnp.exp(gc)[..., None])
    a_qk = jnp.einsum('bnihd,bnjhd->bnhij', q, k) * decay
    g_last = gc[:, :, -1]
    q_dec = q * jnp.exp(gc)[..., None]
    k_dec = k * jnp.exp(g_last[:, :, None, :] - gc)[..., None]

    def step(state, xs):
        qd, kd, uc, wc, aqk, gl = xs
        v_new = uc - jnp.einsum('bihk,bhkv->bihv', wc, state)
        o = jnp.einsum('bihk,bhkv->bihv', qd, state) + jnp.einsum('bhij,bjhv->bihv', aqk, v_new)
        state = state * jnp.exp(gl)[:, :, None, None] + jnp.einsum('bjhk,bjhv->bhkv', kd, v_new)
        return state, o

    xs = tuple(jnp.moveaxis(a, 1, 0) for a in (q_dec, k_dec, u, w, a_qk, g_last))
    s_fin, o = lax.scan(step, s0.astype(jnp.float32), xs)
    o = jnp.moveaxis(o, 0, 1).reshape(b, t, h, dv)
    return o, s_fin


def _gdn_bidir(q, k, v, g, beta, s0):
    flip = lambda a: jnp.flip(a, axis=1)
    o_f, s_f = _gdn_chunked(q, k, v, g[:, :, 0], beta[:, :, 0], s0[:, 0])
    o_b, s_b = _gdn_chunked(flip(q), flip(k), flip(v), flip(g[:, :, 1]), flip(beta[:, :, 1]), s0[:, 1])
    return o_f + flip(o_b), jnp.stack([s_f, s_b], axis=1)


def _gdn_output(o, norm_g, gate):
    b, t = o.shape[:2]
    o = _rmsnorm(o, norm_g).reshape(b, t, C_WIDTH).astype(gate.dtype)
    return o * jax.nn.silu(gate)


def _modulated_proj(x, cond, w_mod, b_mod, g_pre, w_in):
    mod = jax.nn.silu(cond) @ w_mod + b_mod
    shift, scale, gate = jnp.split(mod, 3, axis=-1)
    h = _rmsnorm(x, g_pre) * (1.0 + scale[:, None, :]) + shift[:, None, :]
    z = h @ w_in
    offs, acc = [], 0
    for width in IN_SPLITS[:-1]:
        acc += width
        offs.append(acc)
    return jnp.split(z, offs, axis=-1), gate


def _residual(x, outs, gate, w_out, g_post):
    mix = jnp.concatenate(outs, axis=-1) @ w_out
    return x + gate[:, None, :] * _rmsnorm(mix, g_post)


def _context_layer(x, c_ctx, prm):
    (w_mod, b_mod, g_pre, w_in, sink, ln_g, ln_b, sgu_w, sgu_b,
     conv_w, a_log, dt_bias, norm_g, g_post, w_out) = prm
    b, s = x.shape[:2]
    parts, gate = _modulated_proj(x, c_ctx[None, :], w_mod, b_mod, g_pre, w_in)
    aq, ak, av, ag, bu, bv, bg, cq, ck, cv, ca, cb, cg = parts
    q = aq.reshape(b, s, A_KV_HEADS, A_GROUP, HEAD_DIM)
    k = ak.reshape(b, s, A_KV_HEADS, HEAD_DIM)
    v = av.reshape(b, s, A_KV_HEADS, HEAD_DIM)
    o_a = _context_attention(q, k, v, sink.reshape(A_KV_HEADS, A_GROUP)) * jax.nn.silu(ag)
    o_b = _sgu(bu, bv, ln_g, ln_b, sgu_w, sgu_b) * jax.nn.silu(bg)
    q_c, k_c, v_c, g_c, beta_c = _gdn_features(cq, ck, cv, ca, cb, conv_w, a_log, dt_bias)
    s0 = jnp.zeros((b, N_DIRS, C_HEADS, C_HEAD_DIM, C_HEAD_DIM), jnp.float32)
    o_c, s_fin = _gdn_bidir(q_c, k_c, v_c, g_c, beta_c, s0)
    o_c = _gdn_output(o_c, norm_g, cg)
    return _residual(x, [o_a, o_b, o_c], gate, w_out, g_post), k, v, s_fin


def _latent_layer(x, c, prm, k_ctx, v_ctx, s_ctx):
    (w_mod, b_mod, g_pre, w_in, sink, ln_g, ln_b, sgu_w, sgu_b,
     conv_w, a_log, dt_bias, norm_g, g_post, w_out) = prm
    b, n = x.shape[:2]
    parts, gate = _modulated_proj(x, c, w_mod, b_mod, g_pre, w_in)
    aq, ak, av, ag, bu, bv, bg, cq, ck, cv, ca, cb, cg = parts
    q = _rope_2d(aq.reshape(b, n, A_HEADS, HEAD_DIM)).reshape(b, n, A_KV_HEADS, A_GROUP, HEAD_DIM)
    k = _rope_2d(ak.reshape(b, n, A_KV_HEADS, HEAD_DIM))
    v = av.reshape(b, n, A_KV_HEADS, HEAD_DIM)
    o_a = _latent_attention(q, k, v, k_ctx, v_ctx, sink.reshape(A_KV_HEADS, A_GROUP)) * jax.nn.silu(ag)
    o_b = _sgu(bu, bv, ln_g, ln_b, sgu_w, sgu_b) * jax.nn.silu(bg)
    q_c, k_c, v_c, g_c, beta_c = _gdn_features(cq, ck, cv, ca, cb, conv_w, a_log, dt_bias)
    o_c, _ = _gdn_bidir(q_c, k_c, v_c, g_c, beta_c, s_ctx)
    o_c = _gdn_output(o_c, norm_g, cg)
    return _residual(x, [o_a, o_b, o_c], gate, w_out, g_post)


def setup_inputs(seed: int = 0) -> dict:
    key = jax.random.key(seed)
    ks = jax.random.split(key, 24)
    f32 = jnp.float32

    def nrm(k, shape, s):
        return s * jax.random.normal(k, shape, f32)

    dt = jnp.exp(jax.random.uniform(ks[18], (DEPTH, N_DIRS, C_HEADS), f32,
                                    math.log(1e-3), math.log(1e-1)))
    return {
        'x_prompt': nrm(ks[0], (BATCH, SEQ, D_MODEL), 1.0),
        'x_sample': nrm(ks[1], (DEC_BATCH, DEC_SEQ, D_MODEL), 1.0),
        'cache_k': nrm(ks[2], (DEC_BATCH, DEPTH, PAST_LEN, A_KV_HEADS, HEAD_DIM), 1.0),
        'cache_v': nrm(ks[3], (DEC_BATCH, DEPTH, PAST_LEN, A_KV_HEADS, HEAD_DIM), 1.0),
        'state_delta': nrm(ks[4], (DEC_BATCH, DEPTH, N_DIRS, C_HEADS, C_HEAD_DIM, C_HEAD_DIM), 0.2),
        'c': nrm(ks[5], (DEC_BATCH, D_MODEL), 1.0),
        'c_ctx': nrm(ks[6], (D_MODEL,), 1.0),
        'w_mod': nrm(ks[7], (DEPTH, D_MODEL, 3 * D_MODEL), 0.5 * D_MODEL ** -0.5),
        'b_mod': nrm(ks[8], (DEPTH, 3 * D_MODEL), 0.01),
        'g_pre': 1.0 + nrm(ks[9], (DEPTH, D_MODEL), 0.05),
        'w_in': nrm(ks[10], (DEPTH, D_MODEL, IN_COLS), D_MODEL ** -0.5),
        'attn_sink': nrm(ks[11], (DEPTH, A_HEADS), 0.5),
        'sgu_ln_g': 1.0 + nrm(ks[12], (DEPTH, B_WIDTH), 0.05),
        'sgu_ln_b': nrm(ks[13], (DEPTH, B_WIDTH), 0.01),
        'sgu_w': nrm(ks[14], (DEPTH, B_GROUPS, SGU_CHUNK, SGU_CHUNK), 0.5 * SGU_CHUNK ** -0.5),
        'sgu_b': 1.0 + nrm(ks[15], (DEPTH, B_GROUPS, SGU_CHUNK), 0.05),
        'gdn_conv_w': nrm(ks[16], (DEPTH, CONV_K, 3 * C_WIDTH), CONV_K ** -0.5),
        'gdn_a_log': jnp.log(jax.random.uniform(ks[17], (DEPTH, N_DIRS, C_HEADS), f32, 1.0, 16.0)),
        'gdn_dt_bias': dt + jnp.log(-jnp.expm1(-dt)),
        'gdn_norm_g': 1.0 + nrm(ks[19], (DEPTH, C_HEAD_DIM), 0.05),
        'g_post': 1.0 + nrm(ks[20], (DEPTH, D_MODEL), 0.05),
        'w_out': nrm(ks[21], (DEPTH, D_MIX, D_MODEL), D_MIX ** -0.5),
    }


def reference(x_prompt, x_sample, cache_k, cache_v, state_delta, c, c_ctx,
              w_mod, b_mod, g_pre, w_in, attn_sink, sgu_ln_g, sgu_ln_b, sgu_w, sgu_b,
              gdn_conv_w, gdn_a_log, gdn_dt_bias, gdn_norm_g, g_post, w_out):
    xp = x_prompt
    xs = x_sample
    new_k, new_v, new_s = [], [], []
    for l in range(DEPTH):
        prm = (w_mod[l], b_mod[l], g_pre[l], w_in[l], attn_sink[l], sgu_ln_g[l], sgu_ln_b[l],
               sgu_w[l], sgu_b[l], gdn_conv_w[l], gdn_a_log[l], gdn_dt_bias[l], gdn_norm_g[l],
               g_post[l], w_out[l])
        xp, k_l, v_l, s_l = _context_layer(xp, c_ctx, prm)
        xs = _latent_layer(xs, c, prm, cache_k[:, l], cache_v[:, l], state_delta[:, l])
        new_k.append(k_l)
        new_v.append(v_l)
        new_s.append(s_l)
    new_cache_k = jnp.stack(new_k, axis=1)
    new_cache_v = jnp.stack(new_v, axis=1)
    new_state_delta = jnp.stack(new_s, axis=1)
    return (xp, xs, new_cache_k, new_cache_v, new_state_delta)
```

```python
import contextlib
import math
import os

import numpy as np
import concourse.bass as bass
import concourse.mybir as mybir
from concourse.bass_utils import run_bass_kernel_spmd

F32 = mybir.dt.float32
BF16 = mybir.dt.bfloat16
AF = mybir.ActivationFunctionType
ALU = mybir.AluOpType
AX = mybir.AxisListType

D = 1024
NPROMPT = 4
TP = 256
TS = 4096
NTOK = NPROMPT * TP + TS
RT = 4096
RCH = RT // 64
EPS = 1e-6
NEGBIG = -30000.0
A_CA, A_CG = 768, 784
WA_COLS = 1040
SEM_EPOCH = 20000
STRICT_SAME_ENGINE = True
NSC = 6
SC_GC, SC_GB, SC_EGC, SC_BETA, SC_BEXP, SC_EGLMGC = range(6)


class Prog:
    QUEUES = ("pe", "act", "dve", "pool", "sp")

    def __init__(self, nc, need=None):
        self.nc = nc
        self.dry = need is None
        self.need_in = need
        self.need = {}
        self.state = {}
        self.ops = []
        self.val = []
        self.cnt = {}
        self.waited = {q: {} for q in self.QUEUES}
        self.sems = {}
        self.es = None
        self.n_wait = 0
        if not self.dry:
            self.eng = {"pe": nc.tensor, "act": nc.scalar, "dve": nc.vector, "pool": nc.gpsimd, "sp": nc.sync}

    def sem(self, name):
        s = self.sems.get(name)
        if s is None:
            s = self.es.enter_context(self.nc.semaphore(name))
            self.sems[name] = s
        return s

    def op(self, q, fn, r=(), w=(), dma=None, extra=(), rg=0):
        idx = len(self.ops)
        deps = {}
        pr = [k for k in r if isinstance(k, tuple) and k[0] == "ps"]
        if pr:
            r = [k for k in r if k not in pr]
            w = list(w) + pr
        for k in r:
            st = self.state.get(k)
            if st is not None and st[0] is not None:
                deps[st[0]] = True
        for k in w:
            st = self.state.get(k)
            if st is not None:
                if st[0] is not None:
                    deps.setdefault(st[0], False)
                for x in st[1].values():
                    deps.setdefault(x, False)
        for x in extra:
            deps[x] = True
        rkey = ("dma", dma) if dma is not None else q
        for k in r:
            self.state.setdefault(k, [None, {}])[1][rkey] = idx
        for k in w:
            self.state[k] = [idx, {}]
        deps.pop(idx, None)
        kept = []
        for d, raw in deps.items():
            pq, pdma, prg = self.ops[d]
            if pdma is None and dma is None and pq == q:
                if q == "pe" and prg == rg:
                    continue
                if q != "pe" and not raw and not STRICT_SAME_ENGINE:
                    continue
            kept.append(d)
        self.ops.append((q, dma, rg))
        if self.dry:
            for d in kept:
                self.need[d] = True
            self.val.append(None)
            return idx
        e = self.eng[q]
        ws = {}
        for d in kept:
            s, v = self.val[d]
            if ws.get(s, 0) < v:
                ws[s] = v
        for s, v in ws.items():
            if self.waited[q].get(s, 0) >= v:
                continue
            e.wait_ge(self.sem(s), v)
            self.waited[q][s] = v
            self.n_wait += 1
        ins = fn(e)
        if dma is not None:
            c = self.cnt.get(dma, 0) + 16
            self.cnt[dma] = c
            ins.then_inc(self.sem(dma), 16)
            self.val.append((dma, c))
        elif self.need_in.get(idx):
            c = self.cnt.get(q, 0) + 1
            self.cnt[q] = c
            name = "%s_e%d" % (q, (c - 1) // SEM_EPOCH)
            ins.then_inc(self.sem(name), 1)
            self.val.append((name, (c - 1) % SEM_EPOCH + 1))
        else:
            self.val.append(None)
        return idx

    def barrier(self):
        last = {}
        for i, (q, dma, _) in enumerate(self.ops):
            last[("dma", dma) if dma is not None else q] = i
        ids = list(last.values())
        for q in self.QUEUES:
            self.op(q, lambda e: e.nop(), extra=ids)

    def finish(self):
        if self.dry:
            return
        last = {}
        for v in self.val:
            if v is not None:
                last[v[0]] = max(last.get(v[0], 0), v[1])
        for s, v in last.items():
            if self.waited["sp"].get(s, 0) < v:
                self.nc.sync.wait_ge(self.sem(s), v)


class StopBuild(Exception):
    pass


class Tile:
    def __init__(self, h, shape):
        self.h = h
        self.shape = list(shape)
        self.fs = int(np.prod(shape[1:]))

    def __getitem__(self, k):
        return self.h[k]

    def ap(self, off, dims, parts=None, pbase=0):
        n = self.shape[0] if parts is None else parts
        return bass.AP(self.h, pbase * self.fs + off, [[self.fs, n]] + [list(d) for d in dims])


D3 = [[256, 2], [64, 4], [1, 64]]


class Builder:
    def __init__(self, nc, prog, dbg):
        self.nc = nc
        self.p = prog
        self.dbg = dbg
        self.dbg_out = {}
        self.uid = 0
        self.tileA = 0
        self.stA = 0
        self.tileC = 0
        self.kstop = int(os.environ.get("KSTOP", "99"))

    def stage(self, n):
        if n >= self.kstop:
            self.stopped = True
        return getattr(self, "stopped", False)

    def sb(self, es, name, shape, dt=F32):
        self.uid += 1
        h = es.enter_context(self.nc.sbuf_tensor("%s_%d" % (name, self.uid), list(shape), dt))
        return Tile(h, shape)

    def dram(self, name, shape, kind, dt=F32):
        return self.nc.dram_tensor(name, list(shape), dt, kind=kind)

    def dump(self, name, ap_fn, shape, rkeys):
        if not self.dbg or name in self.dbg_out:
            return
        t = self.dram("dbg_" + name, shape, "ExternalOutput")
        self.dbg_out[name] = t
        self.p.op("pool", lambda e: e.dma_start(out=t.ap(), in_=ap_fn()), r=rkeys, dma="dbg_" + name)

    def psf(self, b, off, dims, parts=128):
        return bass.AP(self.PSALL, b * 512 + off, [[4096, parts]] + [list(d) for d in dims])

    def psb(self, b, off, dims, parts=128):
        return bass.AP(self.PSALLB, b * 1024 + off, [[8192, parts]] + [list(d) for d in dims])

    def build(self):
        nc, p = self.nc, self.p
        dr = self.dram
        self.xp = dr("xp", [NPROMPT, TP, D], "ExternalInput")
        self.xs = dr("xs", [TS, D], "ExternalInput")
        self.ck = dr("ck", [2, 256, 128], "ExternalInput")
        self.cv = dr("cv", [2, 256, 128], "ExternalInput")
        self.sd = dr("sd", [2, 2, 4, 64, 64], "ExternalInput")
        self.cvec = dr("cvec", [2, D], "ExternalInput")
        self.w_mod = dr("w_mod", [2, D, 3 * D], "ExternalInput")
        self.b_mod = dr("b_mod", [2, 3 * D], "ExternalInput")
        self.g_pre = dr("g_pre", [2, D], "ExternalInput")
        self.w_in = dr("w_in", [2, D, 3088], "ExternalInput")
        self.sink = dr("sink", [2, 8], "ExternalInput")
        self.ln_g = dr("ln_g", [2, 256], "ExternalInput")
        self.ln_b = dr("ln_b", [2, 256], "ExternalInput")
        self.sgu_w = dr("sgu_w", [2, 4, 128, 128], "ExternalInput")
        self.sgu_b = dr("sgu_b", [2, 4, 128], "ExternalInput")
        self.conv_w = dr("conv_w", [2, 5, 768], "ExternalInput")
        self.a_log = dr("a_log", [2, 8], "ExternalInput")
        self.dt_bias = dr("dt_bias", [2, 8], "ExternalInput")
        self.norm_g = dr("norm_g", [2, 64], "ExternalInput")
        self.g_post = dr("g_post", [2, D], "ExternalInput")
        self.w_out = dr("w_out", [2, D, D], "ExternalInput")
        self.yp = dr("yp", [NPROMPT, TP, D], "ExternalOutput")
        self.ys = dr("ys", [TS, D], "ExternalOutput")
        self.nk = dr("nk", [NPROMPT, 2, 256, 128], "ExternalOutput")
        self.nv = dr("nv", [NPROMPT, 2, 256, 128], "ExternalOutput")
        self.ns = dr("ns", [NPROMPT, 2, 2, 4, 64, 64], "ExternalOutput")
        self.xp1 = dr("xp1", [NPROMPT, TP, D], "Internal")
        self.xs1 = dr("xs1", [TS, D], "Internal")
        self.gps = dr("gps", [2, 2, D], "Internal")
        self.cgD = dr("cgD", [128, 2, NTOK], "Internal", BF16)

        with contextlib.ExitStack() as es:
            p.es = es
            self.PSALL = es.enter_context(nc.psum_tensor("psall", [128, 4096], F32))
            self.PSALLB = self.PSALL.bitcast(BF16)
            self.consts(es)
            if not self.stage(1):
                for l in range(2):
                    if not self.stage(0):
                        self.layer(l)
            p.finish()

    def consts(self, es):
        p, sb = self.p, self.sb
        self.identf = sb(es, "identf", [128, 128])
        self.identb = sb(es, "identb", [128, 128], BF16)
        self.nidentf = sb(es, "nidentf", [64, 64])
        self.ones64 = sb(es, "ones64", [64, 128])
        self.Uf = sb(es, "Uf", [64, 64])
        self.Ub = sb(es, "Ub", [64, 64])
        self.NEG = sb(es, "NEG", [64, 2, 8, 64], BF16)
        self.amask = sb(es, "amask", [128, 2, 128], BF16)
        self.onesblk = sb(es, "onesblk", [128, 128], BF16)
        self.epsc = sb(es, "epsc", [128, 1])
        self.onecol = sb(es, "onecol", [128, 1], BF16)
        self.sel = sb(es, "sel", [2, 2, 128])
        self.blkm = sb(es, "blkm", [64, 8, 64], BF16)
        with contextlib.ExitStack() as ts:
            tmpf = sb(ts, "ctmpf", [128, 1024])
            idf, idb = self.identf, self.identb
            p.op("pool", lambda e: e.memset(idf[:], 1.0), w=["identf"])
            p.op("pool", lambda e: e.affine_select(out=idf[:], in_=idf[:], pattern=[[-1, 128]], compare_op=ALU.is_equal,
                                                   fill=0.0, base=0, channel_multiplier=1), r=["identf"], w=["identf"])
            p.op("pool", lambda e: e.tensor_copy(out=idb[:], in_=idf[:]), r=["identf"], w=["identb"])
            p.op("pool", lambda e: e.tensor_scalar(out=self.nidentf[:], in0=idf[0:64, 0:64], scalar1=-1.0, scalar2=None,
                                                   op0=ALU.mult), r=["identf"], w=["nidentf"])
            p.op("pool", lambda e: e.memset(self.ones64[:], 1.0), w=["ones64"])
            p.op("pool", lambda e: e.memset(self.epsc[:], EPS), w=["epsc"])
            p.op("pool", lambda e: e.memset(self.onecol[:], 1.0), w=["onecol"])
            p.op("pool", lambda e: e.memset(self.Uf[:], 1.0), w=["Uf"])
            p.op("pool", lambda e: e.affine_select(out=self.Uf[:], in_=self.Uf[:], pattern=[[1, 64]], compare_op=ALU.is_ge,
                                                   fill=0.0, base=0, channel_multiplier=-1), r=["Uf"], w=["Uf"])
            p.op("pool", lambda e: e.memset(self.Ub[:], 1.0), w=["Ub"])
            p.op("pool", lambda e: e.affine_select(out=self.Ub[:], in_=self.Ub[:], pattern=[[-1, 64]], compare_op=ALU.is_ge,
                                                   fill=0.0, base=0, channel_multiplier=1), r=["Ub"], w=["Ub"])
            p.op("pool", lambda e: e.memset(tmpf[0:64, :], 0.0), w=["ctmpf"])
            for which in range(2):
                for d in range(2):
                    sgn = 1 if d == 0 else -1
                    base = -1 if which == 0 else 0
                    o = (which * 8 + d * 4) * 64

                    def f(e, o=o, sgn=sgn, base=base):
                        v = tmpf.ap(o, [[64, 4], [1, 64]], parts=64)
                        return e.affine_select(out=v, in_=v, pattern=[[0, 4], [sgn, 64]], compare_op=ALU.is_ge,
                                               fill=NEGBIG, base=base, channel_multiplier=-sgn)
                    p.op("pool", f, r=["ctmpf"], w=["ctmpf"])
            p.op("pool", lambda e: e.tensor_copy(out=self.NEG[:].rearrange("p a b c -> p (a b c)"), in_=tmpf[0:64, :]),
                 r=["ctmpf"], w=["NEG"])
            p.op("pool", lambda e: e.memset(tmpf[0:64, 0:512], 1.0), r=["NEG"], w=["ctmpf"])
            p.op("pool", lambda e: e.affine_select(out=tmpf.ap(0, [[64, 8], [1, 64]], parts=32),
                                                   in_=tmpf.ap(0, [[64, 8], [1, 64]], parts=32), pattern=[[0, 8], [-1, 64]],
                                                   compare_op=ALU.is_ge, fill=0.0, base=31, channel_multiplier=0),
                 r=["ctmpf"], w=["ctmpf"])
            p.op("pool", lambda e: e.affine_select(out=tmpf.ap(0, [[64, 8], [1, 64]], parts=32, pbase=32),
                                                   in_=tmpf.ap(0, [[64, 8], [1, 64]], parts=32, pbase=32),
                                                   pattern=[[0, 8], [1, 64]], compare_op=ALU.is_ge, fill=0.0, base=-32,
                                                   channel_multiplier=0), r=["ctmpf"], w=["ctmpf"])
            p.op("pool", lambda e: e.tensor_copy(out=self.blkm[:].rearrange("p a b -> p (a b)"), in_=tmpf[0:64, 0:512]),
                 r=["ctmpf"], w=["blkm"])
            p.op("pool", lambda e: e.memset(tmpf[:, 0:256], 1.0), r=["blkm"], w=["ctmpf"])
            p.op("pool", lambda e: e.affine_select(out=tmpf[:, 0:128], in_=tmpf[:, 0:128], pattern=[[-1, 128]],
                                                   compare_op=ALU.is_ge, fill=0.0, base=0, channel_multiplier=1),
                 r=["ctmpf"], w=["ctmpf"])
            p.op("pool", lambda e: e.affine_select(out=tmpf[:, 128:256], in_=tmpf[:, 128:256], pattern=[[1, 128]],
                                                   compare_op=ALU.is_ge, fill=0.0, base=0, channel_multiplier=-1),
                 r=["ctmpf"], w=["ctmpf"])
            p.op("pool", lambda e: e.tensor_copy(out=self.amask[:].rearrange("p a b -> p (a b)"), in_=tmpf[:, 0:256]),
                 r=["ctmpf"], w=["amask"])
            p.op("pool", lambda e: e.memset(self.onesblk[:], 0.0), w=["onesblk"])
            p.op("pool", lambda e: e.memset(self.onesblk[0:64, 0:64], 1.0), r=["onesblk"], w=["onesblk"])
            p.op("pool", lambda e: e.memset(self.onesblk[64:128, 64:128], 1.0), r=["onesblk"], w=["onesblk"])
            p.op("pool", lambda e: e.memset(self.sel[:], 1.0), w=["sel"])
            p.op("pool", lambda e: e.affine_select(out=self.sel[:, 0, :], in_=self.sel[:, 0, :], pattern=[[0, 128]],
                                                   compare_op=ALU.is_ge, fill=0.0, base=0, channel_multiplier=-1),
                 r=["sel"], w=["sel"])
            p.op("pool", lambda e: e.affine_select(out=self.sel[:, 1, :], in_=self.sel[:, 1, :], pattern=[[0, 128]],
                                                   compare_op=ALU.is_ge, fill=0.0, base=-1, channel_multiplier=1),
                 r=["sel"], w=["sel"])
            p.barrier()

    def rope_tables(self, es):
        p, sb = self.p, self.sb
        self.ropeC = sb(es, "ropeC", [128, 32, 64])
        self.ropeS = sb(es, "ropeS", [128, 32, 64])
        with contextlib.ExitStack() as ts:
            posr = sb(ts, "posr", [128, 32])
            posc = sb(ts, "posc", [128, 1])
            invf = sb(ts, "invf", [128, 16])
            ang = sb(ts, "ang", [128, 33, 16])
            sn = [sb(ts, "sn%d" % i, [128, 33, 16]) for i in range(2)]
            cs = [sb(ts, "cs%d" % i, [128, 33, 16]) for i in range(2)]
            tq = sb(ts, "tq", [128, 33, 16])
            hpi = sb(ts, "hpi", [128, 1])
            p.op("pool", lambda e: e.iota(posr[:], [[2, 32]], base=0, channel_multiplier=0,
                                          allow_small_or_imprecise_dtypes=True), w=["posr"])
            p.op("pool", lambda e: e.tensor_scalar(out=posr[64:128, :], in0=posr[64:128, :], scalar1=1.0, scalar2=None,
                                                   op0=ALU.add), r=["posr"], w=["posr"])
            p.op("pool", lambda e: e.iota(posc[:], [[0, 1]], base=0, channel_multiplier=1,
                                          allow_small_or_imprecise_dtypes=True), w=["posc"])
            p.op("pool", lambda e: e.tensor_scalar(out=posc[64:128, :], in0=posc[64:128, :], scalar1=-64.0, scalar2=None,
                                                   op0=ALU.add), r=["posc"], w=["posc"])
            p.op("pool", lambda e: e.iota(invf[:], [[1, 16]], base=0, channel_multiplier=0,
                                          allow_small_or_imprecise_dtypes=True), w=["invf"])
            p.op("pool", lambda e: e.memset(hpi[:], 0.5 * math.pi), w=["hpi"])
            p.op("act", lambda e: e.activation(out=invf[:], in_=invf[:], func=AF.Exp, scale=-math.log(10000.0) / 16.0),
                 r=["invf"], w=["invf"])
            p.op("dve", lambda e: e.tensor_tensor(out=ang[:, 0:32, :], in0=posr.ap(0, [[1, 32], [0, 16]]),
                                                  in1=invf.ap(0, [[0, 32], [1, 16]]), op=ALU.mult),
                 r=["posr", "invf"], w=["ang"])
            p.op("dve", lambda e: e.tensor_scalar(out=ang[:, 32, :], in0=invf[:], scalar1=posc[:, 0:1], scalar2=None,
                                                  op0=ALU.mult), r=["posc", "invf", "ang"], w=["ang"])
            p.op("act", lambda e: e.activation(out=sn[0][:], in_=ang[:], func=AF.Sin, scale=1.0 / 64), r=["ang"], w=["sn0"])
            p.op("act", lambda e: e.activation(out=cs[0][:], in_=ang[:], func=AF.Sin, scale=1.0 / 64, bias=hpi[:, 0:1]),
                 r=["ang", "hpi"], w=["cs0"])
            for it in range(6):
                a, b = it % 2, (it + 1) % 2
                p.op("dve", lambda e, a=a: e.tensor_tensor(out=tq[:], in0=sn[a][:], in1=sn[a][:], op=ALU.mult),
                     r=["sn%d" % a], w=["tq"])
                p.op("dve", lambda e, a=a, b=b: e.scalar_tensor_tensor(out=sn[b][:], in0=sn[a][:], scalar=2.0, in1=cs[a][:],
                                                                       op0=ALU.mult, op1=ALU.mult),
                     r=["sn%d" % a, "cs%d" % a], w=["sn%d" % b])
                p.op("dve", lambda e, b=b: e.tensor_scalar(out=cs[b][:], in0=tq[:], scalar1=-2.0, scalar2=1.0, op0=ALU.mult,
                                                           op1=ALU.add), r=["tq"], w=["cs%d" % b])
            snf, csf = sn[0], cs[0]
            C, S = self.ropeC, self.ropeS
            colv = lambda t: t.ap(32 * 16, [[0, 32], [1, 16]])
            for blk in range(2):
                p.op("pool", lambda e, blk=blk: e.tensor_copy(out=C[:, :, blk * 16:(blk + 1) * 16], in_=csf[:, 0:32, :]),
                     r=["cs0"], w=["ropeC"])
                p.op("pool", lambda e, blk=blk: e.tensor_copy(out=C[:, :, 32 + blk * 16:48 + blk * 16], in_=colv(csf)),
                     r=["cs0"], w=["ropeC"])
            p.op("pool", lambda e: e.tensor_scalar(out=S[:, :, 0:16], in0=snf[:, 0:32, :], scalar1=-1.0, scalar2=None,
                                                   op0=ALU.mult), r=["sn0"], w=["ropeS"])
            p.op("pool", lambda e: e.tensor_copy(out=S[:, :, 16:32], in_=snf[:, 0:32, :]), r=["sn0"], w=["ropeS"])
            p.op("pool", lambda e: e.tensor_scalar(out=S[:, :, 32:48], in0=colv(snf), scalar1=-1.0, scalar2=None,
                                                   op0=ALU.mult), r=["sn0"], w=["ropeS"])
            p.op("pool", lambda e: e.tensor_copy(out=S[:, :, 48:64], in_=colv(snf)), r=["sn0"], w=["ropeS"])
            self.dump("ropeC", lambda: C[:], [128, 32, 64], ["ropeC"])
            self.dump("ropeS", lambda: S[:], [128, 32, 64], ["ropeS"])
            p.barrier()

    def seqs(self, rnd=None):
        out = []
        for i in range(NPROMPT):
            out.append(dict(kind="p", idx=i, T=TP, toff=i * TP, rtoff=i * TP, rc0=i * (TP // 64), cond=0, ST=256, rnd=0))
        out.append(dict(kind="s", idx=0, T=TS, toff=NPROMPT * TP, rtoff=0, rc0=0, cond=1, ST=512, rnd=1))
        if rnd is not None:
            out = [s for s in out if s["rnd"] == rnd]
        return out

    def x_src(self, l, sq, t0, n=128):
        if sq["kind"] == "p":
            t = self.xp if l == 0 else self.xp1
            return t.ap()[sq["idx"], t0:t0 + n, :]
        t = self.xs if l == 0 else self.xs1
        return t.ap()[t0:t0 + n, :]

    def x_dst(self, l, sq, t0, n=128):
        if sq["kind"] == "p":
            t = self.xp1 if l == 0 else self.yp
            return t.ap()[sq["idx"], t0:t0 + n, :]
        t = self.xs1 if l == 0 else self.ys
        return t.ap()[t0:t0 + n, :]

    def xkey(self, l, sq, t0):
        return ("xd", l, sq["kind"], sq["idx"], t0)

    def layer(self, l):
        p = self.p
        with contextlib.ExitStack() as esL:
            self.layer_params(esL, l)
            if self.stage(2):
                return
            self.ocT = self.sb(esL, "ocT", [128, 2, NTOK], BF16)
            for rnd in range(2):
                if self.stage(0):
                    return
                with contextlib.ExitStack() as esAB:
                    self.KQVT = self.sb(esAB, "KQVT", [128, 3, 2, RT], BF16)
                    self.SC = self.sb(esAB, "SC", [64, NSC, 2, RCH, 4])
                    self.EGL = self.sb(esAB, "EGL", [128, 2, RCH, 4])
                    self.abtok = self.sb(esAB, "abtok", [64, RCH, 16])
                    with contextlib.ExitStack() as esA:
                        self.phaseA_setup(esA, l)
                        if not self.stage(3):
                            for sq in self.seqs(rnd):
                                self.phaseA(l, sq)
                            if l == 0 and rnd == 0:
                                self.dump("KQVT", lambda: self.KQVT[:, :, :, 0:1024], [128, 3, 2, 1024], ["KQVT"])
                                self.dump("SC", lambda: self.SC[:, :, :, 0:16, :], [64, NSC, 2, 16, 4], ["SC"])
                                self.dump("abtok", lambda: self.abtok[:, 0:16, :], [64, 16, 16], ["abtok"])
                        p.barrier()
                    if self.stage(4):
                        continue
                    with contextlib.ExitStack() as esB:
                        self.phaseB_setup(esB, l)
                        for sq in self.seqs(rnd):
                            self.phaseB(l, sq)
                        p.barrier()
                    self.stage(5 + rnd)
            if self.stage(0):
                return
            with contextlib.ExitStack() as esC:
                self.phaseC_setup(esC, l)
                if not self.stage(7):
                    for sq in self.seqs():
                        self.phaseC(l, sq)
                p.barrier()
            self.stage(8)

    def layer_params(self, es, l):
        p, sb = self.p, self.sb
        self.modT = sb(es, "modT", [128, 2, 8, 2])
        self.esink = sb(es, "esink", [128, 8])
        self.lngb = sb(es, "lngb", [128, 2, 256])
        self.WsT = sb(es, "WsT", [128, 4, 128], BF16)
        self.bsT = sb(es, "bsT", [128, 4])
        self.cwT = sb(es, "cwT", [128, 6, 5])
        self.dtb = sb(es, "dtb", [64, 8])
        self.nea = sb(es, "nea", [64, 8])
        self.normg = sb(es, "normg", [128, 1])
        self.kTctx = sb(es, "kTctx", [64, 2, 256], BF16)
        self.vctx = sb(es, "vctx", [128, 2, 2, 64], BF16)
        modT = self.modT
        with contextlib.ExitStack() as ts:
            cv2 = sb(ts, "cv2", [2, D])
            scv = sb(ts, "scv", [2, D])
            scvT = sb(ts, "scvT", [128, 8, 2])
            wst = [sb(ts, "wmst%d" % i, [128, 3 * D]) for i in range(2)]
            bm2 = sb(ts, "bm2", [2, 3 * D])
            gp2 = sb(ts, "gp2", [2, 2, D])
            modsb = sb(ts, "modsb", [2, 3 * D])
            rows = sb(ts, "rows", [2, 3, D])
            p.op("sp", lambda e: e.dma_start(out=cv2[:], in_=self.cvec.ap()), w=["cv2"], dma="ld_cv2")
            p.op("act", lambda e: e.activation(out=scv[:], in_=cv2[:], func=AF.Silu), r=["cv2"], w=["scv"])
            for kc in range(8):
                p.op("pe", lambda e, kc=kc: e.transpose(self.psf(7, kc * 2, [[1, 2]]), scv[0:2, kc * 128:(kc + 1) * 128],
                                                        self.identf[0:2, 0:2]), r=["scv", "identf"], w=[("ps", 7)])
            p.op("dve", lambda e: e.tensor_copy(out=scvT[:].rearrange("p a b -> p (a b)"), in_=self.psf(7, 0, [[1, 16]])),
                 r=[("ps", 7)], w=["scvT"])
            for i in range(2):
                p.op("sp", lambda e, i=i: e.dma_start(out=bm2[i:i + 1, :], in_=self.b_mod.ap()[l:l + 1, :]), w=["bm2"],
                     dma="ld_bm2")
                p.op("sp", lambda e, i=i: e.dma_start(out=gp2[i:i + 1, 0, :], in_=self.g_pre.ap()[l:l + 1, :]), w=["gp2"],
                     dma="ld_gp2")
                p.op("sp", lambda e, i=i: e.dma_start(out=gp2[i:i + 1, 1, :], in_=self.g_post.ap()[l:l + 1, :]), w=["gp2"],
                     dma="ld_gp2")
            for kc in range(8):
                st = wst[kc % 2]
                p.op("sp", lambda e, kc=kc, st=st: e.dma_start(out=st[:], in_=self.w_mod.ap()[l, kc * 128:(kc + 1) * 128, :]),
                     w=[("wmst", kc % 2)], dma="ld_wm%d" % (kc % 2))
                for n in range(6):
                    p.op("pe", lambda e, kc=kc, st=st, n=n: e.matmul(self.psf(n, 0, [[1, 512]], parts=2), lhsT=scvT[:, kc, :],
                                                                     rhs=st[:, n * 512:(n + 1) * 512], start=(kc == 0),
                                                                     stop=(kc == 7)),
                         r=["scvT", ("wmst", kc % 2)], w=[("ps", n)])
            for n in range(6):
                p.op("dve", lambda e, n=n: e.tensor_tensor(out=modsb[:, n * 512:(n + 1) * 512],
                                                           in0=self.psf(n, 0, [[1, 512]], parts=2),
                                                           in1=bm2[:, n * 512:(n + 1) * 512], op=ALU.add),
                     r=[("ps", n), "bm2"], w=["modsb"])
            p.op("dve", lambda e: e.scalar_tensor_tensor(out=rows[:, 0, :], in0=modsb[:, D:2 * D], scalar=1.0, in1=gp2[:, 0, :],
                                                         op0=ALU.add, op1=ALU.mult), r=["modsb", "gp2"], w=["rows"])
            p.op("dve", lambda e: e.tensor_copy(out=rows[:, 1, :], in_=modsb[:, 0:D]), r=["modsb"], w=["rows"])
            p.op("dve", lambda e: e.tensor_tensor(out=rows[:, 2, :], in0=modsb[:, 2 * D:3 * D], in1=gp2[:, 1, :], op=ALU.mult),
                 r=["modsb", "gp2"], w=["rows"])
            p.op("sp", lambda e: e.dma_start(out=self.gps.ap()[l], in_=rows[:, 2, :]), r=["rows"], w=[("gps", l)],
                 dma="st_gps")
            for a in range(2):
                for kc in range(8):
                    p.op("pe", lambda e, a=a, kc=kc: e.transpose(self.psf(7, (a * 8 + kc) * 2, [[1, 2]]),
                                                                 rows[0:2, a, kc * 128:(kc + 1) * 128], self.identf[0:2, 0:2]),
                         r=["rows", "identf"], w=[("ps", 7)])
            p.op("dve", lambda e: e.tensor_copy(out=modT[:].rearrange("p a b c -> p (a b c)"), in_=self.psf(7, 0, [[1, 32]])),
                 r=[("ps", 7)], w=["modT"])
            self.dump("modT%d" % l, lambda: modT[:], [128, 2, 8, 2], ["modT"])

            stg = sb(ts, "stg", [128, 4, 128])
            p.op("sp", lambda e: e.dma_start(out=self.esink[:], in_=bass.AP(self.sink, l * 8, [[0, 128], [1, 8]])),
                 w=["esink"], dma="ld_sink")
            p.op("act", lambda e: e.activation(out=self.esink[:], in_=self.esink[:], func=AF.Exp), r=["esink"], w=["esink"])
            p.op("sp", lambda e: e.dma_start(out=self.lngb[:, 0, :], in_=bass.AP(self.ln_g, l * 256, [[0, 128], [1, 256]])),
                 w=["lngb"], dma="ld_lng")
            p.op("sp", lambda e: e.dma_start(out=self.lngb[:, 1, :], in_=bass.AP(self.ln_b, l * 256, [[0, 128], [1, 256]])),
                 w=["lngb"], dma="ld_lng")
            p.op("sp", lambda e: e.dma_start(out=self.dtb[:], in_=bass.AP(self.dt_bias, l * 8, [[0, 64], [1, 8]])),
                 w=["dtb"], dma="ld_dtb")
            p.op("sp", lambda e: e.dma_start(out=self.nea[:], in_=bass.AP(self.a_log, l * 8, [[0, 64], [1, 8]])),
                 w=["nea"], dma="ld_nea")
            p.op("act", lambda e: e.activation(out=self.nea[:], in_=self.nea[:], func=AF.Exp), r=["nea"], w=["nea"])
            p.op("dve", lambda e: e.tensor_scalar(out=self.nea[:], in0=self.nea[:], scalar1=-1.0, scalar2=None, op0=ALU.mult),
                 r=["nea"], w=["nea"])
            for hf in range(2):
                p.op("sp", lambda e, hf=hf: e.dma_start(out=self.normg[hf * 64:(hf + 1) * 64, :],
                                                        in_=bass.AP(self.norm_g, l * 64, [[1, 64], [1, 1]])),
                     w=["normg"], dma="ld_normg")
            p.op("sp", lambda e: e.dma_start(out=stg[:], in_=self.sgu_w.ap()[l].rearrange("g t s -> t g s")), w=["stg"],
                 dma="ld_stg")
            for g in range(4):
                p.op("pe", lambda e, g=g: e.transpose(self.psf(0, g * 128, [[1, 128]]), stg[:, g, :], self.identf[:]),
                     r=["stg", "identf"], w=[("ps", 0)])
            p.op("act", lambda e: e.activation(out=self.WsT[:].rearrange("p a b -> p (a b)"), in_=self.psf(0, 0, [[1, 512]]),
                                               func=AF.Copy), r=[("ps", 0)], w=["WsT"])
            stg2 = sb(ts, "stg2", [8, 768])
            p.op("sp", lambda e: e.dma_start(out=stg2[0:4, 0:128], in_=self.sgu_b.ap()[l]), w=["stg2"], dma="ld_stg2")
            p.op("pe", lambda e: e.transpose(self.psf(1, 0, [[1, 4]]), stg2[0:4, 0:128], self.identf[0:4, 0:4]),
                 r=["stg2", "identf"], w=[("ps", 1)])
            p.op("dve", lambda e: e.tensor_copy(out=self.bsT[:], in_=self.psf(1, 0, [[1, 4]])), r=[("ps", 1)], w=["bsT"])
            p.op("sp", lambda e: e.dma_start(out=stg2[0:5, :], in_=self.conv_w.ap()[l]), w=["stg2"], dma="ld_stg2")
            for b in range(6):
                p.op("pe", lambda e, b=b: e.transpose(self.psf(1, 8 + b * 5, [[1, 5]]), stg2[0:5, b * 128:(b + 1) * 128],
                                                      self.identf[0:5, 0:5]), r=["stg2", "identf"], w=[("ps", 1)])
            p.op("dve", lambda e: e.tensor_copy(out=self.cwT[:].rearrange("p a b -> p (a b)"), in_=self.psf(1, 8, [[1, 30]])),
                 r=[("ps", 1)], w=["cwT"])
            ckf = sb(ts, "ckf", [128, 2, 2, 128])
            kcb = sb(ts, "kcb", [128, 2, 2, 64], BF16)
            p.op("sp", lambda e: e.dma_start(out=ckf[:, 0], in_=self.ck.ap()[l].rearrange("(b p) c -> p b c", p=128)),
                 w=["ckf"], dma="ld_ckf")
            p.op("sp", lambda e: e.dma_start(out=ckf[:, 1], in_=self.cv.ap()[l].rearrange("(b p) c -> p b c", p=128)),
                 w=["ckf"], dma="ld_ckf")
            p.op("dve", lambda e: e.tensor_copy(out=kcb[:], in_=ckf[:, 0].rearrange("p b (k d) -> p b k d", k=2)),
                 r=["ckf"], w=["kcb"])
            p.op("dve", lambda e: e.tensor_copy(out=self.vctx[:], in_=ckf[:, 1].rearrange("p b (k d) -> p b k d", k=2)),
                 r=["ckf"], w=["vctx"])
            for blk in range(2):
                for kv in range(2):
                    p.op("pe", lambda e, blk=blk, kv=kv: e.transpose(
                        self.psb(2, (kv * 2 + blk) * 128, [[1, 128]], parts=64), kcb[:, blk, kv, :], self.identb[:]),
                        r=["kcb", "identb"], w=[("ps", 2)])
            p.op("act", lambda e: e.activation(out=self.kTctx[:].rearrange("p a b -> p (a b)"),
                                               in_=self.psb(2, 0, [[1, 512]], parts=64), func=AF.Copy), r=[("ps", 2)],
                 w=["kTctx"])
            p.barrier()

    def make_hT(self, l, sq, t0, hT, col0, slot):
        g = self.make_hT_gen(l, sq, t0, hT, col0, slot)
        try:
            while True:
                next(g)
        except StopIteration as e:
            return e.value

    def make_hT_gen(self, l, sq, t0, hT, col0, slot):
        p = self.p
        xt = self.xin[slot]
        c = sq["cond"]
        xk = ("xin", slot)
        src = self.x_src(l, sq, t0)
        p.op("sp", lambda e: e.dma_start(out=xt[:], in_=src), r=[self.xkey(l, sq, t0)], w=[xk], dma="ld_xin%d" % slot)
        junk, st = self.junk, self.stat
        sk = ("stat", slot)
        yield
        p.op("act", lambda e: e.activation(out=junk[:], in_=xt[:], func=AF.Square, accum_out=st[:, slot, 0:1]),
             r=[xk], w=["junk", sk])
        p.op("act", lambda e: e.activation(out=st[:, slot, 1:2], in_=st[:, slot, 0:1], func=AF.Ln, scale=1.0 / D,
                                           bias=self.epsc[:, 0:1]), r=[sk, "epsc"], w=[sk])
        p.op("act", lambda e: e.activation(out=st[:, slot, 2:3], in_=st[:, slot, 1:2], func=AF.Exp, scale=-0.5),
             r=[sk], w=[sk])
        yield
        xn = self.xn
        p.op("act", lambda e: e.activation(out=xn[:], in_=xt[:], func=AF.Identity, scale=st[:, slot, 2:3]),
             r=[xk, sk], w=["xn"])
        yield
        for kc in range(8):
            p.op("pe", lambda e, kc=kc: e.transpose(self.psb(6, kc * 128, [[1, 128]]), xn[:, kc * 128:(kc + 1) * 128],
                                                    self.identb[:]), r=["xn", "identb"], w=[("ps", 6)])
        yield
        mt = self.modT
        hk = ("hT", id(hT), col0)
        for kc in range(8):
            sc_ap = lambda kc=kc: mt.ap(kc * 2 + c, [[1, 1]])
            sh_ap = lambda kc=kc: mt.ap(16 + kc * 2 + c, [[1, 1]])
            if False:
                p.op("act", lambda e, kc=kc, sc_ap=sc_ap, sh_ap=sh_ap: e.activation(
                    out=hT[:, kc, col0:col0 + 128], in_=self.psb(6, kc * 128, [[1, 128]]), func=AF.Identity,
                    scale=sc_ap(), bias=sh_ap()), r=[("ps", 6), "modT"], w=[hk])
            else:
                p.op("dve", lambda e, kc=kc, sc_ap=sc_ap, sh_ap=sh_ap: e.tensor_scalar(
                    out=hT[:, kc, col0:col0 + 128], in0=self.psb(6, kc * 128, [[1, 128]]), scalar1=sc_ap(), scalar2=sh_ap(),
                    op0=ALU.mult, op1=ALU.add), r=[("ps", 6), "modT"], w=[hk])
        yield
        return hk

    def phaseA_setup(self, es, l):
        p, sb = self.p, self.sb
        self.wA = sb(es, "wA", [128, 8, WA_COLS], BF16)
        for kc in range(8):
            p.op("pool", lambda e, kc=kc: e.dma_start(out=self.wA[:, kc, :],
                                                      in_=self.w_in.ap()[l, kc * 128:(kc + 1) * 128, 2048:3088]),
                 w=["wA"], dma="ld_wA")
        self.convdiag = sb(es, "convdiag", [128, 30, 128], BF16)
        p.op("dve", lambda e: e.tensor_tensor(out=self.convdiag[:], in0=self.identb.ap(0, [[0, 30], [1, 128]]),
                                              in1=self.cwT.ap(0, [[1, 30], [0, 128]]), op=ALU.mult),
             r=["cwT", "identb"], w=["convdiag"])
        self.xin = [sb(es, "xinA%d" % i, [128, D]) for i in range(2)]
        self.junk = sb(es, "junkA", [128, D], BF16)
        self.stat = sb(es, "statA", [128, 2, 4])
        self.xn = sb(es, "xnA", [128, D], BF16)
        self.hTtmp = sb(es, "hTtmpA", [128, 8, 128])
        self.hTA = [sb(es, "hTA%d" % i, [128, 8, 512], BF16) for i in range(2)]
        self.convin = [sb(es, "convin%d" % i, [128, 6, 516], BF16) for i in range(2)]
        self.cgst = [sb(es, "cgst%d" % i, [128, 2, 512], BF16) for i in range(2)]
        self.ysil = sb(es, "ysil", [128, 512])
        self.sqb = sb(es, "sqb", [128, 512], BF16)
        self.rn = sb(es, "rn", [128, 512])
        self.scA = sb(es, "scA", [64, 3, 2, 64, 4])

    def phaseA(self, l, sq):
        ST, T = sq["ST"], sq["T"]
        NS = T // ST
        NTS = ST // 128
        hks = {}

        def H(s):
            hT = self.hTA[s % 2]
            ks = []
            for j in range(NTS):
                hk = yield from self.make_hT_gen(l, sq, s * ST + j * 128, hT, j * 128, self.tileA % 2)
                self.tileA += 1
                ks.append(hk)
            hks[s] = ks

        def run(gens):
            gens = [g for g in gens if g is not None]
            while gens:
                for g in list(gens):
                    try:
                        next(g)
                    except StopIteration:
                        gens.remove(g)

        run([H(0)])
        for s in range(NS + 1):
            run([self.phaseA_P(l, sq, s, NS, hks.get(s)), H(s + 1) if s + 1 < NS else None])
        self.gdn_scalars(sq)

    def phaseA_P(self, l, sq, s, NS, hks):
        p = self.p
        ST, toff = sq["ST"], sq["toff"]
        wA = self.wA
        if s < NS:
            t0 = s * ST
            hT = self.hTA[s % 2]
            cin = self.convin[s % 2]
            ck_ = ("convin", s % 2)
            cslot = self.stA % 2
            self.stA += 1
            cgs = self.cgst[cslot]
            for cb in range(8):
                col = cb * 128 if cb < 6 else A_CG + (cb - 6) * 128
                b = cb % 4
                for kc in range(8):
                    p.op("pe", lambda e, kc=kc, col=col, b=b, hT=hT: e.matmul(
                        self.psf(b, 0, [[1, ST]]), lhsT=wA[:, kc, col:col + 128], rhs=hT[:, kc, 0:ST],
                        start=(kc == 0), stop=(kc == 7)), r=["wA"] + hks, w=[("ps", b)])
                yield
                if cb < 6 and cb % 2 == 1:
                    p.op("dve", lambda e, cb=cb, b=b, cin=cin: e.tensor_copy(out=cin[:, cb, 2:2 + ST],
                                                                             in_=self.psf(b, 0, [[1, ST]])),
                         r=[("ps", b)], w=[ck_])
                elif cb < 6:
                    p.op("act", lambda e, cb=cb, b=b, cin=cin: e.activation(out=cin[:, cb, 2:2 + ST],
                                                                            in_=self.psf(b, 0, [[1, ST]]), func=AF.Copy),
                         r=[("ps", b)], w=[ck_])
                else:
                    p.op("act", lambda e, cb=cb, b=b, cgs=cgs: e.activation(out=cgs[:, cb - 6, 0:ST],
                                                                            in_=self.psf(b, 0, [[1, ST]]), func=AF.Silu),
                         r=[("ps", b)], w=[("cgst", cslot)])
            g0 = toff + t0
            p.op("sp", lambda e, cgs=cgs, g0=g0: e.dma_start(out=self.cgD.ap()[:, :, g0:g0 + ST], in_=cgs[:, :, 0:ST]),
                 r=[("cgst", cslot)], w=[("cgD", g0)], dma="st_cg%d" % cslot)
            nch = ST // 64
            for c in range(nch):
                for kc in range(8):
                    p.op("pe", lambda e, c=c, kc=kc, hT=hT: e.matmul(
                        self.psf(5, c * 16, [[1, 16]], parts=64), lhsT=hT[:, kc, c * 64:(c + 1) * 64],
                        rhs=wA[:, kc, A_CA:A_CA + 16], start=(kc == 0), stop=(kc == 7)), r=["wA"] + hks, w=[("ps", 5)])
            yield
            ch0 = sq["rc0"] + s * nch
            p.op("dve", lambda e, ch0=ch0, nch=nch: e.tensor_copy(
                out=self.abtok[:, ch0:ch0 + nch, :], in_=self.psf(5, 0, [[16, nch], [1, 16]], parts=64)),
                r=[("ps", 5)], w=["abtok"])
            if s == 0:
                p.op("pool", lambda e, cin=cin: e.memset(cin[:, :, 0:2], 0.0), w=[ck_])
            else:
                prev = self.convin[(s - 1) % 2]
                pk = ("convin", (s - 1) % 2)
                p.op("pool", lambda e, cin=cin, prev=prev: e.tensor_copy(out=cin[:, :, 0:2], in_=prev[:, :, ST:ST + 2]),
                     r=[pk], w=[ck_])
                p.op("pool", lambda e, cin=cin, prev=prev: e.tensor_copy(out=prev[:, :, ST + 2:ST + 4], in_=cin[:, :, 2:4]),
                     r=[ck_], w=[pk])
            if s == NS - 1:
                p.op("pool", lambda e, cin=cin: e.memset(cin[:, :, ST + 2:ST + 4], 0.0), w=[ck_])
            yield
        if s >= 1:
            yield from self.conv_tile(sq, s - 1, ST)

    def conv_tile(self, sq, s, ST):
        p = self.p
        cin = self.convin[s % 2]
        ck_ = ("convin", s % 2)
        g0 = sq["rtoff"] + s * ST
        for cb in range(6):
            b = 4 + (cb % 2) * 3
            for j in range(5):
                p.op("pe", lambda e, cb=cb, j=j, b=b: e.matmul(
                    self.psf(b, 0, [[1, ST]]), lhsT=self.convdiag[:, cb * 5 + j, :], rhs=cin[:, cb, j:j + ST],
                    start=(j == 0), stop=(j == 4)), r=[ck_, "convdiag"], w=[("ps", b)])
            yield
            if cb >= 4:
                p.op("act", lambda e, cb=cb, b=b: e.activation(out=self.KQVT[:, 2, cb - 4, g0:g0 + ST],
                                                               in_=self.psf(b, 0, [[1, ST]]), func=AF.Silu),
                     r=[("ps", b)], w=["KQVT"])
                continue
            which = 1 if cb < 2 else 0
            hp = cb % 2
            ys, sqb, rn = self.ysil, self.sqb, self.rn
            p.op("act", lambda e, b=b: e.activation(out=ys[:, 0:ST], in_=self.psf(b, 0, [[1, ST]]), func=AF.Silu),
                 r=[("ps", b)], w=["ysil"])
            p.op("act", lambda e: e.activation(out=sqb[:, 0:ST], in_=ys[:, 0:ST], func=AF.Square), r=["ysil"], w=["sqb"])
            yield
            p.op("pe", lambda e: e.matmul(self.psf(3, 0, [[1, ST]]), lhsT=self.onesblk[:], rhs=sqb[:, 0:ST], start=True,
                                          stop=True), r=["sqb", "onesblk"], w=[("ps", 3)])
            yield
            p.op("act", lambda e: e.activation(out=rn[:, 0:ST], in_=self.psf(3, 0, [[1, ST]]), func=AF.Ln,
                                               bias=self.epsc[:, 0:1]), r=[("ps", 3), "epsc"], w=["rn"])
            p.op("act", lambda e: e.activation(out=rn[:, 0:ST], in_=rn[:, 0:ST], func=AF.Exp, scale=-0.5), r=["rn"], w=["rn"])
            sc = 0.125 if which == 1 else 1.0
            p.op("dve", lambda e, which=which, hp=hp, sc=sc: e.scalar_tensor_tensor(
                out=self.KQVT[:, which, hp, g0:g0 + ST], in0=ys[:, 0:ST], scalar=sc, in1=rn[:, 0:ST], op0=ALU.mult,
                op1=ALU.mult), r=["ysil", "rn"], w=["KQVT"])

    def gdn_scalars(self, sq):
        p = self.p
        NCH = sq["T"] // 64
        c0 = sq["rc0"]
        sc, SC, EGL = self.scA, self.SC, self.EGL
        F = 64 * 4
        n = NCH * 4

        def scv(i, d=None):
            if d is None:
                return sc.ap(i * 2 * F, [[F, 2], [4, NCH], [1, 4]])
            return sc.ap((i * 2 + d) * F, [[1, n]])

        def SCv(q):
            return SC.ap(((q * 2) * RCH + c0) * 4, [[RCH * 4, 2], [4, NCH], [1, 4]])

        ab = self.abtok
        a_in = lambda: ab.ap(c0 * 16, [[4, 2], [16, NCH], [1, 4]])
        b_in = lambda: ab.ap(c0 * 16 + 8, [[4, 2], [16, NCH], [1, 4]])
        dtb_bc = lambda: self.dtb.ap(0, [[4, 2], [0, NCH], [1, 4]])
        nea_bc = lambda: self.nea.ap(0, [[4, 2], [0, NCH], [1, 4]])
        p.op("dve", lambda e: e.tensor_tensor(out=scv(0), in0=a_in(), in1=dtb_bc(), op=ALU.add), r=["abtok", "dtb"], w=["sc0"])
        p.op("act", lambda e: e.activation(out=scv(0), in_=scv(0), func=AF.Exp), r=["sc0"], w=["sc0"])
        p.op("act", lambda e: e.activation(out=scv(0), in_=scv(0), func=AF.Ln, bias=1.0), r=["sc0"], w=["sc0"])
        p.op("dve", lambda e: e.tensor_tensor(out=scv(0), in0=scv(0), in1=nea_bc(), op=ALU.mult), r=["sc0", "nea"], w=["sc0"])
        p.op("act", lambda e: e.activation(out=scv(1), in_=b_in(), func=AF.Exp, scale=-1.0), r=["abtok"], w=["sc1"])
        p.op("act", lambda e: e.activation(out=scv(1), in_=scv(1), func=AF.Ln, bias=1.0), r=["sc1"], w=["sc1"])
        p.op("pe", lambda e: e.matmul(self.psf(0, 0, [[1, n]], parts=64), lhsT=self.Uf[:], rhs=scv(0, 0), start=True, stop=True),
             r=["sc0", "Uf"], w=[("ps", 0)])
        p.op("pe", lambda e: e.matmul(self.psf(0, n, [[1, n]], parts=64), lhsT=self.Ub[:], rhs=scv(0, 1), start=True, stop=True),
             r=["sc0", "Ub"], w=[("ps", 0)])
        p.op("pe", lambda e: e.matmul(self.psf(1, 0, [[n, 2], [1, n]]), lhsT=self.ones64[:],
                                      rhs=sc.ap(0, [[F, 2], [1, n]]), start=True, stop=True), r=["sc0", "ones64"], w=[("ps", 1)])
        ps_gc = lambda: self.psf(0, 0, [[n, 2], [4, NCH], [1, 4]], parts=64)
        ps_gl = lambda parts: self.psf(1, 0, [[n, 2], [4, NCH], [1, 4]], parts=parts)
        p.op("act", lambda e: e.activation(out=SCv(SC_GC), in_=ps_gc(), func=AF.Copy), r=[("ps", 0)], w=["SC"])
        p.op("dve", lambda e: e.tensor_tensor(out=SCv(SC_GB), in0=ps_gc(), in1=scv(1), op=ALU.subtract),
             r=[("ps", 0), "sc1"], w=["SC"])
        p.op("act", lambda e: e.activation(out=SCv(SC_EGC), in_=ps_gc(), func=AF.Exp), r=[("ps", 0)], w=["SC"])
        p.op("act", lambda e: e.activation(out=SCv(SC_BETA), in_=scv(1), func=AF.Exp, scale=-1.0), r=["sc1"], w=["SC"])
        p.op("act", lambda e: e.activation(out=SCv(SC_BEXP), in_=SCv(SC_GB), func=AF.Exp), r=["SC"], w=["SC"])
        p.op("act", lambda e: e.activation(out=EGL.ap(c0 * 4, [[RCH * 4, 2], [4, NCH], [1, 4]]), in_=ps_gl(128), func=AF.Exp),
             r=[("ps", 1)], w=["EGL"])
        p.op("dve", lambda e: e.tensor_tensor(out=scv(2), in0=ps_gl(64), in1=SCv(SC_GC), op=ALU.subtract),
             r=[("ps", 1), "SC"], w=["sc2"])
        p.op("act", lambda e: e.activation(out=SCv(SC_EGLMGC), in_=scv(2), func=AF.Exp), r=["sc2"], w=["SC"])

    def phaseB_setup(self, es, l):
        sb = self.sb
        self.S = sb(es, "S", [128, 8, 64])
        self.Sbf = sb(es, "Sbf", [128, 8, 64], BF16)
        self.St = sb(es, "St", [128, 8, 64])
        NB = 2
        self.NB = NB
        mk = lambda name, shape, dt=F32: [sb(es, "%s%d" % (name, i), shape, dt) for i in range(NB)]
        self.E12 = mk("E12", [64, 2, 8, 64])
        self.CBt = mk("CBt", [64, 8, 64])
        self.D12 = mk("D12", [64, 2, 8, 64])
        self.Bm = mk("Bm", [64, 6, 8, 64], BF16)
        self.Am = mk("Am", [64, 5, 8, 64], BF16)
        self.Aqk = mk("Aqk", [64, 8, 64], BF16)
        self.R0b = mk("R0b", [64, 8, 128], BF16)
        self.Qm = mk("Qm", [64, 8, 64], BF16)
        self.R = mk("R", [64, 8, 128], BF16)
        self.KD = mk("KD", [64, 8, 128], BF16)
        self.wT = mk("wT", [64, 8, 64], BF16)
        self.vnew = mk("vnew", [64, 8, 64], BF16)
        self.o1 = mk("o1", [64, 8, 64])
        self.ob = mk("ob", [64, 8, 64], BF16)
        self.sqf = sb(es, "sqf", [128, 512], BF16)
        self.rnf = sb(es, "rnf", [128, 512])
        self.cgl = [sb(es, "cgl%d" % i, [128, 2, 512], BF16) for i in range(2)]
        self.octmp = sb(es, "octmp", [128, 512])

    def sc_step(self, q, sq, s, inner, tile=None):
        NCH = sq["T"] // 64
        c0 = sq["rc0"]
        dstride = RCH * 4 + (NCH - 1 - 2 * s) * 4
        if tile is None:
            return self.SC.ap(((q * 2) * RCH + c0 + s) * 4, [[dstride, 2], [1, 4], [0, inner]])
        return tile.ap((c0 + s) * 4, [[dstride, 2], [1, 4], [0, inner]])

    def phaseB(self, l, sq):
        p = self.p
        NCH = sq["T"] // 64
        S, Sbf = self.S, self.Sbf
        if sq["kind"] == "p":
            p.op("pool", lambda e: e.memset(S[:], 0.0), w=["S"])
            p.op("pool", lambda e: e.memset(Sbf[:], 0.0), w=["Sbf"])
        else:
            for hf in range(2):
                p.op("sp", lambda e, hf=hf: e.dma_start(out=S[hf * 64:(hf + 1) * 64],
                                                        in_=self.sd.ap()[l].rearrange("d h k v -> k (d h) v")), w=["S"],
                     dma="ld_S")
            p.op("act", lambda e: e.activation(out=Sbf[:], in_=S[:], func=AF.Copy), r=["S"], w=["Sbf"])
        active, nxt = [], 0
        self.scan_done = 0
        while nxt < NCH or active:
            if len(active) < 2 and nxt < NCH:
                active.append(self.gdn_step(sq, nxt, nxt % 2))
                nxt += 1
            for g in list(active):
                try:
                    next(g)
                except StopIteration:
                    active.remove(g)
        if sq["kind"] == "p":
            p.op("sp", lambda e: e.dma_start(out=self.ns.ap()[sq["idx"], l].rearrange("d h k v -> k (d h) v"), in_=S[0:64]),
                 r=["S"], dma="st_S")
        self.gdn_finish(sq)

    def ps2(self, b, off):
        return bass.AP(self.PSALL, b * 512 + off, [[4096, 64], [512, 2], [128, 4], [1, 64]])

    def ps2f(self, b):
        return bass.AP(self.PSALL, b * 512, [[4096, 64], [512, 2], [1, 512]])

    def gdn_step(self, sq, s, bset):
        p = self.p
        NCH = sq["T"] // 64
        cf, cbk = s, NCH - 1 - s
        tk = [sq["rtoff"] + cf * 64, sq["rtoff"] + cbk * 64]
        tg = [sq["toff"] + cf * 64, sq["toff"] + cbk * 64]
        i = bset
        ba = 4 * bset
        bb, bc, bd = ba + 1, ba + 2, ba + 3
        K = self.KQVT
        kq = lambda name: (name, i)
        identf, identb = self.identf, self.identb

        def kap(which, h, d, dims):
            return K.ap((which * 2 + h // 2) * RT + tk[d], dims, parts=64, pbase=(h % 2) * 64)

        def ps2(off):
            return bass.AP(self.PSALL, ba * 512 + off, [[4096, 64], [512, 2], [128, 4], [1, 64]])

        def ps2f(b):
            return bass.AP(self.PSALL, b * 512, [[4096, 64], [512, 2], [1, 512]])

        for d in range(2):
            for h in (0, 2, 1, 3):
                p.op("pe", lambda e, d=d, h=h: e.matmul(
                    self.psf(ba + d, h * 128, [[64, 2], [1, 64]], parts=64), lhsT=kap(0, h, d, [[1, 64]]),
                    rhs=kap(0, h, d, [[2 * RT, 2], [1, 64]]), start=True, stop=True), r=["KQVT"], w=[("ps", ba + d)],
                    rg=(h % 2) * 64)
        E, CB, Dm = self.E12[i], self.CBt[i], self.D12[i]
        for which, q in ((0, SC_GB), (1, SC_GC)):
            p.op("dve", lambda e, which=which, q=q: e.tensor_tensor(
                out=E.ap(which * 512, D3), in0=identf.ap(0, [[0, 2], [0, 4], [1, 64]], parts=64),
                in1=self.sc_step(q, sq, s, 64), op=ALU.mult), r=["SC", "identf"], w=[kq("E12")])
        p.op("act", lambda e: e.activation(out=CB.ap(0, D3), in_=self.sc_step(SC_GC, sq, s, 64), func=AF.Copy),
             r=["SC"], w=[kq("CBt")])
        yield
        for which in range(2):
            b = bc + which
            p.op("pe", lambda e, which=which, b=b: e.matmul(self.psf(b, 0, [[1, 512]], parts=64), lhsT=identb[0:64, 0:64],
                                                            rhs=self.NEG[:, which].rearrange("p a b -> p (a b)"),
                                                            start=True, stop=False), r=["NEG", "identb"], w=[("ps", b)])
            p.op("pe", lambda e, which=which, b=b: e.matmul(self.psf(b, 0, [[1, 512]], parts=64), lhsT=self.ones64[:, 0:64],
                                                            rhs=E.ap(which * 512, [[1, 512]]), start=False, stop=False),
                 r=[kq("E12"), "ones64"], w=[("ps", b)])
            p.op("pe", lambda e, which=which, b=b: e.matmul(self.psf(b, 0, [[1, 512]], parts=64), lhsT=self.nidentf[:],
                                                            rhs=CB.ap(0, [[1, 512]]), start=False, stop=True),
                 r=[kq("CBt"), "nidentf"], w=[("ps", b)])
        yield
        for which in range(2):
            b = bc + which
            p.op("act", lambda e, which=which, b=b: e.activation(out=Dm.ap(which * 512, [[1, 512]]),
                                                                 in_=self.psf(b, 0, [[1, 512]], parts=64), func=AF.Exp),
                 r=[("ps", b)], w=[kq("D12")])
        yield
        Bm, Am, Aqk = self.Bm[i], self.Am[i], self.Aqk[i]
        p.op("dve", lambda e: e.scalar_tensor_tensor(out=Bm.ap(0, D3), in0=ps2(0), scalar=-1.0, in1=Dm.ap(0, D3),
                                                     op0=ALU.mult, op1=ALU.mult),
             r=[("ps", ba), ("ps", bb), kq("D12")], w=[kq("Bm")])
        p.op("dve", lambda e: e.tensor_tensor(out=Aqk.ap(0, D3), in0=ps2(64), in1=Dm.ap(512, D3), op=ALU.mult),
             r=[("ps", ba), ("ps", bb), kq("D12")], w=[kq("Aqk")])
        yield
        p.op("dve", lambda e: e.tensor_tensor(out=Bm.ap(5 * 512, [[1, 512]]), in0=Bm.ap(0, [[1, 512]]),
                                              in1=self.blkm[:].rearrange("p a b -> p (a b)"), op=ALU.mult),
             r=[kq("Bm"), "blkm"], w=[kq("Bm")])
        p.op("dve", lambda e: e.tensor_tensor(out=Bm.ap(0, [[1, 512]]), in0=Bm.ap(0, [[1, 512]]),
                                              in1=Bm.ap(5 * 512, [[1, 512]]), op=ALU.subtract), r=[kq("Bm")], w=[kq("Bm")])
        for h in (0, 2, 1, 3):
            for d in range(2):
                pb = (h % 2) * 64
                for slot_, wi in ((0, 2), (1, 0)):
                    p.op("pe", lambda e, d=d, h=h, pb=pb, slot_=slot_, wi=wi: e.transpose(
                        self.psb(bb, slot_ * 512 + (d * 4 + h) * 64, [[1, 64]], parts=64), kap(wi, h, d, [[1, 64]]),
                        identb[pb:pb + 64, pb:pb + 64]), r=["KQVT", "identb"], w=[("ps", bb)], rg=pb)
        yield
        lv = lambda k: 5 if k == 0 else k
        for q_ in range(8):
            p.op("pe", lambda e, q_=q_: e.transpose(self.psb(ba, q_ * 64, [[1, 64]], parts=64),
                                                    Bm.ap((5 * 8 + q_) * 64, [[1, 64]]), identb[0:64, 0:64]),
                 r=[kq("Bm"), "identb"], w=[("ps", ba)])
        R, R0, KD = self.R[i], self.R0b[i], self.KD[i]
        pv = lambda slot_: self.psb(bb, slot_ * 512, D3, parts=64)
        p.op("dve", lambda e: e.tensor_tensor(out=R0.ap(0, [[512, 2], [128, 4], [1, 64]]), in0=pv(0),
                                              in1=self.sc_step(SC_BETA, sq, s, 64), op=ALU.mult),
             r=[("ps", bb), "SC"], w=[kq("R0")])
        p.op("dve", lambda e: e.tensor_tensor(out=R0.ap(64, [[512, 2], [128, 4], [1, 64]]), in0=pv(1),
                                              in1=self.sc_step(SC_BEXP, sq, s, 64), op=ALU.mult),
             r=[("ps", bb), "SC"], w=[kq("R0")])
        yield
        p.op("act", lambda e: e.activation(out=Am.ap(0, [[1, 512]]), in_=self.psb(ba, 0, [[1, 512]], parts=64), func=AF.Copy),
             r=[("ps", ba)], w=[kq("Am")])
        for dup in range(2):
            p.op("dve", lambda e, dup=dup: e.tensor_tensor(out=KD.ap(dup * 64, [[512, 2], [128, 4], [1, 64]]), in0=pv(1),
                                                           in1=self.sc_step(SC_EGLMGC, sq, s, 64), op=ALU.mult),
                 r=[("ps", bb), "SC"], w=[kq("KD")])
        yield
        evac_ctr = [0]
        Qm = self.Qm[i]

        def apply(lhs_fn, rhs_t, rkeys, first):
            for q_ in range(8):
                p.op("pe", lambda e, q_=q_: e.matmul(
                    self.psf(bc + q_ // 4, (q_ % 4) * 128, [[1, 128]], parts=64), lhsT=lhs_fn(q_),
                    rhs=rhs_t.ap(q_ * 128, [[1, 128]]), start=(first and q_ % 4 == 0), stop=True, skip_group_check=True),
                    r=rkeys, w=[("ps", bc), ("ps", bd)])

        def evac():
            evac_ctr[0] += 1
            if evac_ctr[0] % 2 == 1:
                p.op("act", lambda e: e.activation(out=R.ap(0, [[512, 2], [1, 512]]), in_=ps2f(bc), func=AF.Copy),
                     r=[("ps", bc), ("ps", bd)], w=[kq("R")])
            else:
                p.op("dve", lambda e: e.tensor_copy(out=R.ap(0, [[512, 2], [1, 512]]), in_=ps2f(bc)),
                     r=[("ps", bc), ("ps", bd)], w=[kq("R")])

        bslot = lambda slot: (lambda q_: Bm.ap((slot * 8 + q_) * 64, [[1, 64]]))
        ident_l = lambda q_: identb[0:64, 0:64]
        qslot = lambda q_: Qm.ap(q_ * 64, [[1, 64]])

        for q_ in range(8):
            p.op("pe", lambda e, q_=q_: e.matmul(self.psf(bd, q_ * 64, [[1, 64]], parts=64), lhsT=identb[0:64, 0:64],
                                                 rhs=identb[0:64, 0:64], start=(q_ == 0), stop=True, skip_group_check=True),
                 r=["identb"], w=[("ps", bd)])
        for k in range(5):
            for q_ in range(8):
                rhs = (lambda q_: identb[0:64, 0:64]) if k == 0 else qslot
                p.op("pe", lambda e, k=k, q_=q_, rhs=rhs: e.matmul(
                    self.psf(bd, q_ * 64, [[1, 64]], parts=64), lhsT=Am.ap((k * 8 + q_) * 64, [[1, 64]]), rhs=rhs(q_),
                    start=False, stop=True, skip_group_check=True), r=[kq("Am"), kq("Qm"), "identb"], w=[("ps", bd)])
            if k < 4:
                for q_ in range(8):
                    p.op("pe", lambda e, k=k, q_=q_: e.matmul(
                        self.psf(ba, q_ * 64, [[1, 64]], parts=64), lhsT=Bm.ap((lv(k) * 8 + q_) * 64, [[1, 64]]),
                        rhs=Am.ap((k * 8 + q_) * 64, [[1, 64]]), start=True, stop=True), r=[kq("Bm"), kq("Am")],
                        w=[("ps", ba)])
                    if k < 3:
                        p.op("pe", lambda e, k=k, q_=q_: e.matmul(
                            self.psf(bb, q_ * 64, [[1, 64]], parts=64), lhsT=Am.ap((k * 8 + q_) * 64, [[1, 64]]),
                            rhs=Bm.ap((lv(k) * 8 + q_) * 64, [[1, 64]]), start=True, stop=True), r=[kq("Bm"), kq("Am")],
                            w=[("ps", bb)])
            yield
            if k % 2 == 0:
                p.op("dve", lambda e: e.tensor_copy(out=Qm.ap(0, [[1, 512]]), in_=self.psf(bd, 0, [[1, 512]], parts=64)),
                     r=[("ps", bd)], w=[kq("Qm")])
            else:
                p.op("act", lambda e: e.activation(out=Qm.ap(0, [[1, 512]]), in_=self.psf(bd, 0, [[1, 512]], parts=64),
                                                   func=AF.Copy), r=[("ps", bd)], w=[kq("Qm")])
            if k < 4:
                p.op("act", lambda e, k=k: e.activation(out=Am.ap((k + 1) * 512, [[1, 512]]),
                                                        in_=self.psf(ba, 0, [[1, 512]], parts=64), func=AF.Copy),
                     r=[("ps", ba)], w=[kq("Am")])
                if k < 3:
                    p.op("act", lambda e, k=k: e.activation(out=Bm.ap((k + 1) * 512, [[1, 512]]),
                                                            in_=self.psf(bb, 0, [[1, 512]], parts=64), func=AF.Copy),
                         r=[("ps", bb)], w=[kq("Bm")])
            yield
        apply(qslot, R0, [kq("Qm"), kq("R0")], True)
        yield
        evac()
        yield
        apply(ident_l, R0, ["identb", kq("R0")], True)
        apply(bslot(0), R, [kq("Bm"), kq("R")], False)
        yield
        evac()
        yield
        apply(qslot, R, [kq("Qm"), kq("R")], True)
        yield
        evac()
        yield
        rk = kq("R")
        wT = self.wT[i]
        for q_ in range(8):
            p.op("pe", lambda e, q_=q_: e.transpose(self.psb(ba, 512 + q_ * 64, [[1, 64]], parts=64),
                                                    R.ap(q_ * 128 + 64, [[1, 64]]), identb[0:64, 0:64]),
                 r=[rk, "identb"], w=[("ps", ba)])
        yield
        p.op("act", lambda e: e.activation(out=wT.ap(0, [[1, 512]]), in_=self.psb(ba, 512, [[1, 512]], parts=64), func=AF.Copy),
             r=[("ps", ba)], w=[kq("wT")])
        yield
        while self.scan_done < s:
            yield
        S, Sbf, St = self.S, self.Sbf, self.St
        vnew, o1, ob = self.vnew[i], self.o1[i], self.ob[i]
        for q_ in range(8):
            p.op("pe", lambda e, q_=q_: e.matmul(self.psf(ba, q_ * 64, [[1, 64]], parts=64), lhsT=wT.ap(q_ * 64, [[1, 64]]),
                                                 rhs=Sbf.ap(q_ * 64, [[1, 64]], parts=64), start=True, stop=True),
                 r=[kq("wT"), "Sbf"], w=[("ps", ba)])
        for h in (0, 2, 1, 3):
            for d in range(2):
                q_ = d * 4 + h
                pb = (h % 2) * 64
                p.op("pe", lambda e, d=d, h=h, q_=q_, pb=pb: e.matmul(
                    self.psf(bb, q_ * 64, [[1, 64]], parts=64), lhsT=kap(1, h, d, [[1, 64]]),
                    rhs=Sbf.ap(q_ * 64, [[1, 64]], parts=64, pbase=pb), start=True, stop=True), r=["KQVT", "Sbf"],
                    w=[("ps", bb)], rg=pb)
        yield
        p.op("dve", lambda e: e.tensor_tensor(out=vnew.ap(0, [[64, 8], [1, 64]]), in0=R.ap(0, [[128, 8], [1, 64]]),
                                              in1=self.psf(ba, 0, [[64, 8], [1, 64]], parts=64), op=ALU.subtract),
             r=[rk, ("ps", ba)], w=[kq("vnew")])
        p.op("dve", lambda e: e.tensor_tensor(out=o1.ap(0, D3), in0=self.psf(bb, 0, D3, parts=64),
                                              in1=self.sc_step(SC_EGC, sq, s, 64), op=ALU.mult),
             r=[("ps", bb), "SC"], w=[kq("o1")])
        p.op("pool", lambda e: e.tensor_tensor(out=St.ap(0, D3), in0=S.ap(0, D3),
                                               in1=self.sc_step(0, sq, s, 64, tile=self.EGL), op=ALU.mult),
             r=["S", "EGL"], w=["St"])
        yield
        for q_ in range(8):
            p.op("pe", lambda e, q_=q_: e.matmul(self.psf(bd, q_ * 64, [[1, 64]]), lhsT=KD.ap(q_ * 128, [[1, 128]]),
                                                 rhs=vnew.ap(q_ * 64, [[1, 64]]), start=True, stop=True),
                 r=[kq("KD"), kq("vnew")], w=[("ps", bd)])
        for q_ in range(8):
            p.op("pe", lambda e, q_=q_: e.matmul(self.psf(bc, q_ * 64, [[1, 64]], parts=64), lhsT=Aqk.ap(q_ * 64, [[1, 64]]),
                                                 rhs=vnew.ap(q_ * 64, [[1, 64]]), start=True, stop=True),
                 r=[kq("Aqk"), kq("vnew")], w=[("ps", bc)])
        yield
        p.op("dve", lambda e: e.tensor_tensor(out=S.ap(0, [[1, 512]]), in0=St.ap(0, [[1, 512]]),
                                              in1=self.psf(bd, 0, [[1, 512]]), op=ALU.add), r=["St", ("ps", bd)], w=["S"])
        p.op("act", lambda e: e.activation(out=Sbf.ap(0, [[1, 512]]), in_=S.ap(0, [[1, 512]]), func=AF.Copy),
             r=["S"], w=["Sbf"])
        self.scan_done = s + 1
        p.op("dve", lambda e: e.tensor_tensor(out=ob.ap(0, [[1, 512]]), in0=o1.ap(0, [[1, 512]]),
                                              in1=self.psf(bc, 0, [[1, 512]], parts=64), op=ALU.add),
             r=[kq("o1"), ("ps", bc)], w=[kq("ob")])
        yield
        for d in range(2):
            for hp in range(2):
                p.op("pe", lambda e, d=d, hp=hp: e.transpose(self.psb(ba, (d * 2 + hp) * 64, [[1, 64]]),
                                                             ob.ap((d * 4 + hp * 2) * 64, [[1, 128]]), identb[0:64, 0:64]),
                     r=[kq("ob"), "identb"], w=[("ps", ba)])
        yield
        for d in range(2):
            dst = lambda d=d: self.ocT.ap(tg[d], [[NTOK, 2], [1, 64]])
            src = lambda d=d: self.psb(ba, d * 128, [[64, 2], [1, 64]])
            if s < NCH // 2:
                p.op("act", lambda e, dst=dst, src=src: e.activation(out=dst(), in_=src(), func=AF.Copy),
                     r=[("ps", ba)], w=["ocT"])
            else:
                p.op("dve", lambda e, dst=dst, src=src: e.tensor_tensor(out=dst(), in0=dst(), in1=src(), op=ALU.add),
                     r=[("ps", ba), "ocT"], w=["ocT"])
        yield

    def gdn_finish(self, sq):
        p = self.p
        T, toff = sq["T"], sq["toff"]
        BL = min(512, T)
        sqf, rnf, oct_ = self.sqf, self.rnf, self.octmp
        for bi, t0 in enumerate(range(0, T, BL)):
            g0 = toff + t0
            cg = self.cgl[bi % 2]
            ck = ("cgl", bi % 2)
            p.op("sp", lambda e, cg=cg, g0=g0: e.dma_start(out=cg[:, :, 0:BL], in_=self.cgD.ap()[:, :, g0:g0 + BL]),
                 r=[("cgD", g0)], w=[ck], dma="ld_cgl%d" % (bi % 2))
            for hp in range(2):
                src = lambda hp=hp, g0=g0: self.ocT[:, hp, g0:g0 + BL]
                p.op("act", lambda e, src=src: e.activation(out=sqf[:, 0:BL], in_=src(), func=AF.Square), r=["ocT"], w=["sqf"])
                p.op("pe", lambda e: e.matmul(self.psf(3, 0, [[1, BL]]), lhsT=self.onesblk[:], rhs=sqf[:, 0:BL], start=True,
                                              stop=True), r=["sqf", "onesblk"], w=[("ps", 3)])
                p.op("act", lambda e: e.activation(out=rnf[:, 0:BL], in_=self.psf(3, 0, [[1, BL]]), func=AF.Ln, scale=1.0 / 64,
                                                   bias=self.epsc[:, 0:1]), r=[("ps", 3), "epsc"], w=["rnf"])
                p.op("act", lambda e: e.activation(out=rnf[:, 0:BL], in_=rnf[:, 0:BL], func=AF.Exp, scale=-0.5), r=["rnf"],
                     w=["rnf"])
                p.op("dve", lambda e, src=src: e.scalar_tensor_tensor(out=oct_[:, 0:BL], in0=src(), scalar=self.normg[:, 0:1],
                                                                      in1=rnf[:, 0:BL], op0=ALU.mult, op1=ALU.mult),
                     r=["ocT", "normg", "rnf"], w=["octmp"])
                p.op("pool", lambda e, src=src, hp=hp, cg=cg: e.tensor_tensor(out=src(), in0=oct_[:, 0:BL], in1=cg[:, hp, 0:BL],
                                                                              op=ALU.mult), r=["octmp", ck], w=["ocT"])

    def phaseC_setup(self, es, l):
        p, sb = self.p, self.sb
        self.rope_tables(es)
        self.wC = sb(es, "wC", [128, 8, 2048], BF16)
        self.wO = sb(es, "wO", [128, 8, D], BF16)
        for kc in range(8):
            p.op("pool", lambda e, kc=kc: e.dma_start(out=self.wC[:, kc, :],
                                                      in_=self.w_in.ap()[l, kc * 128:(kc + 1) * 128, 0:2048]),
                 w=["wC"], dma="ld_wC")
            p.op("pool", lambda e, kc=kc: e.dma_start(out=self.wO[:, kc, :], in_=self.w_out.ap()[l, kc * 128:(kc + 1) * 128, :]),
                 w=["wO"], dma="ld_wO")
        self.gpbc = sb(es, "gpbc", [128, 2, D])
        p.op("sp", lambda e: e.dma_start(out=self.gpbc[:].rearrange("p a b -> p (a b)"),
                                         in_=bass.AP(self.gps, l * 2 * D, [[0, 128], [1, 2 * D]])),
             r=[("gps", l)], w=["gpbc"], dma="ld_gpbc")
        self.NX = 4
        self.xin = [sb(es, "xinC%d" % i, [128, D]) for i in range(self.NX)]
        self.junk = sb(es, "junkC", [128, D], BF16)
        self.junk2 = sb(es, "junk2C", [128, 512], BF16)
        self.stat = sb(es, "statC", [128, self.NX, 4])
        self.xn = sb(es, "xnC", [128, D], BF16)
        self.hTtmp = sb(es, "hTtmpC", [128, 8, 128])
        self.hTC = [sb(es, "hTC%d" % i, [128, 8, 128], BF16) for i in range(2)]
        NR = 4
        self.NR = NR
        self.qT = [sb(es, "qT%d" % i, [64, 8, 128], BF16) for i in range(NR)]
        self.kT = [sb(es, "kT%d" % i, [64, 2, 128], BF16) for i in range(NR)]
        self.vtok = [sb(es, "vtok%d" % i, [128, 2, 64], BF16) for i in range(NR)]
        self.gA = [sb(es, "gA%d" % i, [128, 512]) for i in range(NR)]
        self.oT = [sb(es, "oT%d" % i, [128, 6, 128], BF16) for i in range(NR)]
        self.kvout = [sb(es, "kvout%d" % i, [128, 2, 128]) for i in range(2)]
        self.qr = sb(es, "qr", [128, 640], BF16)
        self.kdup = sb(es, "kdup", [128, 2, 2, 64], BF16)
        self.rt1 = sb(es, "rt1", [128, 512])
        self.rt2 = sb(es, "rt2", [128, 512])
        self.sig = [sb(es, "sig%d" % i, [128, 256]) for i in range(3)]
        self.negone = sb(es, "negone", [128, 256])
        p.op("pool", lambda e: e.memset(self.negone[:], -1.0), w=["negone"])
        self.Pm = [sb(es, "Pm%d" % i, [128, 5, 512], BF16) for i in range(2)]
        self.attst = sb(es, "attst", [128, 2, 8])
        self.oatt = sb(es, "oatt", [128, 512])
        self.oab = sb(es, "oab", [128, 512], BF16)
        self.lnst = sb(es, "lnst", [128, 8, 4])
        self.vsq = sb(es, "vsq", [128, 256])
        self.vn = sb(es, "vn", [128, 256])
        self.vg = sb(es, "vg", [128, 256], BF16)
        self.m1 = sb(es, "m1", [128, 256])
        self.ug = sb(es, "ug", [128, 256])
        self.obb = sb(es, "obb", [128, 256], BF16)
        self.ytmp = sb(es, "ytmp", [128, D])

    def silu_from_psum(self, src_fn, n, out_fn, rk, wk, slot):
        p = self.p
        sig = self.sig[slot]
        sk = ("sig", slot)
        p.op("act", lambda e: e.activation(out=sig[:, 0:n], in_=src_fn(), func=AF.Exp, scale=-1.0), r=rk, w=[sk])
        yield
        p.op("act", lambda e: e.activation(out=sig[:, 0:n], in_=sig[:, 0:n], func=AF.Ln, bias=1.0), r=[sk], w=[sk])
        p.op("act", lambda e: e.activation(out=sig[:, 0:n], in_=sig[:, 0:n], func=AF.Exp, scale=-1.0), r=[sk], w=[sk])
        yield
        p.op("dve", lambda e: e.tensor_tensor(out=out_fn(), in0=src_fn(), in1=sig[:, 0:n], op=ALU.mult), r=rk + [sk], w=wk)

    def phaseC(self, l, sq):
        NT = sq["T"] // 128
        base = self.tileC
        self.kv_done = 0
        fi, bi = 0, 0
        fg, bg = None, None
        while bi < NT:
            if fg is None and fi < NT and fi <= bi + 3:
                fg = self.c_tile_front(l, sq, fi, base + fi)
                fi += 1
            if bg is None:
                bg = self.c_tile_back(l, sq, bi, NT, base + bi)
            if fg is not None:
                try:
                    next(fg)
                except StopIteration:
                    fg = None
            for _ in range(2):
                try:
                    next(bg)
                except StopIteration:
                    bg = None
                    bi += 1
                    break
        self.tileC += NT

    def c_tile_front(self, l, sq, i, gi):
        p = self.p
        lat = sq["kind"] == "s"
        t0 = i * 128
        slot = gi % self.NX
        r = gi % self.NR
        hT = self.hTC[gi % 2]
        hk = yield from self.make_hT_gen(l, sq, t0, hT, 0, slot)
        wC = self.wC

        def proj(g, b):
            for kc in range(8):
                p.op("pe", lambda e, kc=kc: e.matmul(self.psf(b, 0, [[1, 512]]), lhsT=hT[:, kc, :],
                                                     rhs=wC[:, kc, g * 512:(g + 1) * 512], start=(kc == 0), stop=(kc == 7)),
                     r=[hk, "wC"], w=[("ps", b)])
        qr, kdup = self.qr, self.kdup
        proj(0, 0)
        proj(1, 1)
        yield
        kvo = self.kvout[gi % 2]
        if lat:
            yield from self.rope(0, 0, 8, i, lambda: qr.ap(0, [[64, 8], [1, 64]]), ["qr"])
            yield from self.rope(1, 0, 2, i, lambda: qr.ap(512, [[64, 2], [1, 64]]), ["qr"])
        else:
            p.op("act", lambda e: e.activation(out=qr[:, 0:512], in_=self.psf(0, 0, [[1, 512]]), func=AF.Copy),
                 r=[("ps", 0)], w=["qr"])
            p.op("act", lambda e: e.activation(out=qr[:, 512:640], in_=self.psf(1, 0, [[1, 128]]), func=AF.Copy),
                 r=[("ps", 1)], w=["qr"])
            p.op("act", lambda e: e.activation(out=kvo[:].rearrange("p a b -> p (a b)"), in_=self.psf(1, 0, [[1, 256]]),
                                               func=AF.Copy), r=[("ps", 1)], w=[("kvout", gi % 2)])
            for which, dst in ((0, self.nk), (1, self.nv)):
                p.op("sp", lambda e, which=which, dst=dst: e.dma_start(out=dst.ap()[sq["idx"], l, t0:t0 + 128, :],
                                                                       in_=kvo[:, which, :]),
                     r=[("kvout", gi % 2)], dma="st_kv%d" % (gi % 2))
        vt = self.vtok[r]
        p.op("act", lambda e: e.activation(out=vt[:].rearrange("p a b -> p (a b)"), in_=self.psf(1, 128, [[1, 128]]),
                                           func=AF.Copy), r=[("ps", 1)], w=[("vtok", r)])
        gA = self.gA[r]
        yield from self.silu_from_psum(lambda: self.psf(1, 256, [[1, 256]]), 256, lambda: gA[:, 0:256], [("ps", 1)],
                                       [("gA", r)], 0)
        proj(2, 0)
        proj(3, 1)
        yield
        for h in range(8):
            p.op("pe", lambda e, h=h: e.transpose(self.psb(7, h * 128, [[1, 128]], parts=64), qr[:, h * 64:(h + 1) * 64],
                                                  self.identb[:]), r=["qr", "identb"], w=[("ps", 7)])
        for kv in range(2):
            p.op("pe", lambda e, kv=kv: e.transpose(self.psb(6, kv * 128, [[1, 128]], parts=64),
                                                    qr[:, 512 + kv * 64:512 + (kv + 1) * 64], self.identb[:]),
                 r=["qr", "identb"], w=[("ps", 6)])
        yield
        qT, kT = self.qT[r], self.kT[r]
        p.op("dve", lambda e: e.tensor_copy(out=qT[:].rearrange("p a b -> p (a b)"), in_=self.psb(7, 0, [[1, 1024]], parts=64)),
             r=[("ps", 7)], w=[("qT", r)])
        p.op("act", lambda e: e.activation(out=kT[:].rearrange("p a b -> p (a b)"), in_=self.psb(6, 0, [[1, 256]], parts=64),
                                           func=AF.Copy), r=[("ps", 6)], w=[("kT", r)])
        self.kv_done = i + 1
        yield from self.silu_from_psum(lambda: self.psf(0, 0, [[1, 256]]), 256, lambda: gA[:, 256:512], [("ps", 0)],
                                       [("gA", r)], 1)
        yield from self.sgu(r)

    def rope(self, bank, off, nh, tile_i, out_fn, wk):
        p = self.p
        rt1, rt2 = self.rt1, self.rt2
        rk = [("ps", bank)]
        p.op("dve", lambda e: e.tensor_tensor(out=rt1.ap(0, [[64, nh], [1, 64]]), in0=self.psf(bank, off, [[64, nh], [1, 64]]),
                                              in1=self.ropeC.ap(tile_i * 64, [[0, nh], [1, 64]]), op=ALU.mult),
             r=rk + ["ropeC"], w=["rt1"])
        for sw in range(2):
            p.op("dve", lambda e, sw=sw: e.tensor_tensor(
                out=rt2.ap(sw * 16, [[64, nh], [32, 2], [1, 16]]),
                in0=self.psf(bank, off + (1 - sw) * 16, [[64, nh], [32, 2], [1, 16]]),
                in1=self.ropeS.ap(tile_i * 64 + sw * 16, [[0, nh], [32, 2], [1, 16]]), op=ALU.mult),
                r=rk + ["ropeS"], w=["rt2"])
        yield
        p.op("pool", lambda e: e.tensor_tensor(out=out_fn(), in0=rt1.ap(0, [[64, nh], [1, 64]]),
                                               in1=rt2.ap(0, [[64, nh], [1, 64]]), op=ALU.add), r=["rt1", "rt2"], w=wk)
        yield

    def sgu(self, r):
        p = self.p
        st, vsq, vn, vg, m1, ug, obb = self.lnst, self.vsq, self.vn, self.vg, self.m1, self.ug, self.obb
        bv = lambda: self.psf(1, 0, [[64, 4], [1, 64]])
        p.op("dve", lambda e: e.tensor_reduce(out=st[:, 0:4, 0], in_=bv(), axis=AX.X, op=ALU.add), r=[("ps", 1)], w=["lnst"])
        p.op("act", lambda e: e.activation(out=vsq[:], in_=self.psf(1, 0, [[1, 256]]), func=AF.Square), r=[("ps", 1)], w=["vsq"])
        yield
        p.op("dve", lambda e: e.tensor_reduce(out=st[:, 0:4, 1], in_=vsq.ap(0, [[64, 4], [1, 64]]), axis=AX.X, op=ALU.add),
             r=["vsq"], w=["lnst"])
        p.op("dve", lambda e: e.tensor_scalar(out=st[:, 0:4, 2], in0=st[:, 0:4, 0], scalar1=1.0 / 64, scalar2=None, op0=ALU.mult),
             r=["lnst"], w=["lnst"])
        p.op("dve", lambda e: e.tensor_tensor(out=st[:, 4:8, 0], in0=st[:, 0:4, 2], in1=st[:, 0:4, 2], op=ALU.mult),
             r=["lnst"], w=["lnst"])
        p.op("dve", lambda e: e.scalar_tensor_tensor(out=st[:, 4:8, 1], in0=st[:, 0:4, 1], scalar=1.0 / 64, in1=st[:, 4:8, 0],
                                                     op0=ALU.mult, op1=ALU.subtract), r=["lnst"], w=["lnst"])
        yield
        p.op("act", lambda e: e.activation(out=st[:, 4:8, 2], in_=st[:, 4:8, 1], func=AF.Ln, bias=self.epsc[:, 0:1]),
             r=["lnst", "epsc"], w=["lnst"])
        p.op("act", lambda e: e.activation(out=st[:, 4:8, 3], in_=st[:, 4:8, 2], func=AF.Exp, scale=-0.5), r=["lnst"], w=["lnst"])
        p.op("dve", lambda e: e.tensor_tensor(out=vn.ap(0, [[64, 4], [1, 64]]), in0=bv(), in1=st.ap(2, [[4, 4], [0, 64]]),
                                              op=ALU.subtract), r=[("ps", 1), "lnst"], w=["vn"])
        yield
        p.op("pool", lambda e: e.tensor_tensor(out=vn.ap(0, [[64, 4], [1, 64]]), in0=vn.ap(0, [[64, 4], [1, 64]]),
                                               in1=st.ap(4 * 4 + 3, [[4, 4], [0, 64]]), op=ALU.mult), r=["vn", "lnst"], w=["vn"])
        p.op("pool", lambda e: e.tensor_tensor(out=vn[:], in0=vn[:], in1=self.lngb[:, 0, :], op=ALU.mult), r=["vn", "lngb"],
             w=["vn"])
        p.op("pool", lambda e: e.tensor_tensor(out=vg[:], in0=vn[:], in1=self.lngb[:, 1, :], op=ALU.add), r=["vn", "lngb"],
             w=["vg"])
        yield from self.silu_from_psum(lambda: self.psf(1, 256, [[1, 256]]), 256, lambda: ug[:], [("ps", 1)], ["ug"], 2)
        p.op("dve", lambda e: e.tensor_tensor(out=ug[:], in0=ug[:], in1=self.psf(0, 256, [[1, 256]]), op=ALU.mult),
             r=["ug", ("ps", 0)], w=["ug"])
        for g in range(4):
            p.op("pe", lambda e, g=g: e.matmul(self.psf(5, g * 64, [[1, 64]]), lhsT=self.WsT[:, g, :],
                                               rhs=vg[:, g * 64:(g + 1) * 64], start=True, stop=True), r=["vg", "WsT"],
                 w=[("ps", 5)])
        yield
        p.op("dve", lambda e: e.tensor_tensor(out=m1.ap(0, [[64, 4], [1, 64]]), in0=self.psf(5, 0, [[64, 4], [1, 64]]),
                                              in1=self.bsT.ap(0, [[1, 4], [0, 64]]), op=ALU.add), r=[("ps", 5), "bsT"], w=["m1"])
        yield
        p.op("pool", lambda e: e.tensor_tensor(out=obb[:], in0=m1[:], in1=ug[:], op=ALU.mult), r=["m1", "ug"], w=["obb"])
        yield
        for hp in range(2):
            p.op("pe", lambda e, hp=hp: e.transpose(self.psb(6, 256 + hp * 128, [[1, 128]]), obb[:, hp * 128:(hp + 1) * 128],
                                                    self.identb[:]), r=["obb", "identb"], w=[("ps", 6)])
        yield
        oT = self.oT[r]
        p.op("act", lambda e: e.activation(out=oT[:, 4:6, :].rearrange("p a b -> p (a b)"), in_=self.psb(6, 256, [[1, 256]]),
                                           func=AF.Copy), r=[("ps", 6)], w=[("oTb", r)])
        yield

    def c_tile_back(self, l, sq, i, NT, gi):
        p = self.p
        lat = sq["kind"] == "s"
        NR = self.NR
        r = gi % NR
        while self.kv_done < min(i + 2, NT):
            yield
        blocks = []
        if lat:
            if i >= 1:
                blocks.append(("t", (gi - 1) % NR, 0))
            blocks.append(("t", r, None))
            if i + 1 < NT:
                blocks.append(("t", (gi + 1) % NR, 1))
            blocks.append(("c", 0, None))
            blocks.append(("c", 1, None))
        else:
            for j in range(NT):
                blocks.append(("t", (gi - i + j) % NR, None))
        qT = self.qT[r]
        nb = len(blocks)
        for g in range(2):
            Pg = self.Pm[g]
            for bi, (kind, idx, msk) in enumerate(blocks):
                b = 2 + (bi % 2)
                for hl in range(4):
                    h = g * 4 + hl
                    if kind == "t":
                        kt = self.kT[idx]
                        lhs = lambda kt=kt: kt[:, g, :]
                        rk = [("kT", idx)]
                    else:
                        lhs = lambda idx=idx: self.kTctx[:, g, idx * 128:(idx + 1) * 128]
                        rk = ["kTctx"]
                    p.op("pe", lambda e, hl=hl, h=h, lhs=lhs, b=b: e.matmul(
                        self.psf(b, hl * 128, [[1, 128]]), lhsT=lhs(), rhs=qT[:, h, :], start=True, stop=True),
                        r=rk + [("qT", r)], w=[("ps", b)])
                yield
                p.op("act", lambda e, bi=bi, b=b, Pg=Pg: e.activation(out=Pg[:, bi, :], in_=self.psf(b, 0, [[1, 512]]),
                                                                      func=AF.Exp, scale=0.125), r=[("ps", b)],
                     w=[("Pm", g, bi)])
                if msk is not None:
                    p.op("pool", lambda e, bi=bi, msk=msk, Pg=Pg: e.tensor_tensor(
                        out=Pg.ap(bi * 512, [[128, 4], [1, 128]]), in0=Pg.ap(bi * 512, [[128, 4], [1, 128]]),
                        in1=self.amask.ap(msk * 128, [[0, 4], [1, 128]]), op=ALU.mult), r=[("Pm", g, bi), "amask"],
                        w=[("Pm", g, bi)])
            yield
            for hl in range(4):
                h = g * 4 + hl
                for bi, (kind, idx, msk) in enumerate(blocks):
                    if kind == "t":
                        vt = self.vtok[idx]
                        rhs = lambda vt=vt: vt[:, g, :]
                        rk = [("vtok", idx)]
                    else:
                        rhs = lambda idx=idx: self.vctx[:, idx, g, :]
                        rk = ["vctx"]
                    p.op("pe", lambda e, h=h, hl=hl, bi=bi, rhs=rhs, Pg=Pg: e.matmul(
                        self.psf(4, h * 64, [[1, 64]]), lhsT=Pg[:, bi, hl * 128:(hl + 1) * 128], rhs=rhs(), start=(bi == 0),
                        stop=(bi == nb - 1)), r=rk + [("Pm", g, bi)], w=[("ps", 4)])
                for bi in range(nb):
                    p.op("pe", lambda e, h=h, hl=hl, bi=bi, Pg=Pg: e.matmul(
                        self.psf(5, 256 + h, [[1, 1]]), lhsT=Pg[:, bi, hl * 128:(hl + 1) * 128], rhs=self.onecol[:, 0:1],
                        start=(bi == 0), stop=(bi == nb - 1)), r=[("Pm", g, bi), "onecol"], w=[("ps", 5)])
                yield
        st = self.attst
        p.op("dve", lambda e: e.tensor_tensor(out=st[:, 0, :], in0=self.psf(5, 256, [[1, 8]]), in1=self.esink[:], op=ALU.add),
             r=[("ps", 5), "esink"], w=["attst"])
        p.op("dve", lambda e: e.reciprocal(out=st[:, 1, :], in_=st[:, 0, :]), r=["attst"], w=["attst"])
        oatt, oab, gA = self.oatt, self.oab, self.gA[r]
        p.op("dve", lambda e: e.tensor_tensor(out=oatt.ap(0, [[64, 8], [1, 64]]), in0=self.psf(4, 0, [[64, 8], [1, 64]]),
                                              in1=st.ap(8, [[1, 8], [0, 64]]), op=ALU.mult), r=[("ps", 4), "attst"], w=["oatt"])
        yield
        p.op("pool", lambda e: e.tensor_tensor(out=oab[:], in0=oatt[:], in1=gA[:], op=ALU.mult), r=["oatt", ("gA", r)],
             w=["oab"])
        yield
        for hp in range(4):
            p.op("pe", lambda e, hp=hp: e.transpose(self.psb(4, hp * 128, [[1, 128]]), oab[:, hp * 128:(hp + 1) * 128],
                                                    self.identb[:]), r=["oab", "identb"], w=[("ps", 4)])
        yield
        oT = self.oT[r]
        p.op("act", lambda e: e.activation(out=oT[:, 0:4, :].rearrange("p a b -> p (a b)"), in_=self.psb(4, 0, [[1, 512]]),
                                           func=AF.Copy), r=[("ps", 4)], w=[("oTa", r)])
        yield
        g0 = sq["toff"] + i * 128
        wO = self.wO
        for n in range(2):
            b = 2 + n
            for kc in range(8):
                if kc < 6:
                    lhs = lambda kc=kc: oT[:, kc, :]
                    rk = [("oTa", r), ("oTb", r)]
                else:
                    lhs = lambda kc=kc: self.ocT[:, kc - 6, g0:g0 + 128]
                    rk = ["ocT"]
                p.op("pe", lambda e, kc=kc, n=n, b=b, lhs=lhs: e.matmul(self.psf(b, 0, [[1, 512]]), lhsT=lhs(),
                                                                        rhs=wO[:, kc, n * 512:(n + 1) * 512],
                                                                        start=(kc == 0), stop=(kc == 7)),
                     r=rk + ["wO"], w=[("ps", b)])
        yield
        slot = gi % self.NX
        stt, junk = self.stat, self.junk2
        sk = ("stat", slot)
        for n in range(2):
            p.op("act", lambda e, n=n: e.activation(out=junk[:, 0:512], in_=self.psf(2 + n, 0, [[1, 512]]), func=AF.Square,
                                                    accum_out=stt[:, slot, n:n + 1]), r=[("ps", 2 + n)], w=["junk2", sk])
        yield
        p.op("dve", lambda e: e.tensor_tensor(out=stt[:, slot, 3:4], in0=stt[:, slot, 0:1], in1=stt[:, slot, 1:2], op=ALU.add),
             r=[sk], w=[sk])
        yield
        p.op("act", lambda e: e.activation(out=stt[:, slot, 1:2], in_=stt[:, slot, 3:4], func=AF.Ln, scale=1.0 / D,
                                           bias=self.epsc[:, 0:1]), r=[sk, "epsc"], w=[sk])
        p.op("act", lambda e: e.activation(out=stt[:, slot, 2:3], in_=stt[:, slot, 1:2], func=AF.Exp, scale=-0.5), r=[sk], w=[sk])
        yield
        c = sq["cond"]
        ytmp = self.ytmp
        xt = self.xin[slot]
        for n in range(2):
            p.op("dve", lambda e, n=n: e.scalar_tensor_tensor(out=ytmp[:, n * 512:(n + 1) * 512],
                                                              in0=self.psf(2 + n, 0, [[1, 512]]), scalar=stt[:, slot, 2:3],
                                                              in1=self.gpbc[:, c, n * 512:(n + 1) * 512], op0=ALU.mult,
                                                              op1=ALU.mult), r=[("ps", 2 + n), sk, "gpbc"], w=["ytmp"])
        yield
        p.op("pool", lambda e: e.tensor_tensor(out=xt[:], in0=ytmp[:], in1=xt[:], op=ALU.add), r=["ytmp", ("xin", slot)],
             w=[("xin", slot)])
        yield
        dst = self.x_dst(l, sq, i * 128)
        p.op("pool", lambda e: e.dma_start(out=dst, in_=xt[:]), r=[("xin", slot)], w=[self.xkey(l + 1, sq, i * 128)],
             dma="st_y%d" % slot)
        yield


def build_program(dbg=False):
    nc0 = bass.Bass("TRN2", target_bir_lowering=False)
    p0 = Prog(nc0, None)
    b0 = Builder(nc0, p0, dbg)
    b0.build()
    nc = bass.Bass("TRN2", target_bir_lowering=False)
    p1 = Prog(nc, p0.need)
    b1 = Builder(nc, p1, dbg)
    b1.build()
    return nc, b1, p1


_CACHE = {}


def make_in_maps(x_prompt, x_sample, cache_k, cache_v, state_delta, c, c_ctx, w_mod, b_mod, g_pre, w_in, attn_sink,
                 sgu_ln_g, sgu_ln_b, sgu_w, sgu_b, gdn_conv_w, gdn_a_log, gdn_dt_bias, gdn_norm_g, g_post, w_out):
    f = lambda a: np.ascontiguousarray(np.asarray(a, dtype=np.float32))
    x_prompt, x_sample, cache_k, cache_v, state_delta = map(f, (x_prompt, x_sample, cache_k, cache_v, state_delta))
    c, c_ctx = f(c), f(c_ctx)
    shared = dict(w_mod=f(w_mod), b_mod=f(b_mod), g_pre=f(g_pre), w_in=f(w_in), sink=f(attn_sink), ln_g=f(sgu_ln_g),
                  ln_b=f(sgu_ln_b), sgu_w=f(sgu_w), sgu_b=f(sgu_b), conv_w=f(gdn_conv_w),
                  a_log=f(gdn_a_log).reshape(2, 8), dt_bias=f(gdn_dt_bias).reshape(2, 8), norm_g=f(gdn_norm_g),
                  g_post=f(g_post), w_out=f(w_out))
    in_maps = []
    for core in range(8):
        m = dict(shared)
        m["xp"] = x_prompt[core * 4:(core + 1) * 4]
        m["xs"] = x_sample[core]
        m["ck"] = cache_k[core].reshape(2, 256, 128)
        m["cv"] = cache_v[core].reshape(2, 256, 128)
        m["sd"] = state_delta[core]
        m["cvec"] = np.ascontiguousarray(np.stack([c_ctx, c[core]], 0))
        in_maps.append(m)
    return in_maps


def kernel(**inputs):
    in_maps = make_in_maps(**inputs)
    if "nc" not in _CACHE:
        _CACHE["nc"] = build_program(False)[0]
    nc = _CACHE["nc"]
    res = run_bass_kernel_spmd(nc, in_maps, core_ids=list(range(8)))
    R = res.results
    yp = np.concatenate([R[i]["yp"] for i in range(8)], 0)
    ys = np.stack([R[i]["ys"] for i in range(8)], 0)
    nk = np.concatenate([R[i]["nk"] for i in range(8)], 0).reshape(32, 2, 256, 2, 64)
    nv = np.concatenate([R[i]["nv"] for i in range(8)], 0).reshape(32, 2, 256, 2, 64)
    ns = np.concatenate([R[i]["ns"] for i in range(8)], 0)
    return (yp.astype(np.float32), ys.astype(np.float32), nk.astype(np.float32), nv.astype(np.float32),
            ns.astype(np.float32))
```

```python
import contextlib
import math
import os

import numpy as np
import concourse.bass as bass
import concourse.mybir as mybir
from concourse.bass_utils import run_bass_kernel_spmd

F32 = mybir.dt.float32
BF16 = mybir.dt.bfloat16
AF = mybir.ActivationFunctionType
ALU = mybir.AluOpType
AX = mybir.AxisListType

D = 1024
NPROMPT = 4
TP = 256
TS = 4096
NTOK = NPROMPT * TP + TS
RT = 4096
RCH = RT // 64
EPS = 1e-6
NEGBIG = -30000.0
A_CA, A_CG = 768, 784
WA_COLS = 1040
SEM_EPOCH = 20000
STRICT_SAME_ENGINE = False
NSC = 6
SC_GC, SC_GB, SC_EGC, SC_BETA, SC_BEXP, SC_EGLMGC = range(6)


class Prog:
    QUEUES = ("pe", "act", "dve", "pool", "sp")

    def __init__(self, nc, need=None):
        self.nc = nc
        self.dry = need is None
        self.need_in = need
        self.need = {}
        self.state = {}
        self.ops = []
        self.val = []
        self.cnt = {}
        self.waited = {q: {} for q in self.QUEUES}
        self.sems = {}
        self.es = None
        self.n_wait = 0
        if not self.dry:
            self.eng = {"pe": nc.tensor, "act": nc.scalar, "dve": nc.vector, "pool": nc.gpsimd, "sp": nc.sync}

    def sem(self, name):
        s = self.sems.get(name)
        if s is None:
            s = self.es.enter_context(self.nc.semaphore(name))
            self.sems[name] = s
        return s

    def op(self, q, fn, r=(), w=(), dma=None, extra=(), rg=0):
        idx = len(self.ops)
        deps = {}
        pr = [k for k in r if isinstance(k, tuple) and k[0] == "ps"]
        if pr:
            r = [k for k in r if k not in pr]
            w = list(w) + pr
        for k in r:
            st = self.state.get(k)
            if st is not None and st[0] is not None:
                deps[st[0]] = True
        for k in w:
            st = self.state.get(k)
            if st is not None:
                if st[0] is not None:
                    deps.setdefault(st[0], False)
                for x in st[1].values():
                    deps.setdefault(x, False)
        for x in extra:
            deps[x] = True
        rkey = ("dma", dma) if dma is not None else q
        for k in r:
            self.state.setdefault(k, [None, {}])[1][rkey] = idx
        for k in w:
            self.state[k] = [idx, {}]
        deps.pop(idx, None)
        kept = []
        for d, raw in deps.items():
            pq, pdma, prg = self.ops[d]
            if pdma is None and dma is None and pq == q:
                if q == "pe" and prg == rg:
                    continue
                if q != "pe" and not raw and not STRICT_SAME_ENGINE:
                    continue
            kept.append(d)
        self.ops.append((q, dma, rg))
        if self.dry:
            for d in kept:
                self.need[d] = True
            self.val.append(None)
            return idx
        e = self.eng[q]
        ws = {}
        for d in kept:
            s, v = self.val[d]
            if ws.get(s, 0) < v:
                ws[s] = v
        for s, v in ws.items():
            if self.waited[q].get(s, 0) >= v:
                continue
            e.wait_ge(self.sem(s), v)
            self.waited[q][s] = v
            self.n_wait += 1
        ins = fn(e)
        if dma is not None:
            c = self.cnt.get(dma, 0) + 16
            self.cnt[dma] = c
            ins.then_inc(self.sem(dma), 16)
            self.val.append((dma, c))
        elif self.need_in.get(idx):
            c = self.cnt.get(q, 0) + 1
            self.cnt[q] = c
            name = "%s_e%d" % (q, (c - 1) // SEM_EPOCH)
            ins.then_inc(self.sem(name), 1)
            self.val.append((name, (c - 1) % SEM_EPOCH + 1))
        else:
            self.val.append(None)
        return idx

    def barrier(self):
        last = {}
        for i, (q, dma, _) in enumerate(self.ops):
            last[("dma", dma) if dma is not None else q] = i
        ids = list(last.values())
        for q in self.QUEUES:
            self.op(q, lambda e: e.nop(), extra=ids)

    def finish(self):
        if self.dry:
            return
        last = {}
        for v in self.val:
            if v is not None:
                last[v[0]] = max(last.get(v[0], 0), v[1])
        for s, v in last.items():
            if self.waited["sp"].get(s, 0) < v:
                self.nc.sync.wait_ge(self.sem(s), v)


class StopBuild(Exception):
    pass


class Tile:
    def __init__(self, h, shape):
        self.h = h
        self.shape = list(shape)
        self.fs = int(np.prod(shape[1:]))

    def __getitem__(self, k):
        return self.h[k]

    def ap(self, off, dims, parts=None, pbase=0):
        n = self.shape[0] if parts is None else parts
        return bass.AP(self.h, pbase * self.fs + off, [[self.fs, n]] + [list(d) for d in dims])


D3 = [[256, 2], [64, 4], [1, 64]]


class Builder:
    def __init__(self, nc, prog, dbg):
        self.nc = nc
        self.p = prog
        self.dbg = dbg
        self.dbg_out = {}
        self.uid = 0
        self.tileA = 0
        self.stA = 0
        self.tileC = 0
        self.kstop = int(os.environ.get("KSTOP", "99"))

    def stage(self, n):
        if n >= self.kstop:
            self.stopped = True
        return getattr(self, "stopped", False)

    def sb(self, es, name, shape, dt=F32):
        self.uid += 1
        h = es.enter_context(self.nc.sbuf_tensor("%s_%d" % (name, self.uid), list(shape), dt))
        return Tile(h, shape)

    def dram(self, name, shape, kind, dt=F32):
        return self.nc.dram_tensor(name, list(shape), dt, kind=kind)

    def dump(self, name, ap_fn, shape, rkeys):
        if not self.dbg or name in self.dbg_out:
            return
        t = self.dram("dbg_" + name, shape, "ExternalOutput")
        self.dbg_out[name] = t
        self.p.op("pool", lambda e: e.dma_start(out=t.ap(), in_=ap_fn()), r=rkeys, dma="dbg_" + name)

    def psf(self, b, off, dims, parts=128):
        return bass.AP(self.PSALL, b * 512 + off, [[4096, parts]] + [list(d) for d in dims])

    def psb(self, b, off, dims, parts=128):
        return bass.AP(self.PSALLB, b * 1024 + off, [[8192, parts]] + [list(d) for d in dims])

    def build(self):
        nc, p = self.nc, self.p
        dr = self.dram
        self.xp = dr("xp", [NPROMPT, TP, D], "ExternalInput")
        self.xs = dr("xs", [TS, D], "ExternalInput")
        self.ck = dr("ck", [2, 256, 128], "ExternalInput")
        self.cv = dr("cv", [2, 256, 128], "ExternalInput")
        self.sd = dr("sd", [2, 2, 4, 64, 64], "ExternalInput")
        self.cvec = dr("cvec", [2, D], "ExternalInput")
        self.w_mod = dr("w_mod", [2, D, 3 * D], "ExternalInput")
        self.b_mod = dr("b_mod", [2, 3 * D], "ExternalInput")
        self.g_pre = dr("g_pre", [2, D], "ExternalInput")
        self.w_in = dr("w_in", [2, D, 3088], "ExternalInput")
        self.sink = dr("sink", [2, 8], "ExternalInput")
        self.ln_g = dr("ln_g", [2, 256], "ExternalInput")
        self.ln_b = dr("ln_b", [2, 256], "ExternalInput")
        self.sgu_w = dr("sgu_w", [2, 4, 128, 128], "ExternalInput")
        self.sgu_b = dr("sgu_b", [2, 4, 128], "ExternalInput")
        self.conv_w = dr("conv_w", [2, 5, 768], "ExternalInput")
        self.a_log = dr("a_log", [2, 8], "ExternalInput")
        self.dt_bias = dr("dt_bias", [2, 8], "ExternalInput")
        self.norm_g = dr("norm_g", [2, 64], "ExternalInput")
        self.g_post = dr("g_post", [2, D], "ExternalInput")
        self.w_out = dr("w_out", [2, D, D], "ExternalInput")
        self.yp = dr("yp", [NPROMPT, TP, D], "ExternalOutput")
        self.ys = dr("ys", [TS, D], "ExternalOutput")
        self.nk = dr("nk", [NPROMPT, 2, 256, 128], "ExternalOutput")
        self.nv = dr("nv", [NPROMPT, 2, 256, 128], "ExternalOutput")
        self.ns = dr("ns", [NPROMPT, 2, 2, 4, 64, 64], "ExternalOutput")
        self.xp1 = dr("xp1", [NPROMPT, TP, D], "Internal")
        self.xs1 = dr("xs1", [TS, D], "Internal")
        self.gps = dr("gps", [2, 2, D], "Internal")
        self.cgD = dr("cgD", [128, 2, NTOK], "Internal", BF16)

        with contextlib.ExitStack() as es:
            p.es = es
            self.PSALL = es.enter_context(nc.psum_tensor("psall", [128, 4096], F32))
            self.PSALLB = self.PSALL.bitcast(BF16)
            self.consts(es)
            if not self.stage(1):
                for l in range(2):
                    if not self.stage(0):
                        self.layer(l)
            p.finish()

    def consts(self, es):
        p, sb = self.p, self.sb
        self.identf = sb(es, "identf", [128, 128])
        self.identb = sb(es, "identb", [128, 128], BF16)
        self.nidentf = sb(es, "nidentf", [64, 64])
        self.ones64 = sb(es, "ones64", [64, 128])
        self.Uf = sb(es, "Uf", [64, 64])
        self.Ub = sb(es, "Ub", [64, 64])
        self.NEG = sb(es, "NEG", [64, 2, 8, 64], BF16)
        self.amask = sb(es, "amask", [128, 2, 128], BF16)
        self.onesblk = sb(es, "onesblk", [128, 128], BF16)
        self.epsc = sb(es, "epsc", [128, 1])
        self.onecol = sb(es, "onecol", [128, 1], BF16)
        self.sel = sb(es, "sel", [2, 2, 128])
        self.blkm = sb(es, "blkm", [64, 8, 64], BF16)
        with contextlib.ExitStack() as ts:
            tmpf = sb(ts, "ctmpf", [128, 1024])
            idf, idb = self.identf, self.identb
            p.op("pool", lambda e: e.memset(idf[:], 1.0), w=["identf"])
            p.op("pool", lambda e: e.affine_select(out=idf[:], in_=idf[:], pattern=[[-1, 128]], compare_op=ALU.is_equal,
                                                   fill=0.0, base=0, channel_multiplier=1), r=["identf"], w=["identf"])
            p.op("pool", lambda e: e.tensor_copy(out=idb[:], in_=idf[:]), r=["identf"], w=["identb"])
            p.op("pool", lambda e: e.tensor_scalar(out=self.nidentf[:], in0=idf[0:64, 0:64], scalar1=-1.0, scalar2=None,
                                                   op0=ALU.mult), r=["identf"], w=["nidentf"])
            p.op("pool", lambda e: e.memset(self.ones64[:], 1.0), w=["ones64"])
            p.op("pool", lambda e: e.memset(self.epsc[:], EPS), w=["epsc"])
            p.op("pool", lambda e: e.memset(self.onecol[:], 1.0), w=["onecol"])
            p.op("pool", lambda e: e.memset(self.Uf[:], 1.0), w=["Uf"])
            p.op("pool", lambda e: e.affine_select(out=self.Uf[:], in_=self.Uf[:], pattern=[[1, 64]], compare_op=ALU.is_ge,
                                                   fill=0.0, base=0, channel_multiplier=-1), r=["Uf"], w=["Uf"])
            p.op("pool", lambda e: e.memset(self.Ub[:], 1.0), w=["Ub"])
            p.op("pool", lambda e: e.affine_select(out=self.Ub[:], in_=self.Ub[:], pattern=[[-1, 64]], compare_op=ALU.is_ge,
                                                   fill=0.0, base=0, channel_multiplier=1), r=["Ub"], w=["Ub"])
            p.op("pool", lambda e: e.memset(tmpf[0:64, :], 0.0), w=["ctmpf"])
            for which in range(2):
                for d in range(2):
                    sgn = 1 if d == 0 else -1
                    base = -1 if which == 0 else 0
                    o = (which * 8 + d * 4) * 64

                    def f(e, o=o, sgn=sgn, base=base):
                        v = tmpf.ap(o, [[64, 4], [1, 64]], parts=64)
                        return e.affine_select(out=v, in_=v, pattern=[[0, 4], [sgn, 64]], compare_op=ALU.is_ge,
                                               fill=NEGBIG, base=base, channel_multiplier=-sgn)
                    p.op("pool", f, r=["ctmpf"], w=["ctmpf"])
            p.op("pool", lambda e: e.tensor_copy(out=self.NEG[:].rearrange("p a b c -> p (a b c)"), in_=tmpf[0:64, :]),
                 r=["ctmpf"], w=["NEG"])
            p.op("pool", lambda e: e.memset(tmpf[0:64, 0:512], 1.0), r=["NEG"], w=["ctmpf"])
            p.op("pool", lambda e: e.affine_select(out=tmpf.ap(0, [[64, 8], [1, 64]], parts=32),
                                                   in_=tmpf.ap(0, [[64, 8], [1, 64]], parts=32), pattern=[[0, 8], [-1, 64]],
                                                   compare_op=ALU.is_ge, fill=0.0, base=31, channel_multiplier=0),
                 r=["ctmpf"], w=["ctmpf"])
            p.op("pool", lambda e: e.affine_select(out=tmpf.ap(0, [[64, 8], [1, 64]], parts=32, pbase=32),
                                                   in_=tmpf.ap(0, [[64, 8], [1, 64]], parts=32, pbase=32),
                                                   pattern=[[0, 8], [1, 64]], compare_op=ALU.is_ge, fill=0.0, base=-32,
                                                   channel_multiplier=0), r=["ctmpf"], w=["ctmpf"])
            p.op("pool", lambda e: e.tensor_copy(out=self.blkm[:].rearrange("p a b -> p (a b)"), in_=tmpf[0:64, 0:512]),
                 r=["ctmpf"], w=["blkm"])
            p.op("pool", lambda e: e.memset(tmpf[:, 0:256], 1.0), r=["blkm"], w=["ctmpf"])
            p.op("pool", lambda e: e.affine_select(out=tmpf[:, 0:128], in_=tmpf[:, 0:128], pattern=[[-1, 128]],
                                                   compare_op=ALU.is_ge, fill=0.0, base=0, channel_multiplier=1),
                 r=["ctmpf"], w=["ctmpf"])
            p.op("pool", lambda e: e.affine_select(out=tmpf[:, 128:256], in_=tmpf[:, 128:256], pattern=[[1, 128]],
                                                   compare_op=ALU.is_ge, fill=0.0, base=0, channel_multiplier=-1),
                 r=["ctmpf"], w=["ctmpf"])
            p.op("pool", lambda e: e.tensor_copy(out=self.amask[:].rearrange("p a b -> p (a b)"), in_=tmpf[:, 0:256]),
                 r=["ctmpf"], w=["amask"])
            p.op("pool", lambda e: e.memset(self.onesblk[:], 0.0), w=["onesblk"])
            p.op("pool", lambda e: e.memset(self.onesblk[0:64, 0:64], 1.0), r=["onesblk"], w=["onesblk"])
            p.op("pool", lambda e: e.memset(self.onesblk[64:128, 64:128], 1.0), r=["onesblk"], w=["onesblk"])
            p.op("pool", lambda e: e.memset(self.sel[:], 1.0), w=["sel"])
            p.op("pool", lambda e: e.affine_select(out=self.sel[:, 0, :], in_=self.sel[:, 0, :], pattern=[[0, 128]],
                                                   compare_op=ALU.is_ge, fill=0.0, base=0, channel_multiplier=-1),
                 r=["sel"], w=["sel"])
            p.op("pool", lambda e: e.affine_select(out=self.sel[:, 1, :], in_=self.sel[:, 1, :], pattern=[[0, 128]],
                                                   compare_op=ALU.is_ge, fill=0.0, base=-1, channel_multiplier=1),
                 r=["sel"], w=["sel"])
            p.barrier()

    def rope_tables(self, es):
        p, sb = self.p, self.sb
        self.ropeC = sb(es, "ropeC", [128, 32, 64])
        self.ropeS = sb(es, "ropeS", [128, 32, 64])
        with contextlib.ExitStack() as ts:
            posr = sb(ts, "posr", [128, 32])
            posc = sb(ts, "posc", [128, 1])
            invf = sb(ts, "invf", [128, 16])
            ang = sb(ts, "ang", [128, 33, 16])
            sn = [sb(ts, "sn%d" % i, [128, 33, 16]) for i in range(2)]
            cs = [sb(ts, "cs%d" % i, [128, 33, 16]) for i in range(2)]
            tq = sb(ts, "tq", [128, 33, 16])
            hpi = sb(ts, "hpi", [128, 1])
            p.op("pool", lambda e: e.iota(posr[:], [[2, 32]], base=0, channel_multiplier=0,
                                          allow_small_or_imprecise_dtypes=True), w=["posr"])
            p.op("pool", lambda e: e.tensor_scalar(out=posr[64:128, :], in0=posr[64:128, :], scalar1=1.0, scalar2=None,
                                                   op0=ALU.add), r=["posr"], w=["posr"])
            p.op("pool", lambda e: e.iota(posc[:], [[0, 1]], base=0, channel_multiplier=1,
                                          allow_small_or_imprecise_dtypes=True), w=["posc"])
            p.op("pool", lambda e: e.tensor_scalar(out=posc[64:128, :], in0=posc[64:128, :], scalar1=-64.0, scalar2=None,
                                                   op0=ALU.add), r=["posc"], w=["posc"])
            p.op("pool", lambda e: e.iota(invf[:], [[1, 16]], base=0, channel_multiplier=0,
                                          allow_small_or_imprecise_dtypes=True), w=["invf"])
            p.op("pool", lambda e: e.memset(hpi[:], 0.5 * math.pi), w=["hpi"])
            p.op("act", lambda e: e.activation(out=invf[:], in_=invf[:], func=AF.Exp, scale=-math.log(10000.0) / 16.0),
                 r=["invf"], w=["invf"])
            p.op("dve", lambda e: e.tensor_tensor(out=ang[:, 0:32, :], in0=posr.ap(0, [[1, 32], [0, 16]]),
                                                  in1=invf.ap(0, [[0, 32], [1, 16]]), op=ALU.mult),
                 r=["posr", "invf"], w=["ang"])
            p.op("dve", lambda e: e.tensor_scalar(out=ang[:, 32, :], in0=invf[:], scalar1=posc[:, 0:1], scalar2=None,
                                                  op0=ALU.mult), r=["posc", "invf", "ang"], w=["ang"])
            p.op("act", lambda e: e.activation(out=sn[0][:], in_=ang[:], func=AF.Sin, scale=1.0 / 64), r=["ang"], w=["sn0"])
            p.op("act", lambda e: e.activation(out=cs[0][:], in_=ang[:], func=AF.Sin, scale=1.0 / 64, bias=hpi[:, 0:1]),
                 r=["ang", "hpi"], w=["cs0"])
            for it in range(6):
                a, b = it % 2, (it + 1) % 2
                p.op("dve", lambda e, a=a: e.tensor_tensor(out=tq[:], in0=sn[a][:], in1=sn[a][:], op=ALU.mult),
                     r=["sn%d" % a], w=["tq"])
                p.op("dve", lambda e, a=a, b=b: e.scalar_tensor_tensor(out=sn[b][:], in0=sn[a][:], scalar=2.0, in1=cs[a][:],
                                                                       op0=ALU.mult, op1=ALU.mult),
                     r=["sn%d" % a, "cs%d" % a], w=["sn%d" % b])
                p.op("dve", lambda e, b=b: e.tensor_scalar(out=cs[b][:], in0=tq[:], scalar1=-2.0, scalar2=1.0, op0=ALU.mult,
                                                           op1=ALU.add), r=["tq"], w=["cs%d" % b])
            snf, csf = sn[0], cs[0]
            C, S = self.ropeC, self.ropeS
            colv = lambda t: t.ap(32 * 16, [[0, 32], [1, 16]])
            for blk in range(2):
                p.op("pool", lambda e, blk=blk: e.tensor_copy(out=C[:, :, blk * 16:(blk + 1) * 16], in_=csf[:, 0:32, :]),
                     r=["cs0"], w=["ropeC"])
                p.op("pool", lambda e, blk=blk: e.tensor_copy(out=C[:, :, 32 + blk * 16:48 + blk * 16], in_=colv(csf)),
                     r=["cs0"], w=["ropeC"])
            p.op("pool", lambda e: e.tensor_scalar(out=S[:, :, 0:16], in0=snf[:, 0:32, :], scalar1=-1.0, scalar2=None,
                                                   op0=ALU.mult), r=["sn0"], w=["ropeS"])
            p.op("pool", lambda e: e.tensor_copy(out=S[:, :, 16:32], in_=snf[:, 0:32, :]), r=["sn0"], w=["ropeS"])
            p.op("pool", lambda e: e.tensor_scalar(out=S[:, :, 32:48], in0=colv(snf), scalar1=-1.0, scalar2=None,
                                                   op0=ALU.mult), r=["sn0"], w=["ropeS"])
            p.op("pool", lambda e: e.tensor_copy(out=S[:, :, 48:64], in_=colv(snf)), r=["sn0"], w=["ropeS"])
            self.dump("ropeC", lambda: C[:], [128, 32, 64], ["ropeC"])
            self.dump("ropeS", lambda: S[:], [128, 32, 64], ["ropeS"])
            p.barrier()

    def seqs(self, rnd=None):
        out = []
        for i in range(NPROMPT):
            out.append(dict(kind="p", idx=i, T=TP, toff=i * TP, rtoff=i * TP, rc0=i * (TP // 64), cond=0, ST=256, rnd=0))
        out.append(dict(kind="s", idx=0, T=TS, toff=NPROMPT * TP, rtoff=0, rc0=0, cond=1, ST=512, rnd=1))
        if rnd is not None:
            out = [s for s in out if s["rnd"] == rnd]
        return out

    def x_src(self, l, sq, t0, n=128):
        if sq["kind"] == "p":
            t = self.xp if l == 0 else self.xp1
            return t.ap()[sq["idx"], t0:t0 + n, :]
        t = self.xs if l == 0 else self.xs1
        return t.ap()[t0:t0 + n, :]

    def x_dst(self, l, sq, t0, n=128):
        if sq["kind"] == "p":
            t = self.xp1 if l == 0 else self.yp
            return t.ap()[sq["idx"], t0:t0 + n, :]
        t = self.xs1 if l == 0 else self.ys
        return t.ap()[t0:t0 + n, :]

    def xkey(self, l, sq, t0):
        return ("xd", l, sq["kind"], sq["idx"], t0)

    def layer(self, l):
        p = self.p
        with contextlib.ExitStack() as esL:
            self.layer_params(esL, l)
            if self.stage(2):
                return
            self.ocT = self.sb(esL, "ocT", [128, 2, NTOK], BF16)
            for rnd in range(2):
                if self.stage(0):
                    return
                with contextlib.ExitStack() as esAB:
                    self.KQVT = self.sb(esAB, "KQVT", [128, 3, 2, RT], BF16)
                    self.SC = self.sb(esAB, "SC", [64, NSC, 2, RCH, 4])
                    self.EGL = self.sb(esAB, "EGL", [128, 2, RCH, 4])
                    self.abtok = self.sb(esAB, "abtok", [64, RCH, 16])
                    with contextlib.ExitStack() as esA:
                        self.phaseA_setup(esA, l)
                        if not self.stage(3):
                            for sq in self.seqs(rnd):
                                self.phaseA(l, sq)
                            if l == 0 and rnd == 0:
                                self.dump("KQVT", lambda: self.KQVT[:, :, :, 0:1024], [128, 3, 2, 1024], ["KQVT"])
                                self.dump("SC", lambda: self.SC[:, :, :, 0:16, :], [64, NSC, 2, 16, 4], ["SC"])
                                self.dump("abtok", lambda: self.abtok[:, 0:16, :], [64, 16, 16], ["abtok"])
                        p.barrier()
                    if self.stage(4):
                        continue
                    with contextlib.ExitStack() as esB:
                        self.phaseB_setup(esB, l)
                        for sq in self.seqs(rnd):
                            self.phaseB(l, sq)
                        p.barrier()
                    self.stage(5 + rnd)
            if self.stage(0):
                return
            with contextlib.ExitStack() as esC:
                self.phaseC_setup(esC, l)
                if not self.stage(7):
                    for sq in self.seqs():
                        self.phaseC(l, sq)
                p.barrier()
            self.stage(8)

    def layer_params(self, es, l):
        p, sb = self.p, self.sb
        self.modT = sb(es, "modT", [128, 2, 8, 2])
        self.esink = sb(es, "esink", [128, 8])
        self.lngb = sb(es, "lngb", [128, 2, 256])
        self.WsT = sb(es, "WsT", [128, 4, 128], BF16)
        self.bsT = sb(es, "bsT", [128, 4])
        self.cwT = sb(es, "cwT", [128, 6, 5])
        self.dtb = sb(es, "dtb", [64, 8])
        self.nea = sb(es, "nea", [64, 8])
        self.normg = sb(es, "normg", [128, 1])
        self.kTctx = sb(es, "kTctx", [64, 2, 256], BF16)
        self.vctx = sb(es, "vctx", [128, 2, 2, 64], BF16)
        modT = self.modT
        with contextlib.ExitStack() as ts:
            cv2 = sb(ts, "cv2", [2, D])
            scv = sb(ts, "scv", [2, D])
            scvT = sb(ts, "scvT", [128, 8, 2])
            wst = [sb(ts, "wmst%d" % i, [128, 3 * D]) for i in range(2)]
            bm2 = sb(ts, "bm2", [2, 3 * D])
            gp2 = sb(ts, "gp2", [2, 2, D])
            modsb = sb(ts, "modsb", [2, 3 * D])
            rows = sb(ts, "rows", [2, 3, D])
            p.op("sp", lambda e: e.dma_start(out=cv2[:], in_=self.cvec.ap()), w=["cv2"], dma="ld_cv2")
            p.op("act", lambda e: e.activation(out=scv[:], in_=cv2[:], func=AF.Silu), r=["cv2"], w=["scv"])
            for kc in range(8):
                p.op("pe", lambda e, kc=kc: e.transpose(self.psf(7, kc * 2, [[1, 2]]), scv[0:2, kc * 128:(kc + 1) * 128],
                                                        self.identf[0:2, 0:2]), r=["scv", "identf"], w=[("ps", 7)])
            p.op("dve", lambda e: e.tensor_copy(out=scvT[:].rearrange("p a b -> p (a b)"), in_=self.psf(7, 0, [[1, 16]])),
                 r=[("ps", 7)], w=["scvT"])
            for i in range(2):
                p.op("sp", lambda e, i=i: e.dma_start(out=bm2[i:i + 1, :], in_=self.b_mod.ap()[l:l + 1, :]), w=["bm2"],
                     dma="ld_bm2")
                p.op("sp", lambda e, i=i: e.dma_start(out=gp2[i:i + 1, 0, :], in_=self.g_pre.ap()[l:l + 1, :]), w=["gp2"],
                     dma="ld_gp2")
                p.op("sp", lambda e, i=i: e.dma_start(out=gp2[i:i + 1, 1, :], in_=self.g_post.ap()[l:l + 1, :]), w=["gp2"],
                     dma="ld_gp2")
            for kc in range(8):
                st = wst[kc % 2]
                p.op("sp", lambda e, kc=kc, st=st: e.dma_start(out=st[:], in_=self.w_mod.ap()[l, kc * 128:(kc + 1) * 128, :]),
                     w=[("wmst", kc % 2)], dma="ld_wm%d" % (kc % 2))
                for n in range(6):
                    p.op("pe", lambda e, kc=kc, st=st, n=n: e.matmul(self.psf(n, 0, [[1, 512]], parts=2), lhsT=scvT[:, kc, :],
                                                                     rhs=st[:, n * 512:(n + 1) * 512], start=(kc == 0),
                                                                     stop=(kc == 7)),
                         r=["scvT", ("wmst", kc % 2)], w=[("ps", n)])
            for n in range(6):
                p.op("dve", lambda e, n=n: e.tensor_tensor(out=modsb[:, n * 512:(n + 1) * 512],
                                                           in0=self.psf(n, 0, [[1, 512]], parts=2),
                                                           in1=bm2[:, n * 512:(n + 1) * 512], op=ALU.add),
                     r=[("ps", n), "bm2"], w=["modsb"])
            p.op("dve", lambda e: e.scalar_tensor_tensor(out=rows[:, 0, :], in0=modsb[:, D:2 * D], scalar=1.0, in1=gp2[:, 0, :],
                                                         op0=ALU.add, op1=ALU.mult), r=["modsb", "gp2"], w=["rows"])
            p.op("dve", lambda e: e.tensor_copy(out=rows[:, 1, :], in_=modsb[:, 0:D]), r=["modsb"], w=["rows"])
            p.op("dve", lambda e: e.tensor_tensor(out=rows[:, 2, :], in0=modsb[:, 2 * D:3 * D], in1=gp2[:, 1, :], op=ALU.mult),
                 r=["modsb", "gp2"], w=["rows"])
            p.op("sp", lambda e: e.dma_start(out=self.gps.ap()[l], in_=rows[:, 2, :]), r=["rows"], w=[("gps", l)],
                 dma="st_gps")
            for a in range(2):
                for kc in range(8):
                    p.op("pe", lambda e, a=a, kc=kc: e.transpose(self.psf(7, (a * 8 + kc) * 2, [[1, 2]]),
                                                                 rows[0:2, a, kc * 128:(kc + 1) * 128], self.identf[0:2, 0:2]),
                         r=["rows", "identf"], w=[("ps", 7)])
            p.op("dve", lambda e: e.tensor_copy(out=modT[:].rearrange("p a b c -> p (a b c)"), in_=self.psf(7, 0, [[1, 32]])),
                 r=[("ps", 7)], w=["modT"])
            self.dump("modT%d" % l, lambda: modT[:], [128, 2, 8, 2], ["modT"])

            stg = sb(ts, "stg", [128, 4, 128])
            p.op("sp", lambda e: e.dma_start(out=self.esink[:], in_=bass.AP(self.sink, l * 8, [[0, 128], [1, 8]])),
                 w=["esink"], dma="ld_sink")
            p.op("act", lambda e: e.activation(out=self.esink[:], in_=self.esink[:], func=AF.Exp), r=["esink"], w=["esink"])
            p.op("sp", lambda e: e.dma_start(out=self.lngb[:, 0, :], in_=bass.AP(self.ln_g, l * 256, [[0, 128], [1, 256]])),
                 w=["lngb"], dma="ld_lng")
            p.op("sp", lambda e: e.dma_start(out=self.lngb[:, 1, :], in_=bass.AP(self.ln_b, l * 256, [[0, 128], [1, 256]])),
                 w=["lngb"], dma="ld_lng")
            p.op("sp", lambda e: e.dma_start(out=self.dtb[:], in_=bass.AP(self.dt_bias, l * 8, [[0, 64], [1, 8]])),
                 w=["dtb"], dma="ld_dtb")
            p.op("sp", lambda e: e.dma_start(out=self.nea[:], in_=bass.AP(self.a_log, l * 8, [[0, 64], [1, 8]])),
                 w=["nea"], dma="ld_nea")
            p.op("act", lambda e: e.activation(out=self.nea[:], in_=self.nea[:], func=AF.Exp), r=["nea"], w=["nea"])
            p.op("dve", lambda e: e.tensor_scalar(out=self.nea[:], in0=self.nea[:], scalar1=-1.0, scalar2=None, op0=ALU.mult),
                 r=["nea"], w=["nea"])
            for hf in range(2):
                p.op("sp", lambda e, hf=hf: e.dma_start(out=self.normg[hf * 64:(hf + 1) * 64, :],
                                                        in_=bass.AP(self.norm_g, l * 64, [[1, 64], [1, 1]])),
                     w=["normg"], dma="ld_normg")
            p.op("sp", lambda e: e.dma_start(out=stg[:], in_=self.sgu_w.ap()[l].rearrange("g t s -> t g s")), w=["stg"],
                 dma="ld_stg")
            for g in range(4):
                p.op("pe", lambda e, g=g: e.transpose(self.psf(0, g * 128, [[1, 128]]), stg[:, g, :], self.identf[:]),
                     r=["stg", "identf"], w=[("ps", 0)])
            p.op("act", lambda e: e.activation(out=self.WsT[:].rearrange("p a b -> p (a b)"), in_=self.psf(0, 0, [[1, 512]]),
                                               func=AF.Copy), r=[("ps", 0)], w=["WsT"])
            stg2 = sb(ts, "stg2", [8, 768])
            p.op("sp", lambda e: e.dma_start(out=stg2[0:4, 0:128], in_=self.sgu_b.ap()[l]), w=["stg2"], dma="ld_stg2")
            p.op("pe", lambda e: e.transpose(self.psf(1, 0, [[1, 4]]), stg2[0:4, 0:128], self.identf[0:4, 0:4]),
                 r=["stg2", "identf"], w=[("ps", 1)])
            p.op("dve", lambda e: e.tensor_copy(out=self.bsT[:], in_=self.psf(1, 0, [[1, 4]])), r=[("ps", 1)], w=["bsT"])
            p.op("sp", lambda e: e.dma_start(out=stg2[0:5, :], in_=self.conv_w.ap()[l]), w=["stg2"], dma="ld_stg2")
            for b in range(6):
                p.op("pe", lambda e, b=b: e.transpose(self.psf(1, 8 + b * 5, [[1, 5]]), stg2[0:5, b * 128:(b + 1) * 128],
                                                      self.identf[0:5, 0:5]), r=["stg2", "identf"], w=[("ps", 1)])
            p.op("dve", lambda e: e.tensor_copy(out=self.cwT[:].rearrange("p a b -> p (a b)"), in_=self.psf(1, 8, [[1, 30]])),
                 r=[("ps", 1)], w=["cwT"])
            ckf = sb(ts, "ckf", [128, 2, 2, 128])
            kcb = sb(ts, "kcb", [128, 2, 2, 64], BF16)
            p.op("sp", lambda e: e.dma_start(out=ckf[:, 0], in_=self.ck.ap()[l].rearrange("(b p) c -> p b c", p=128)),
                 w=["ckf"], dma="ld_ckf")
            p.op("sp", lambda e: e.dma_start(out=ckf[:, 1], in_=self.cv.ap()[l].rearrange("(b p) c -> p b c", p=128)),
                 w=["ckf"], dma="ld_ckf")
            p.op("dve", lambda e: e.tensor_copy(out=kcb[:], in_=ckf[:, 0].rearrange("p b (k d) -> p b k d", k=2)),
                 r=["ckf"], w=["kcb"])
            p.op("dve", lambda e: e.tensor_copy(out=self.vctx[:], in_=ckf[:, 1].rearrange("p b (k d) -> p b k d", k=2)),
                 r=["ckf"], w=["vctx"])
            for blk in range(2):
                for kv in range(2):
                    p.op("pe", lambda e, blk=blk, kv=kv: e.transpose(
                        self.psb(2, (kv * 2 + blk) * 128, [[1, 128]], parts=64), kcb[:, blk, kv, :], self.identb[:]),
                        r=["kcb", "identb"], w=[("ps", 2)])
            p.op("act", lambda e: e.activation(out=self.kTctx[:].rearrange("p a b -> p (a b)"),
                                               in_=self.psb(2, 0, [[1, 512]], parts=64), func=AF.Copy), r=[("ps", 2)],
                 w=["kTctx"])
            p.barrier()

    def make_hT(self, l, sq, t0, hT, col0, slot):
        g = self.make_hT_gen(l, sq, t0, hT, col0, slot)
        try:
            while True:
                next(g)
        except StopIteration as e:
            return e.value

    def make_hT_gen(self, l, sq, t0, hT, col0, slot):
        p = self.p
        xt = self.xin[slot]
        c = sq["cond"]
        xk = ("xin", slot)
        src = self.x_src(l, sq, t0)
        p.op("sp", lambda e: e.dma_start(out=xt[:], in_=src), r=[self.xkey(l, sq, t0)], w=[xk], dma="ld_xin%d" % slot)
        junk, st = self.junk, self.stat
        sk = ("stat", slot)
        yield
        p.op("act", lambda e: e.activation(out=junk[:], in_=xt[:], func=AF.Square, accum_out=st[:, slot, 0:1]),
             r=[xk], w=["junk", sk])
        p.op("act", lambda e: e.activation(out=st[:, slot, 1:2], in_=st[:, slot, 0:1], func=AF.Ln, scale=1.0 / D,
                                           bias=self.epsc[:, 0:1]), r=[sk, "epsc"], w=[sk])
        p.op("act", lambda e: e.activation(out=st[:, slot, 2:3], in_=st[:, slot, 1:2], func=AF.Exp, scale=-0.5),
             r=[sk], w=[sk])
        yield
        xn = self.xn
        p.op("act", lambda e: e.activation(out=xn[:], in_=xt[:], func=AF.Identity, scale=st[:, slot, 2:3]),
             r=[xk, sk], w=["xn"])
        yield
        for kc in range(8):
            p.op("pe", lambda e, kc=kc: e.transpose(self.psb(6, kc * 128, [[1, 128]]), xn[:, kc * 128:(kc + 1) * 128],
                                                    self.identb[:]), r=["xn", "identb"], w=[("ps", 6)])
        yield
        mt = self.modT
        hk = ("hT", id(hT), col0)
        for kc in range(8):
            sc_ap = lambda kc=kc: mt.ap(kc * 2 + c, [[1, 1]])
            sh_ap = lambda kc=kc: mt.ap(16 + kc * 2 + c, [[1, 1]])
            if False:
                p.op("act", lambda e, kc=kc, sc_ap=sc_ap, sh_ap=sh_ap: e.activation(
                    out=hT[:, kc, col0:col0 + 128], in_=self.psb(6, kc * 128, [[1, 128]]), func=AF.Identity,
                    scale=sc_ap(), bias=sh_ap()), r=[("ps", 6), "modT"], w=[hk])
            else:
                p.op("dve", lambda e, kc=kc, sc_ap=sc_ap, sh_ap=sh_ap: e.tensor_scalar(
                    out=hT[:, kc, col0:col0 + 128], in0=self.psb(6, kc * 128, [[1, 128]]), scalar1=sc_ap(), scalar2=sh_ap(),
                    op0=ALU.mult, op1=ALU.add), r=[("ps", 6), "modT"], w=[hk])
        yield
        return hk

    def phaseA_setup(self, es, l):
        p, sb = self.p, self.sb
        self.wA = sb(es, "wA", [128, 8, WA_COLS], BF16)
        for kc in range(8):
            p.op("pool", lambda e, kc=kc: e.dma_start(out=self.wA[:, kc, :],
                                                      in_=self.w_in.ap()[l, kc * 128:(kc + 1) * 128, 2048:3088]),
                 w=["wA"], dma="ld_wA")
        self.convdiag = sb(es, "convdiag", [128, 30, 128], BF16)
        p.op("dve", lambda e: e.tensor_tensor(out=self.convdiag[:], in0=self.identb.ap(0, [[0, 30], [1, 128]]),
                                              in1=self.cwT.ap(0, [[1, 30], [0, 128]]), op=ALU.mult),
             r=["cwT", "identb"], w=["convdiag"])
        self.xin = [sb(es, "xinA%d" % i, [128, D]) for i in range(2)]
        self.junk = sb(es, "junkA", [128, D], BF16)
        self.stat = sb(es, "statA", [128, 2, 4])
        self.xn = sb(es, "xnA", [128, D], BF16)
        self.hTtmp = sb(es, "hTtmpA", [128, 8, 128])
        self.hTA = [sb(es, "hTA%d" % i, [128, 8, 512], BF16) for i in range(2)]
        self.convin = [sb(es, "convin%d" % i, [128, 6, 516], BF16) for i in range(2)]
        self.cgst = [sb(es, "cgst%d" % i, [128, 2, 512], BF16) for i in range(2)]
        self.ysil = sb(es, "ysil", [128, 512])
        self.sqb = sb(es, "sqb", [128, 512], BF16)
        self.rn = sb(es, "rn", [128, 512])
        self.scA = sb(es, "scA", [64, 3, 2, 64, 4])

    def phaseA(self, l, sq):
        ST, T = sq["ST"], sq["T"]
        NS = T // ST
        NTS = ST // 128
        hks = {}

        def H(s):
            hT = self.hTA[s % 2]
            ks = []
            for j in range(NTS):
                hk = yield from self.make_hT_gen(l, sq, s * ST + j * 128, hT, j * 128, self.tileA % 2)
                self.tileA += 1
                ks.append(hk)
            hks[s] = ks

        def run(gens):
            gens = [g for g in gens if g is not None]
            while gens:
                for g in list(gens):
                    try:
                        next(g)
                    except StopIteration:
                        gens.remove(g)

        run([H(0)])
        for s in range(NS + 1):
            run([self.phaseA_P(l, sq, s, NS, hks.get(s)), H(s + 1) if s + 1 < NS else None])
        self.gdn_scalars(sq)

    def phaseA_P(self, l, sq, s, NS, hks):
        p = self.p
        ST, toff = sq["ST"], sq["toff"]
        wA = self.wA
        if s < NS:
            t0 = s * ST
            hT = self.hTA[s % 2]
            cin = self.convin[s % 2]
            ck_ = ("convin", s % 2)
            cslot = self.stA % 2
            self.stA += 1
            cgs = self.cgst[cslot]
            for cb in range(8):
                col = cb * 128 if cb < 6 else A_CG + (cb - 6) * 128
                b = cb % 4
                for kc in range(8):
                    p.op("pe", lambda e, kc=kc, col=col, b=b, hT=hT: e.matmul(
                        self.psf(b, 0, [[1, ST]]), lhsT=wA[:, kc, col:col + 128], rhs=hT[:, kc, 0:ST],
                        start=(kc == 0), stop=(kc == 7)), r=["wA"] + hks, w=[("ps", b)])
                yield
                if cb < 6 and cb % 2 == 1:
                    p.op("dve", lambda e, cb=cb, b=b, cin=cin: e.tensor_copy(out=cin[:, cb, 2:2 + ST],
                                                                             in_=self.psf(b, 0, [[1, ST]])),
                         r=[("ps", b)], w=[ck_])
                elif cb < 6:
                    p.op("act", lambda e, cb=cb, b=b, cin=cin: e.activation(out=cin[:, cb, 2:2 + ST],
                                                                            in_=self.psf(b, 0, [[1, ST]]), func=AF.Copy),
                         r=[("ps", b)], w=[ck_])
                else:
                    p.op("act", lambda e, cb=cb, b=b, cgs=cgs: e.activation(out=cgs[:, cb - 6, 0:ST],
                                                                            in_=self.psf(b, 0, [[1, ST]]), func=AF.Silu),
                         r=[("ps", b)], w=[("cgst", cslot)])
            g0 = toff + t0
            p.op("sp", lambda e, cgs=cgs, g0=g0: e.dma_start(out=self.cgD.ap()[:, :, g0:g0 + ST], in_=cgs[:, :, 0:ST]),
                 r=[("cgst", cslot)], w=[("cgD", g0)], dma="st_cg%d" % cslot)
            nch = ST // 64
            for c in range(nch):
                for kc in range(8):
                    p.op("pe", lambda e, c=c, kc=kc, hT=hT: e.matmul(
                        self.psf(5, c * 16, [[1, 16]], parts=64), lhsT=hT[:, kc, c * 64:(c + 1) * 64],
                        rhs=wA[:, kc, A_CA:A_CA + 16], start=(kc == 0), stop=(kc == 7)), r=["wA"] + hks, w=[("ps", 5)])
            yield
            ch0 = sq["rc0"] + s * nch
            p.op("dve", lambda e, ch0=ch0, nch=nch: e.tensor_copy(
                out=self.abtok[:, ch0:ch0 + nch, :], in_=self.psf(5, 0, [[16, nch], [1, 16]], parts=64)),
                r=[("ps", 5)], w=["abtok"])
            if s == 0:
                p.op("pool", lambda e, cin=cin: e.memset(cin[:, :, 0:2], 0.0), w=[ck_])
            else:
                prev = self.convin[(s - 1) % 2]
                pk = ("convin", (s - 1) % 2)
                p.op("pool", lambda e, cin=cin, prev=prev: e.tensor_copy(out=cin[:, :, 0:2], in_=prev[:, :, ST:ST + 2]),
                     r=[pk], w=[ck_])
                p.op("pool", lambda e, cin=cin, prev=prev: e.tensor_copy(out=prev[:, :, ST + 2:ST + 4], in_=cin[:, :, 2:4]),
                     r=[ck_], w=[pk])
            if s == NS - 1:
                p.op("pool", lambda e, cin=cin: e.memset(cin[:, :, ST + 2:ST + 4], 0.0), w=[ck_])
            yield
        if s >= 1:
            yield from self.conv_tile(sq, s - 1, ST)

    def conv_tile(self, sq, s, ST):
        p = self.p
        cin = self.convin[s % 2]
        ck_ = ("convin", s % 2)
        g0 = sq["rtoff"] + s * ST
        for cb in range(6):
            b = 4 + (cb % 2) * 3
            for j in range(5):
                p.op("pe", lambda e, cb=cb, j=j, b=b: e.matmul(
                    self.psf(b, 0, [[1, ST]]), lhsT=self.convdiag[:, cb * 5 + j, :], rhs=cin[:, cb, j:j + ST],
                    start=(j == 0), stop=(j == 4)), r=[ck_, "convdiag"], w=[("ps", b)])
            yield
            if cb >= 4:
                p.op("act", lambda e, cb=cb, b=b: e.activation(out=self.KQVT[:, 2, cb - 4, g0:g0 + ST],
                                                               in_=self.psf(b, 0, [[1, ST]]), func=AF.Silu),
                     r=[("ps", b)], w=["KQVT"])
                continue
            which = 1 if cb < 2 else 0
            hp = cb % 2
            ys, sqb, rn = self.ysil, self.sqb, self.rn
            p.op("act", lambda e, b=b: e.activation(out=ys[:, 0:ST], in_=self.psf(b, 0, [[1, ST]]), func=AF.Silu),
                 r=[("ps", b)], w=["ysil"])
            p.op("act", lambda e: e.activation(out=sqb[:, 0:ST], in_=ys[:, 0:ST], func=AF.Square), r=["ysil"], w=["sqb"])
            yield
            p.op("pe", lambda e: e.matmul(self.psf(3, 0, [[1, ST]]), lhsT=self.onesblk[:], rhs=sqb[:, 0:ST], start=True,
                                          stop=True), r=["sqb", "onesblk"], w=[("ps", 3)])
            yield
            p.op("act", lambda e: e.activation(out=rn[:, 0:ST], in_=self.psf(3, 0, [[1, ST]]), func=AF.Ln,
                                               bias=self.epsc[:, 0:1]), r=[("ps", 3), "epsc"], w=["rn"])
            p.op("act", lambda e: e.activation(out=rn[:, 0:ST], in_=rn[:, 0:ST], func=AF.Exp, scale=-0.5), r=["rn"], w=["rn"])
            sc = 0.125 if which == 1 else 1.0
            p.op("dve", lambda e, which=which, hp=hp, sc=sc: e.scalar_tensor_tensor(
                out=self.KQVT[:, which, hp, g0:g0 + ST], in0=ys[:, 0:ST], scalar=sc, in1=rn[:, 0:ST], op0=ALU.mult,
                op1=ALU.mult), r=["ysil", "rn"], w=["KQVT"])

    def gdn_scalars(self, sq):
        p = self.p
        NCH = sq["T"] // 64
        c0 = sq["rc0"]
        sc, SC, EGL = self.scA, self.SC, self.EGL
        F = 64 * 4
        n = NCH * 4

        def scv(i, d=None):
            if d is None:
                return sc.ap(i * 2 * F, [[F, 2], [4, NCH], [1, 4]])
            return sc.ap((i * 2 + d) * F, [[1, n]])

        def SCv(q):
            return SC.ap(((q * 2) * RCH + c0) * 4, [[RCH * 4, 2], [4, NCH], [1, 4]])

        ab = self.abtok
        a_in = lambda: ab.ap(c0 * 16, [[4, 2], [16, NCH], [1, 4]])
        b_in = lambda: ab.ap(c0 * 16 + 8, [[4, 2], [16, NCH], [1, 4]])
        dtb_bc = lambda: self.dtb.ap(0, [[4, 2], [0, NCH], [1, 4]])
        nea_bc = lambda: self.nea.ap(0, [[4, 2], [0, NCH], [1, 4]])
        p.op("dve", lambda e: e.tensor_tensor(out=scv(0), in0=a_in(), in1=dtb_bc(), op=ALU.add), r=["abtok", "dtb"], w=["sc0"])
        p.op("act", lambda e: e.activation(out=scv(0), in_=scv(0), func=AF.Exp), r=["sc0"], w=["sc0"])
        p.op("act", lambda e: e.activation(out=scv(0), in_=scv(0), func=AF.Ln, bias=1.0), r=["sc0"], w=["sc0"])
        p.op("dve", lambda e: e.tensor_tensor(out=scv(0), in0=scv(0), in1=nea_bc(), op=ALU.mult), r=["sc0", "nea"], w=["sc0"])
        p.op("act", lambda e: e.activation(out=scv(1), in_=b_in(), func=AF.Exp, scale=-1.0), r=["abtok"], w=["sc1"])
        p.op("act", lambda e: e.activation(out=scv(1), in_=scv(1), func=AF.Ln, bias=1.0), r=["sc1"], w=["sc1"])
        p.op("pe", lambda e: e.matmul(self.psf(0, 0, [[1, n]], parts=64), lhsT=self.Uf[:], rhs=scv(0, 0), start=True, stop=True),
             r=["sc0", "Uf"], w=[("ps", 0)])
        p.op("pe", lambda e: e.matmul(self.psf(0, n, [[1, n]], parts=64), lhsT=self.Ub[:], rhs=scv(0, 1), start=True, stop=True),
             r=["sc0", "Ub"], w=[("ps", 0)])
        p.op("pe", lambda e: e.matmul(self.psf(1, 0, [[n, 2], [1, n]]), lhsT=self.ones64[:],
                                      rhs=sc.ap(0, [[F, 2], [1, n]]), start=True, stop=True), r=["sc0", "ones64"], w=[("ps", 1)])
        ps_gc = lambda: self.psf(0, 0, [[n, 2], [4, NCH], [1, 4]], parts=64)
        ps_gl = lambda parts: self.psf(1, 0, [[n, 2], [4, NCH], [1, 4]], parts=parts)
        p.op("act", lambda e: e.activation(out=SCv(SC_GC), in_=ps_gc(), func=AF.Copy), r=[("ps", 0)], w=["SC"])
        p.op("dve", lambda e: e.tensor_tensor(out=SCv(SC_GB), in0=ps_gc(), in1=scv(1), op=ALU.subtract),
             r=[("ps", 0), "sc1"], w=["SC"])
        p.op("act", lambda e: e.activation(out=SCv(SC_EGC), in_=ps_gc(), func=AF.Exp), r=[("ps", 0)], w=["SC"])
        p.op("act", lambda e: e.activation(out=SCv(SC_BETA), in_=scv(1), func=AF.Exp, scale=-1.0), r=["sc1"], w=["SC"])
        p.op("act", lambda e: e.activation(out=SCv(SC_BEXP), in_=SCv(SC_GB), func=AF.Exp), r=["SC"], w=["SC"])
        p.op("act", lambda e: e.activation(out=EGL.ap(c0 * 4, [[RCH * 4, 2], [4, NCH], [1, 4]]), in_=ps_gl(128), func=AF.Exp),
             r=[("ps", 1)], w=["EGL"])
        p.op("dve", lambda e: e.tensor_tensor(out=scv(2), in0=ps_gl(64), in1=SCv(SC_GC), op=ALU.subtract),
             r=[("ps", 1), "SC"], w=["sc2"])
        p.op("act", lambda e: e.activation(out=SCv(SC_EGLMGC), in_=scv(2), func=AF.Exp), r=["sc2"], w=["SC"])

    def phaseB_setup(self, es, l):
        sb = self.sb
        self.S = sb(es, "S", [128, 8, 64])
        self.Sbf = sb(es, "Sbf", [128, 8, 64], BF16)
        self.St = sb(es, "St", [128, 8, 64])
        NB = 2
        self.NB = NB
        mk = lambda name, shape, dt=F32: [sb(es, "%s%d" % (name, i), shape, dt) for i in range(NB)]
        self.E12 = mk("E12", [64, 2, 8, 64])
        self.CBt = mk("CBt", [64, 8, 64])
        self.D12 = mk("D12", [64, 2, 8, 64])
        self.Bm = mk("Bm", [64, 6, 8, 64], BF16)
        self.Am = mk("Am", [64, 5, 8, 64], BF16)
        self.Aqk = mk("Aqk", [64, 8, 64], BF16)
        self.R0b = mk("R0b", [64, 8, 128], BF16)
        self.Qm = mk("Qm", [64, 8, 64], BF16)
        self.R = mk("R", [64, 8, 128], BF16)
        self.KD = mk("KD", [64, 8, 128], BF16)
        self.wT = mk("wT", [64, 8, 64], BF16)
        self.vnew = mk("vnew", [64, 8, 64], BF16)
        self.o1 = mk("o1", [64, 8, 64])
        self.ob = mk("ob", [64, 8, 64], BF16)
        self.sqf = sb(es, "sqf", [128, 512], BF16)
        self.rnf = sb(es, "rnf", [128, 512])
        self.cgl = [sb(es, "cgl%d" % i, [128, 2, 512], BF16) for i in range(2)]
        self.octmp = sb(es, "octmp", [128, 512])

    def sc_step(self, q, sq, s, inner, tile=None):
        NCH = sq["T"] // 64
        c0 = sq["rc0"]
        dstride = RCH * 4 + (NCH - 1 - 2 * s) * 4
        if tile is None:
            return self.SC.ap(((q * 2) * RCH + c0 + s) * 4, [[dstride, 2], [1, 4], [0, inner]])
        return tile.ap((c0 + s) * 4, [[dstride, 2], [1, 4], [0, inner]])

    def phaseB(self, l, sq):
        p = self.p
        NCH = sq["T"] // 64
        S, Sbf = self.S, self.Sbf
        if sq["kind"] == "p":
            p.op("pool", lambda e: e.memset(S[:], 0.0), w=["S"])
            p.op("pool", lambda e: e.memset(Sbf[:], 0.0), w=["Sbf"])
        else:
            for hf in range(2):
                p.op("sp", lambda e, hf=hf: e.dma_start(out=S[hf * 64:(hf + 1) * 64],
                                                        in_=self.sd.ap()[l].rearrange("d h k v -> k (d h) v")), w=["S"],
                     dma="ld_S")
            p.op("act", lambda e: e.activation(out=Sbf[:], in_=S[:], func=AF.Copy), r=["S"], w=["Sbf"])
        active, nxt = [], 0
        self.scan_done = 0
        while nxt < NCH or active:
            if len(active) < 2 and nxt < NCH:
                active.append(self.gdn_step(sq, nxt, nxt % 2))
                nxt += 1
            for g in list(active):
                try:
                    next(g)
                except StopIteration:
                    active.remove(g)
        if sq["kind"] == "p":
            p.op("sp", lambda e: e.dma_start(out=self.ns.ap()[sq["idx"], l].rearrange("d h k v -> k (d h) v"), in_=S[0:64]),
                 r=["S"], dma="st_S")
        self.gdn_finish(sq)

    def ps2(self, b, off):
        return bass.AP(self.PSALL, b * 512 + off, [[4096, 64], [512, 2], [128, 4], [1, 64]])

    def ps2f(self, b):
        return bass.AP(self.PSALL, b * 512, [[4096, 64], [512, 2], [1, 512]])

    def gdn_step(self, sq, s, bset):
        p = self.p
        NCH = sq["T"] // 64
        cf, cbk = s, NCH - 1 - s
        tk = [sq["rtoff"] + cf * 64, sq["rtoff"] + cbk * 64]
        tg = [sq["toff"] + cf * 64, sq["toff"] + cbk * 64]
        i = bset
        ba = 4 * bset
        bb, bc, bd = ba + 1, ba + 2, ba + 3
        K = self.KQVT
        kq = lambda name: (name, i)
        identf, identb = self.identf, self.identb

        def kap(which, h, d, dims):
            return K.ap((which * 2 + h // 2) * RT + tk[d], dims, parts=64, pbase=(h % 2) * 64)

        def ps2(off):
            return bass.AP(self.PSALL, ba * 512 + off, [[4096, 64], [512, 2], [128, 4], [1, 64]])

        def ps2f(b):
            return bass.AP(self.PSALL, b * 512, [[4096, 64], [512, 2], [1, 512]])

        for d in range(2):
            for h in (0, 2, 1, 3):
                p.op("pe", lambda e, d=d, h=h: e.matmul(
                    self.psf(ba + d, h * 128, [[64, 2], [1, 64]], parts=64), lhsT=kap(0, h, d, [[1, 64]]),
                    rhs=kap(0, h, d, [[2 * RT, 2], [1, 64]]), start=True, stop=True), r=["KQVT"], w=[("ps", ba + d)],
                    rg=(h % 2) * 64)
        E, CB, Dm = self.E12[i], self.CBt[i], self.D12[i]
        for which, q in ((0, SC_GB), (1, SC_GC)):
            p.op("dve", lambda e, which=which, q=q: e.tensor_tensor(
                out=E.ap(which * 512, D3), in0=identf.ap(0, [[0, 2], [0, 4], [1, 64]], parts=64),
                in1=self.sc_step(q, sq, s, 64), op=ALU.mult), r=["SC", "identf"], w=[kq("E12")])
        p.op("act", lambda e: e.activation(out=CB.ap(0, D3), in_=self.sc_step(SC_GC, sq, s, 64), func=AF.Copy),
             r=["SC"], w=[kq("CBt")])
        yield
        for which in range(2):
            b = bc + which
            p.op("pe", lambda e, which=which, b=b: e.matmul(self.psf(b, 0, [[1, 512]], parts=64), lhsT=identb[0:64, 0:64],
                                                            rhs=self.NEG[:, which].rearrange("p a b -> p (a b)"),
                                                            start=True, stop=False), r=["NEG", "identb"], w=[("ps", b)])
            p.op("pe", lambda e, which=which, b=b: e.matmul(self.psf(b, 0, [[1, 512]], parts=64), lhsT=self.ones64[:, 0:64],
                                                            rhs=E.ap(which * 512, [[1, 512]]), start=False, stop=False),
                 r=[kq("E12"), "ones64"], w=[("ps", b)])
            p.op("pe", lambda e, which=which, b=b: e.matmul(self.psf(b, 0, [[1, 512]], parts=64), lhsT=self.nidentf[:],
                                                            rhs=CB.ap(0, [[1, 512]]), start=False, stop=True),
                 r=[kq("CBt"), "nidentf"], w=[("ps", b)])
        yield
        for which in range(2):
            b = bc + which
            p.op("act", lambda e, which=which, b=b: e.activation(out=Dm.ap(which * 512, [[1, 512]]),
                                                                 in_=self.psf(b, 0, [[1, 512]], parts=64), func=AF.Exp),
                 r=[("ps", b)], w=[kq("D12")])
        yield
        Bm, Am, Aqk = self.Bm[i], self.Am[i], self.Aqk[i]
        p.op("dve", lambda e: e.scalar_tensor_tensor(out=Bm.ap(0, D3), in0=ps2(0), scalar=-1.0, in1=Dm.ap(0, D3),
                                                     op0=ALU.mult, op1=ALU.mult),
             r=[("ps", ba), ("ps", bb), kq("D12")], w=[kq("Bm")])
        p.op("dve", lambda e: e.tensor_tensor(out=Aqk.ap(0, D3), in0=ps2(64), in1=Dm.ap(512, D3), op=ALU.mult),
             r=[("ps", ba), ("ps", bb), kq("D12")], w=[kq("Aqk")])
        yield
        p.op("dve", lambda e: e.tensor_tensor(out=Bm.ap(5 * 512, [[1, 512]]), in0=Bm.ap(0, [[1, 512]]),
                                              in1=self.blkm[:].rearrange("p a b -> p (a b)"), op=ALU.mult),
             r=[kq("Bm"), "blkm"], w=[kq("Bm")])
        p.op("dve", lambda e: e.tensor_tensor(out=Bm.ap(0, [[1, 512]]), in0=Bm.ap(0, [[1, 512]]),
                                              in1=Bm.ap(5 * 512, [[1, 512]]), op=ALU.subtract), r=[kq("Bm")], w=[kq("Bm")])
        for h in (0, 2, 1, 3):
            for d in range(2):
                pb = (h % 2) * 64
                for slot_, wi in ((0, 2), (1, 0)):
                    p.op("pe", lambda e, d=d, h=h, pb=pb, slot_=slot_, wi=wi: e.transpose(
                        self.psb(bb, slot_ * 512 + (d * 4 + h) * 64, [[1, 64]], parts=64), kap(wi, h, d, [[1, 64]]),
                        identb[pb:pb + 64, pb:pb + 64]), r=["KQVT", "identb"], w=[("ps", bb)], rg=pb)
        yield
        lv = lambda k: 5 if k == 0 else k
        for q_ in range(8):
            p.op("pe", lambda e, q_=q_: e.transpose(self.psb(ba, q_ * 64, [[1, 64]], parts=64),
                                                    Bm.ap((5 * 8 + q_) * 64, [[1, 64]]), identb[0:64, 0:64]),
                 r=[kq("Bm"), "identb"], w=[("ps", ba)])
        R, R0, KD = self.R[i], self.R0b[i], self.KD[i]
        pv = lambda slot_: self.psb(bb, slot_ * 512, D3, parts=64)
        p.op("dve", lambda e: e.tensor_tensor(out=R0.ap(0, [[512, 2], [128, 4], [1, 64]]), in0=pv(0),
                                              in1=self.sc_step(SC_BETA, sq, s, 64), op=ALU.mult),
             r=[("ps", bb), "SC"], w=[kq("R0")])
        p.op("dve", lambda e: e.tensor_tensor(out=R0.ap(64, [[512, 2], [128, 4], [1, 64]]), in0=pv(1),
                                              in1=self.sc_step(SC_BEXP, sq, s, 64), op=ALU.mult),
             r=[("ps", bb), "SC"], w=[kq("R0")])
        yield
        p.op("act", lambda e: e.activation(out=Am.ap(0, [[1, 512]]), in_=self.psb(ba, 0, [[1, 512]], parts=64), func=AF.Copy),
             r=[("ps", ba)], w=[kq("Am")])
        for dup in range(2):
            p.op("dve", lambda e, dup=dup: e.tensor_tensor(out=KD.ap(dup * 64, [[512, 2], [128, 4], [1, 64]]), in0=pv(1),
                                                           in1=self.sc_step(SC_EGLMGC, sq, s, 64), op=ALU.mult),
                 r=[("ps", bb), "SC"], w=[kq("KD")])
        yield
        evac_ctr = [0]
        Qm = self.Qm[i]

        def apply(lhs_fn, rhs_t, rkeys, first):
            for q_ in range(8):
                p.op("pe", lambda e, q_=q_: e.matmul(
                    self.psf(bc + q_ // 4, (q_ % 4) * 128, [[1, 128]], parts=64), lhsT=lhs_fn(q_),
                    rhs=rhs_t.ap(q_ * 128, [[1, 128]]), start=(first and q_ % 4 == 0), stop=True, skip_group_check=True),
                    r=rkeys, w=[("ps", bc), ("ps", bd)])

        def evac():
            evac_ctr[0] += 1
            if evac_ctr[0] % 2 == 1:
                p.op("act", lambda e: e.activation(out=R.ap(0, [[512, 2], [1, 512]]), in_=ps2f(bc), func=AF.Copy),
                     r=[("ps", bc), ("ps", bd)], w=[kq("R")])
            else:
                p.op("dve", lambda e: e.tensor_copy(out=R.ap(0, [[512, 2], [1, 512]]), in_=ps2f(bc)),
                     r=[("ps", bc), ("ps", bd)], w=[kq("R")])

        bslot = lambda slot: (lambda q_: Bm.ap((slot * 8 + q_) * 64, [[1, 64]]))
        ident_l = lambda q_: identb[0:64, 0:64]
        qslot = lambda q_: Qm.ap(q_ * 64, [[1, 64]])

        for q_ in range(8):
            p.op("pe", lambda e, q_=q_: e.matmul(self.psf(bd, q_ * 64, [[1, 64]], parts=64), lhsT=identb[0:64, 0:64],
                                                 rhs=identb[0:64, 0:64], start=(q_ == 0), stop=True, skip_group_check=True),
                 r=["identb"], w=[("ps", bd)])
        for k in range(5):
            for q_ in range(8):
                rhs = (lambda q_: identb[0:64, 0:64]) if k == 0 else qslot
                p.op("pe", lambda e, k=k, q_=q_, rhs=rhs: e.matmul(
                    self.psf(bd, q_ * 64, [[1, 64]], parts=64), lhsT=Am.ap((k * 8 + q_) * 64, [[1, 64]]), rhs=rhs(q_),
                    start=False, stop=True, skip_group_check=True), r=[kq("Am"), kq("Qm"), "identb"], w=[("ps", bd)])
            if k < 4:
                for q_ in range(8):
                    p.op("pe", lambda e, k=k, q_=q_: e.matmul(
                        self.psf(ba, q_ * 64, [[1, 64]], parts=64), lhsT=Bm.ap((lv(k) * 8 + q_) * 64, [[1, 64]]),
                        rhs=Am.ap((k * 8 + q_) * 64, [[1, 64]]), start=True, stop=True), r=[kq("Bm"), kq("Am")],
                        w=[("ps", ba)])
                    if k < 3:
                        p.op("pe", lambda e, k=k, q_=q_: e.matmul(
                            self.psf(bb, q_ * 64, [[1, 64]], parts=64), lhsT=Am.ap((k * 8 + q_) * 64, [[1, 64]]),
                            rhs=Bm.ap((lv(k) * 8 + q_) * 64, [[1, 64]]), start=True, stop=True), r=[kq("Bm"), kq("Am")],
                            w=[("ps", bb)])
            yield
            if k % 2 == 0:
                p.op("dve", lambda e: e.tensor_copy(out=Qm.ap(0, [[1, 512]]), in_=self.psf(bd, 0, [[1, 512]], parts=64)),
                     r=[("ps", bd)], w=[kq("Qm")])
            else:
                p.op("act", lambda e: e.activation(out=Qm.ap(0, [[1, 512]]), in_=self.psf(bd, 0, [[1, 512]], parts=64),
                                                   func=AF.Copy), r=[("ps", bd)], w=[kq("Qm")])
            if k < 4:
                p.op("act", lambda e, k=k: e.activation(out=Am.ap((k + 1) * 512, [[1, 512]]),
                                                        in_=self.psf(ba, 0, [[1, 512]], parts=64), func=AF.Copy),
                     r=[("ps", ba)], w=[kq("Am")])
                if k < 3:
                    p.op("act", lambda e, k=k: e.activation(out=Bm.ap((k + 1) * 512, [[1, 512]]),
                                                            in_=self.psf(bb, 0, [[1, 512]], parts=64), func=AF.Copy),
                         r=[("ps", bb)], w=[kq("Bm")])
            yield
        apply(qslot, R0, [kq("Qm"), kq("R0")], True)
        yield
        evac()
        yield
        apply(ident_l, R0, ["identb", kq("R0")], True)
        apply(bslot(0), R, [kq("Bm"), kq("R")], False)
        yield
        evac()
        yield
        apply(qslot, R, [kq("Qm"), kq("R")], True)
        yield
        evac()
        yield
        rk = kq("R")
        wT = self.wT[i]
        for q_ in range(8):
            p.op("pe", lambda e, q_=q_: e.transpose(self.psb(ba, 512 + q_ * 64, [[1, 64]], parts=64),
                                                    R.ap(q_ * 128 + 64, [[1, 64]]), identb[0:64, 0:64]),
                 r=[rk, "identb"], w=[("ps", ba)])
        yield
        p.op("act", lambda e: e.activation(out=wT.ap(0, [[1, 512]]), in_=self.psb(ba, 512, [[1, 512]], parts=64), func=AF.Copy),
             r=[("ps", ba)], w=[kq("wT")])
        yield
        while self.scan_done < s:
            yield
        S, Sbf, St = self.S, self.Sbf, self.St
        vnew, o1, ob = self.vnew[i], self.o1[i], self.ob[i]
        for q_ in range(8):
            p.op("pe", lambda e, q_=q_: e.matmul(self.psf(ba, q_ * 64, [[1, 64]], parts=64), lhsT=wT.ap(q_ * 64, [[1, 64]]),
                                                 rhs=Sbf.ap(q_ * 64, [[1, 64]], parts=64), start=True, stop=True),
                 r=[kq("wT"), "Sbf"], w=[("ps", ba)])
        for h in (0, 2, 1, 3):
            for d in range(2):
                q_ = d * 4 + h
                pb = (h % 2) * 64
                p.op("pe", lambda e, d=d, h=h, q_=q_, pb=pb: e.matmul(
                    self.psf(bb, q_ * 64, [[1, 64]], parts=64), lhsT=kap(1, h, d, [[1, 64]]),
                    rhs=Sbf.ap(q_ * 64, [[1, 64]], parts=64, pbase=pb), start=True, stop=True), r=["KQVT", "Sbf"],
                    w=[("ps", bb)], rg=pb)
        yield
        p.op("dve", lambda e: e.tensor_tensor(out=vnew.ap(0, [[64, 8], [1, 64]]), in0=R.ap(0, [[128, 8], [1, 64]]),
                                              in1=self.psf(ba, 0, [[64, 8], [1, 64]], parts=64), op=ALU.subtract),
             r=[rk, ("ps", ba)], w=[kq("vnew")])
        p.op("dve", lambda e: e.tensor_tensor(out=o1.ap(0, D3), in0=self.psf(bb, 0, D3, parts=64),
                                              in1=self.sc_step(SC_EGC, sq, s, 64), op=ALU.mult),
             r=[("ps", bb), "SC"], w=[kq("o1")])
        p.op("pool", lambda e: e.tensor_tensor(out=St.ap(0, D3), in0=S.ap(0, D3),
                                               in1=self.sc_step(0, sq, s, 64, tile=self.EGL), op=ALU.mult),
             r=["S", "EGL"], w=["St"])
        yield
        for q_ in range(8):
            p.op("pe", lambda e, q_=q_: e.matmul(self.psf(bd, q_ * 64, [[1, 64]]), lhsT=KD.ap(q_ * 128, [[1, 128]]),
                                                 rhs=vnew.ap(q_ * 64, [[1, 64]]), start=True, stop=True),
                 r=[kq("KD"), kq("vnew")], w=[("ps", bd)])
        for q_ in range(8):
            p.op("pe", lambda e, q_=q_: e.matmul(self.psf(bc, q_ * 64, [[1, 64]], parts=64), lhsT=Aqk.ap(q_ * 64, [[1, 64]]),
                                                 rhs=vnew.ap(q_ * 64, [[1, 64]]), start=True, stop=True),
                 r=[kq("Aqk"), kq("vnew")], w=[("ps", bc)])
        yield
        p.op("dve", lambda e: e.tensor_tensor(out=S.ap(0, [[1, 512]]), in0=St.ap(0, [[1, 512]]),
                                              in1=self.psf(bd, 0, [[1, 512]]), op=ALU.add), r=["St", ("ps", bd)], w=["S"])
        p.op("act", lambda e: e.activation(out=Sbf.ap(0, [[1, 512]]), in_=S.ap(0, [[1, 512]]), func=AF.Copy),
             r=["S"], w=["Sbf"])
        self.scan_done = s + 1
        p.op("dve", lambda e: e.tensor_tensor(out=ob.ap(0, [[1, 512]]), in0=o1.ap(0, [[1, 512]]),
                                              in1=self.psf(bc, 0, [[1, 512]], parts=64), op=ALU.add),
             r=[kq("o1"), ("ps", bc)], w=[kq("ob")])
        yield
        for d in range(2):
            for hp in range(2):
                p.op("pe", lambda e, d=d, hp=hp: e.transpose(self.psb(ba, (d * 2 + hp) * 64, [[1, 64]]),
                                                             ob.ap((d * 4 + hp * 2) * 64, [[1, 128]]), identb[0:64, 0:64]),
                     r=[kq("ob"), "identb"], w=[("ps", ba)])
        yield
        for d in range(2):
            dst = lambda d=d: self.ocT.ap(tg[d], [[NTOK, 2], [1, 64]])
            src = lambda d=d: self.psb(ba, d * 128, [[64, 2], [1, 64]])
            if s < NCH // 2:
                p.op("act", lambda e, dst=dst, src=src: e.activation(out=dst(), in_=src(), func=AF.Copy),
                     r=[("ps", ba)], w=["ocT"])
            else:
                p.op("dve", lambda e, dst=dst, src=src: e.tensor_tensor(out=dst(), in0=dst(), in1=src(), op=ALU.add),
                     r=[("ps", ba), "ocT"], w=["ocT"])
        yield

    def gdn_finish(self, sq):
        p = self.p
        T, toff = sq["T"], sq["toff"]
        BL = min(512, T)
        sqf, rnf, oct_ = self.sqf, self.rnf, self.octmp
        for bi, t0 in enumerate(range(0, T, BL)):
            g0 = toff + t0
            cg = self.cgl[bi % 2]
            ck = ("cgl", bi % 2)
            p.op("sp", lambda e, cg=cg, g0=g0: e.dma_start(out=cg[:, :, 0:BL], in_=self.cgD.ap()[:, :, g0:g0 + BL]),
                 r=[("cgD", g0)], w=[ck], dma="ld_cgl%d" % (bi % 2))
            for hp in range(2):
                src = lambda hp=hp, g0=g0: self.ocT[:, hp, g0:g0 + BL]
                p.op("act", lambda e, src=src: e.activation(out=sqf[:, 0:BL], in_=src(), func=AF.Square), r=["ocT"], w=["sqf"])
                p.op("pe", lambda e: e.matmul(self.psf(3, 0, [[1, BL]]), lhsT=self.onesblk[:], rhs=sqf[:, 0:BL], start=True,
                                              stop=True), r=["sqf", "onesblk"], w=[("ps", 3)])
                p.op("act", lambda e: e.activation(out=rnf[:, 0:BL], in_=self.psf(3, 0, [[1, BL]]), func=AF.Ln, scale=1.0 / 64,
                                                   bias=self.epsc[:, 0:1]), r=[("ps", 3), "epsc"], w=["rnf"])
                p.op("act", lambda e: e.activation(out=rnf[:, 0:BL], in_=rnf[:, 0:BL], func=AF.Exp, scale=-0.5), r=["rnf"],
                     w=["rnf"])
                p.op("dve", lambda e, src=src: e.scalar_tensor_tensor(out=oct_[:, 0:BL], in0=src(), scalar=self.normg[:, 0:1],
                                                                      in1=rnf[:, 0:BL], op0=ALU.mult, op1=ALU.mult),
                     r=["ocT", "normg", "rnf"], w=["octmp"])
                p.op("pool", lambda e, src=src, hp=hp, cg=cg: e.tensor_tensor(out=src(), in0=oct_[:, 0:BL], in1=cg[:, hp, 0:BL],
                                                                              op=ALU.mult), r=["octmp", ck], w=["ocT"])

    def phaseC_setup(self, es, l):
        p, sb = self.p, self.sb
        self.rope_tables(es)
        self.wC = sb(es, "wC", [128, 8, 2048], BF16)
        self.wO = sb(es, "wO", [128, 8, D], BF16)
        for kc in range(8):
            p.op("pool", lambda e, kc=kc: e.dma_start(out=self.wC[:, kc, :],
                                                      in_=self.w_in.ap()[l, kc * 128:(kc + 1) * 128, 0:2048]),
                 w=["wC"], dma="ld_wC")
            p.op("pool", lambda e, kc=kc: e.dma_start(out=self.wO[:, kc, :], in_=self.w_out.ap()[l, kc * 128:(kc + 1) * 128, :]),
                 w=["wO"], dma="ld_wO")
        self.gpbc = sb(es, "gpbc", [128, 2, D])
        p.op("sp", lambda e: e.dma_start(out=self.gpbc[:].rearrange("p a b -> p (a b)"),
                                         in_=bass.AP(self.gps, l * 2 * D, [[0, 128], [1, 2 * D]])),
             r=[("gps", l)], w=["gpbc"], dma="ld_gpbc")
        self.NX = 3
        self.xin = [sb(es, "xinC%d" % i, [128, D]) for i in range(self.NX)]
        self.junk = sb(es, "junkC", [128, D], BF16)
        self.junk2 = sb(es, "junk2C", [128, 512], BF16)
        self.stat = sb(es, "statC", [128, self.NX, 4])
        self.xn = sb(es, "xnC", [128, D], BF16)
        self.hTtmp = sb(es, "hTtmpC", [128, 8, 128])
        self.hTC = [sb(es, "hTC%d" % i, [128, 8, 128], BF16) for i in range(2)]
        NR = 4
        self.NR = NR
        self.qT = [sb(es, "qT%d" % i, [64, 8, 128], BF16) for i in range(NR)]
        self.kT = [sb(es, "kT%d" % i, [64, 2, 128], BF16) for i in range(NR)]
        self.vtok = [sb(es, "vtok%d" % i, [128, 2, 64], BF16) for i in range(NR)]
        self.gA = [sb(es, "gA%d" % i, [128, 512]) for i in range(NR)]
        self.oT = [sb(es, "oT%d" % i, [128, 6, 128], BF16) for i in range(NR)]
        self.kvout = [sb(es, "kvout%d" % i, [128, 2, 128]) for i in range(2)]
        self.qr = sb(es, "qr", [128, 640], BF16)
        self.kdup = sb(es, "kdup", [128, 2, 2, 64], BF16)
        self.rt1 = sb(es, "rt1", [128, 512])
        self.rt2 = sb(es, "rt2", [128, 512])
        self.sig = [sb(es, "sig%d" % i, [128, 256]) for i in range(3)]
        self.negone = sb(es, "negone", [128, 256])
        p.op("pool", lambda e: e.memset(self.negone[:], -1.0), w=["negone"])
        self.Pm = [sb(es, "Pm%d" % i, [128, 5, 512], BF16) for i in range(2)]
        self.attst = sb(es, "attst", [128, 2, 8])
        self.oatt = sb(es, "oatt", [128, 512])
        self.oab = sb(es, "oab", [128, 512], BF16)
        self.lnst = sb(es, "lnst", [128, 8, 4])
        self.vsq = sb(es, "vsq", [128, 256])
        self.vn = sb(es, "vn", [128, 256])
        self.vg = sb(es, "vg", [128, 256], BF16)
        self.m1 = sb(es, "m1", [128, 256])
        self.ug = sb(es, "ug", [128, 256])
        self.obb = sb(es, "obb", [128, 256], BF16)
        self.ytmp = sb(es, "ytmp", [128, D])

    def silu_from_psum(self, src_fn, n, out_fn, rk, wk, slot):
        p = self.p
        sig = self.sig[slot]
        sk = ("sig", slot)
        p.op("act", lambda e: e.activation(out=sig[:, 0:n], in_=src_fn(), func=AF.Exp, scale=-1.0), r=rk, w=[sk])
        yield
        p.op("act", lambda e: e.activation(out=sig[:, 0:n], in_=sig[:, 0:n], func=AF.Ln, bias=1.0), r=[sk], w=[sk])
        p.op("act", lambda e: e.activation(out=sig[:, 0:n], in_=sig[:, 0:n], func=AF.Exp, scale=-1.0), r=[sk], w=[sk])
        yield
        p.op("dve", lambda e: e.tensor_tensor(out=out_fn(), in0=src_fn(), in1=sig[:, 0:n], op=ALU.mult), r=rk + [sk], w=wk)

    def phaseC(self, l, sq):
        NT = sq["T"] // 128
        base = self.tileC
        self.kv_done = 0
        fi, bi = 0, 0
        fg, bg = None, None
        while bi < NT:
            if fg is None and fi < NT and fi <= bi + 2:
                fg = self.c_tile_front(l, sq, fi, base + fi)
                fi += 1
            if bg is None:
                bg = self.c_tile_back(l, sq, bi, NT, base + bi)
            if fg is not None:
                try:
                    next(fg)
                except StopIteration:
                    fg = None
            for _ in range(2):
                try:
                    next(bg)
                except StopIteration:
                    bg = None
                    bi += 1
                    break
        self.tileC += NT

    def c_tile_front(self, l, sq, i, gi):
        p = self.p
        lat = sq["kind"] == "s"
        t0 = i * 128
        slot = gi % self.NX
        r = gi % self.NR
        hT = self.hTC[gi % 2]
        hk = yield from self.make_hT_gen(l, sq, t0, hT, 0, slot)
        wC = self.wC

        def proj(g, b):
            for kc in range(8):
                p.op("pe", lambda e, kc=kc: e.matmul(self.psf(b, 0, [[1, 512]]), lhsT=hT[:, kc, :],
                                                     rhs=wC[:, kc, g * 512:(g + 1) * 512], start=(kc == 0), stop=(kc == 7)),
                     r=[hk, "wC"], w=[("ps", b)])
        qr, kdup = self.qr, self.kdup
        proj(0, 0)
        proj(1, 1)
        yield
        kvo = self.kvout[gi % 2]
        if lat:
            yield from self.rope(0, 0, 8, i, lambda: qr.ap(0, [[64, 8], [1, 64]]), ["qr"])
            yield from self.rope(1, 0, 2, i, lambda: qr.ap(512, [[64, 2], [1, 64]]), ["qr"])
        else:
            p.op("act", lambda e: e.activation(out=qr[:, 0:512], in_=self.psf(0, 0, [[1, 512]]), func=AF.Copy),
                 r=[("ps", 0)], w=["qr"])
            p.op("act", lambda e: e.activation(out=qr[:, 512:640], in_=self.psf(1, 0, [[1, 128]]), func=AF.Copy),
                 r=[("ps", 1)], w=["qr"])
            p.op("act", lambda e: e.activation(out=kvo[:].rearrange("p a b -> p (a b)"), in_=self.psf(1, 0, [[1, 256]]),
                                               func=AF.Copy), r=[("ps", 1)], w=[("kvout", gi % 2)])
            for which, dst in ((0, self.nk), (1, self.nv)):
                p.op("sp", lambda e, which=which, dst=dst: e.dma_start(out=dst.ap()[sq["idx"], l, t0:t0 + 128, :],
                                                                       in_=kvo[:, which, :]),
                     r=[("kvout", gi % 2)], dma="st_kv%d" % (gi % 2))
        vt = self.vtok[r]
        p.op("act", lambda e: e.activation(out=vt[:].rearrange("p a b -> p (a b)"), in_=self.psf(1, 128, [[1, 128]]),
                                           func=AF.Copy), r=[("ps", 1)], w=[("vtok", r)])
        gA = self.gA[r]
        yield from self.silu_from_psum(lambda: self.psf(1, 256, [[1, 256]]), 256, lambda: gA[:, 0:256], [("ps", 1)],
                                       [("gA", r)], 0)
        proj(2, 0)
        proj(3, 1)
        yield
        for h in range(8):
            p.op("pe", lambda e, h=h: e.transpose(self.psb(7, h * 128, [[1, 128]], parts=64), qr[:, h * 64:(h + 1) * 64],
                                                  self.identb[:]), r=["qr", "identb"], w=[("ps", 7)])
        for kv in range(2):
            p.op("pe", lambda e, kv=kv: e.transpose(self.psb(6, kv * 128, [[1, 128]], parts=64),
                                                    qr[:, 512 + kv * 64:512 + (kv + 1) * 64], self.identb[:]),
                 r=["qr", "identb"], w=[("ps", 6)])
        yield
        qT, kT = self.qT[r], self.kT[r]
        p.op("dve", lambda e: e.tensor_copy(out=qT[:].rearrange("p a b -> p (a b)"), in_=self.psb(7, 0, [[1, 1024]], parts=64)),
             r=[("ps", 7)], w=[("qT", r)])
        p.op("act", lambda e: e.activation(out=kT[:].rearrange("p a b -> p (a b)"), in_=self.psb(6, 0, [[1, 256]], parts=64),
                                           func=AF.Copy), r=[("ps", 6)], w=[("kT", r)])
        self.kv_done = i + 1
        yield from self.silu_from_psum(lambda: self.psf(0, 0, [[1, 256]]), 256, lambda: gA[:, 256:512], [("ps", 0)],
                                       [("gA", r)], 1)
        yield from self.sgu(r)

    def rope(self, bank, off, nh, tile_i, out_fn, wk):
        p = self.p
        rt1, rt2 = self.rt1, self.rt2
        rk = [("ps", bank)]
        p.op("dve", lambda e: e.tensor_tensor(out=rt1.ap(0, [[64, nh], [1, 64]]), in0=self.psf(bank, off, [[64, nh], [1, 64]]),
                                              in1=self.ropeC.ap(tile_i * 64, [[0, nh], [1, 64]]), op=ALU.mult),
             r=rk + ["ropeC"], w=["rt1"])
        for sw in range(2):
            p.op("dve", lambda e, sw=sw: e.tensor_tensor(
                out=rt2.ap(sw * 16, [[64, nh], [32, 2], [1, 16]]),
                in0=self.psf(bank, off + (1 - sw) * 16, [[64, nh], [32, 2], [1, 16]]),
                in1=self.ropeS.ap(tile_i * 64 + sw * 16, [[0, nh], [32, 2], [1, 16]]), op=ALU.mult),
                r=rk + ["ropeS"], w=["rt2"])
        yield
        p.op("pool", lambda e: e.tensor_tensor(out=out_fn(), in0=rt1.ap(0, [[64, nh], [1, 64]]),
                                               in1=rt2.ap(0, [[64, nh], [1, 64]]), op=ALU.add), r=["rt1", "rt2"], w=wk)
        yield

    def sgu(self, r):
        p = self.p
        st, vsq, vn, vg, m1, ug, obb = self.lnst, self.vsq, self.vn, self.vg, self.m1, self.ug, self.obb
        bv = lambda: self.psf(1, 0, [[64, 4], [1, 64]])
        p.op("dve", lambda e: e.tensor_reduce(out=st[:, 0:4, 0], in_=bv(), axis=AX.X, op=ALU.add), r=[("ps", 1)], w=["lnst"])
        p.op("act", lambda e: e.activation(out=vsq[:], in_=self.psf(1, 0, [[1, 256]]), func=AF.Square), r=[("ps", 1)], w=["vsq"])
        yield
        p.op("dve", lambda e: e.tensor_reduce(out=st[:, 0:4, 1], in_=vsq.ap(0, [[64, 4], [1, 64]]), axis=AX.X, op=ALU.add),
             r=["vsq"], w=["lnst"])
        p.op("dve", lambda e: e.tensor_scalar(out=st[:, 0:4, 2], in0=st[:, 0:4, 0], scalar1=1.0 / 64, scalar2=None, op0=ALU.mult),
             r=["lnst"], w=["lnst"])
        p.op("dve", lambda e: e.tensor_tensor(out=st[:, 4:8, 0], in0=st[:, 0:4, 2], in1=st[:, 0:4, 2], op=ALU.mult),
             r=["lnst"], w=["lnst"])
        p.op("dve", lambda e: e.scalar_tensor_tensor(out=st[:, 4:8, 1], in0=st[:, 0:4, 1], scalar=1.0 / 64, in1=st[:, 4:8, 0],
                                                     op0=ALU.mult, op1=ALU.subtract), r=["lnst"], w=["lnst"])
        yield
        p.op("act", lambda e: e.activation(out=st[:, 4:8, 2], in_=st[:, 4:8, 1], func=AF.Ln, bias=self.epsc[:, 0:1]),
             r=["lnst", "epsc"], w=["lnst"])
        p.op("act", lambda e: e.activation(out=st[:, 4:8, 3], in_=st[:, 4:8, 2], func=AF.Exp, scale=-0.5), r=["lnst"], w=["lnst"])
        p.op("dve", lambda e: e.tensor_tensor(out=vn.ap(0, [[64, 4], [1, 64]]), in0=bv(), in1=st.ap(2, [[4, 4], [0, 64]]),
                                              op=ALU.subtract), r=[("ps", 1), "lnst"], w=["vn"])
        yield
        p.op("pool", lambda e: e.tensor_tensor(out=vn.ap(0, [[64, 4], [1, 64]]), in0=vn.ap(0, [[64, 4], [1, 64]]),
                                               in1=st.ap(4 * 4 + 3, [[4, 4], [0, 64]]), op=ALU.mult), r=["vn", "lnst"], w=["vn"])
        p.op("pool", lambda e: e.tensor_tensor(out=vn[:], in0=vn[:], in1=self.lngb[:, 0, :], op=ALU.mult), r=["vn", "lngb"],
             w=["vn"])
        p.op("pool", lambda e: e.tensor_tensor(out=vg[:], in0=vn[:], in1=self.lngb[:, 1, :], op=ALU.add), r=["vn", "lngb"],
             w=["vg"])
        yield from self.silu_from_psum(lambda: self.psf(1, 256, [[1, 256]]), 256, lambda: ug[:], [("ps", 1)], ["ug"], 2)
        p.op("dve", lambda e: e.tensor_tensor(out=ug[:], in0=ug[:], in1=self.psf(0, 256, [[1, 256]]), op=ALU.mult),
             r=["ug", ("ps", 0)], w=["ug"])
        for g in range(4):
            p.op("pe", lambda e, g=g: e.matmul(self.psf(5, g * 64, [[1, 64]]), lhsT=self.WsT[:, g, :],
                                               rhs=vg[:, g * 64:(g + 1) * 64], start=True, stop=True), r=["vg", "WsT"],
                 w=[("ps", 5)])
        yield
        p.op("dve", lambda e: e.tensor_tensor(out=m1.ap(0, [[64, 4], [1, 64]]), in0=self.psf(5, 0, [[64, 4], [1, 64]]),
                                              in1=self.bsT.ap(0, [[1, 4], [0, 64]]), op=ALU.add), r=[("ps", 5), "bsT"], w=["m1"])
        yield
        p.op("pool", lambda e: e.tensor_tensor(out=obb[:], in0=m1[:], in1=ug[:], op=ALU.mult), r=["m1", "ug"], w=["obb"])
        yield
        for hp in range(2):
            p.op("pe", lambda e, hp=hp: e.transpose(self.psb(6, 256 + hp * 128, [[1, 128]]), obb[:, hp * 128:(hp + 1) * 128],
                                                    self.identb[:]), r=["obb", "identb"], w=[("ps", 6)])
        yield
        oT = self.oT[r]
        p.op("act", lambda e: e.activation(out=oT[:, 4:6, :].rearrange("p a b -> p (a b)"), in_=self.psb(6, 256, [[1, 256]]),
                                           func=AF.Copy), r=[("ps", 6)], w=[("oTb", r)])
        yield

    def c_tile_back(self, l, sq, i, NT, gi):
        p = self.p
        lat = sq["kind"] == "s"
        NR = self.NR
        r = gi % NR
        while self.kv_done < min(i + 2, NT):
            yield
        blocks = []
        if lat:
            if i >= 1:
                blocks.append(("t", (gi - 1) % NR, 0))
            blocks.append(("t", r, None))
            if i + 1 < NT:
                blocks.append(("t", (gi + 1) % NR, 1))
            blocks.append(("c", 0, None))
            blocks.append(("c", 1, None))
        else:
            for j in range(NT):
                blocks.append(("t", (gi - i + j) % NR, None))
        qT = self.qT[r]
        nb = len(blocks)
        for g in range(2):
            Pg = self.Pm[g]
            for bi, (kind, idx, msk) in enumerate(blocks):
                b = 2 + (bi % 2)
                for hl in range(4):
                    h = g * 4 + hl
                    if kind == "t":
                        kt = self.kT[idx]
                        lhs = lambda kt=kt: kt[:, g, :]
                        rk = [("kT", idx)]
                    else:
                        lhs = lambda idx=idx: self.kTctx[:, g, idx * 128:(idx + 1) * 128]
                        rk = ["kTctx"]
                    p.op("pe", lambda e, hl=hl, h=h, lhs=lhs, b=b: e.matmul(
                        self.psf(b, hl * 128, [[1, 128]]), lhsT=lhs(), rhs=qT[:, h, :], start=True, stop=True),
                        r=rk + [("qT", r)], w=[("ps", b)])
                yield
                p.op("act", lambda e, bi=bi, b=b, Pg=Pg: e.activation(out=Pg[:, bi, :], in_=self.psf(b, 0, [[1, 512]]),
                                                                      func=AF.Exp, scale=0.125), r=[("ps", b)],
                     w=[("Pm", g, bi)])
                if msk is not None:
                    p.op("pool", lambda e, bi=bi, msk=msk, Pg=Pg: e.tensor_tensor(
                        out=Pg.ap(bi * 512, [[128, 4], [1, 128]]), in0=Pg.ap(bi * 512, [[128, 4], [1, 128]]),
                        in1=self.amask.ap(msk * 128, [[0, 4], [1, 128]]), op=ALU.mult), r=[("Pm", g, bi), "amask"],
                        w=[("Pm", g, bi)])
            yield
            for hl in range(4):
                h = g * 4 + hl
                for bi, (kind, idx, msk) in enumerate(blocks):
                    if kind == "t":
                        vt = self.vtok[idx]
                        rhs = lambda vt=vt: vt[:, g, :]
                        rk = [("vtok", idx)]
                    else:
                        rhs = lambda idx=idx: self.vctx[:, idx, g, :]
                        rk = ["vctx"]
                    p.op("pe", lambda e, h=h, hl=hl, bi=bi, rhs=rhs, Pg=Pg: e.matmul(
                        self.psf(4, h * 64, [[1, 64]]), lhsT=Pg[:, bi, hl * 128:(hl + 1) * 128], rhs=rhs(), start=(bi == 0),
                        stop=(bi == nb - 1)), r=rk + [("Pm", g, bi)], w=[("ps", 4)])
                for bi in range(nb):
                    p.op("pe", lambda e, h=h, hl=hl, bi=bi, Pg=Pg: e.matmul(
                        self.psf(5, 256 + h, [[1, 1]]), lhsT=Pg[:, bi, hl * 128:(hl + 1) * 128], rhs=self.onecol[:, 0:1],
                        start=(bi == 0), stop=(bi == nb - 1)), r=[("Pm", g, bi), "onecol"], w=[("ps", 5)])
                yield
        st = self.attst
        p.op("dve", lambda e: e.tensor_tensor(out=st[:, 0, :], in0=self.psf(5, 256, [[1, 8]]), in1=self.esink[:], op=ALU.add),
             r=[("ps", 5), "esink"], w=["attst"])
        p.op("dve", lambda e: e.reciprocal(out=st[:, 1, :], in_=st[:, 0, :]), r=["attst"], w=["attst"])
        oatt, oab, gA = self.oatt, self.oab, self.gA[r]
        p.op("dve", lambda e: e.tensor_tensor(out=oatt.ap(0, [[64, 8], [1, 64]]), in0=self.psf(4, 0, [[64, 8], [1, 64]]),
                                              in1=st.ap(8, [[1, 8], [0, 64]]), op=ALU.mult), r=[("ps", 4), "attst"], w=["oatt"])
        yield
        p.op("pool", lambda e: e.tensor_tensor(out=oab[:], in0=oatt[:], in1=gA[:], op=ALU.mult), r=["oatt", ("gA", r)],
             w=["oab"])
        yield
        for hp in range(4):
            p.op("pe", lambda e, hp=hp: e.transpose(self.psb(4, hp * 128, [[1, 128]]), oab[:, hp * 128:(hp + 1) * 128],
                                                    self.identb[:]), r=["oab", "identb"], w=[("ps", 4)])
        yield
        oT = self.oT[r]
        p.op("act", lambda e: e.activation(out=oT[:, 0:4, :].rearrange("p a b -> p (a b)"), in_=self.psb(4, 0, [[1, 512]]),
                                           func=AF.Copy), r=[("ps", 4)], w=[("oTa", r)])
        yield
        g0 = sq["toff"] + i * 128
        wO = self.wO
        for n in range(2):
            b = 2 + n
            for kc in range(8):
                if kc < 6:
                    lhs = lambda kc=kc: oT[:, kc, :]
                    rk = [("oTa", r), ("oTb", r)]
                else:
                    lhs = lambda kc=kc: self.ocT[:, kc - 6, g0:g0 + 128]
                    rk = ["ocT"]
                p.op("pe", lambda e, kc=kc, n=n, b=b, lhs=lhs: e.matmul(self.psf(b, 0, [[1, 512]]), lhsT=lhs(),
                                                                        rhs=wO[:, kc, n * 512:(n + 1) * 512],
                                                                        start=(kc == 0), stop=(kc == 7)),
                     r=rk + ["wO"], w=[("ps", b)])
        yield
        slot = gi % self.NX
        stt, junk = self.stat, self.junk2
        sk = ("stat", slot)
        for n in range(2):
            p.op("act", lambda e, n=n: e.activation(out=junk[:, 0:512], in_=self.psf(2 + n, 0, [[1, 512]]), func=AF.Square,
                                                    accum_out=stt[:, slot, n:n + 1]), r=[("ps", 2 + n)], w=["junk2", sk])
        yield
        p.op("dve", lambda e: e.tensor_tensor(out=stt[:, slot, 3:4], in0=stt[:, slot, 0:1], in1=stt[:, slot, 1:2], op=ALU.add),
             r=[sk], w=[sk])
        yield
        p.op("act", lambda e: e.activation(out=stt[:, slot, 1:2], in_=stt[:, slot, 3:4], func=AF.Ln, scale=1.0 / D,
                                           bias=self.epsc[:, 0:1]), r=[sk, "epsc"], w=[sk])
        p.op("act", lambda e: e.activation(out=stt[:, slot, 2:3], in_=stt[:, slot, 1:2], func=AF.Exp, scale=-0.5), r=[sk], w=[sk])
        yield
        c = sq["cond"]
        ytmp = self.ytmp
        xt = self.xin[slot]
        for n in range(2):
            p.op("dve", lambda e, n=n: e.scalar_tensor_tensor(out=ytmp[:, n * 512:(n + 1) * 512],
                                                              in0=self.psf(2 + n, 0, [[1, 512]]), scalar=stt[:, slot, 2:3],
                                                              in1=self.gpbc[:, c, n * 512:(n + 1) * 512], op0=ALU.mult,
                                                              op1=ALU.mult), r=[("ps", 2 + n), sk, "gpbc"], w=["ytmp"])
        yield
        p.op("pool", lambda e: e.tensor_tensor(out=xt[:], in0=ytmp[:], in1=xt[:], op=ALU.add), r=["ytmp", ("xin", slot)],
             w=[("xin", slot)])
        yield
        dst = self.x_dst(l, sq, i * 128)
        p.op("pool", lambda e: e.dma_start(out=dst, in_=xt[:]), r=[("xin", slot)], w=[self.xkey(l + 1, sq, i * 128)],
             dma="st_y%d" % slot)
        yield


def build_program(dbg=False):
    nc0 = bass.Bass("TRN2", target_bir_lowering=False)
    p0 = Prog(nc0, None)
    b0 = Builder(nc0, p0, dbg)
    b0.build()
    nc = bass.Bass("TRN2", target_bir_lowering=False)
    p1 = Prog(nc, p0.need)
    b1 = Builder(nc, p1, dbg)
    b1.build()
    return nc, b1, p1


_CACHE = {}


def make_in_maps(x_prompt, x_sample, cache_k, cache_v, state_delta, c, c_ctx, w_mod, b_mod, g_pre, w_in, attn_sink,
                 sgu_ln_g, sgu_ln_b, sgu_w, sgu_b, gdn_conv_w, gdn_a_log, gdn_dt_bias, gdn_norm_g, g_post, w_out):
    f = lambda a: np.ascontiguousarray(np.asarray(a, dtype=np.float32))
    x_prompt, x_sample, cache_k, cache_v, state_delta = map(f, (x_prompt, x_sample, cache_k, cache_v, state_delta))
    c, c_ctx = f(c), f(c_ctx)
    shared = dict(w_mod=f(w_mod), b_mod=f(b_mod), g_pre=f(g_pre), w_in=f(w_in), sink=f(attn_sink), ln_g=f(sgu_ln_g),
                  ln_b=f(sgu_ln_b), sgu_w=f(sgu_w), sgu_b=f(sgu_b), conv_w=f(gdn_conv_w),
                  a_log=f(gdn_a_log).reshape(2, 8), dt_bias=f(gdn_dt_bias).reshape(2, 8), norm_g=f(gdn_norm_g),
                  g_post=f(g_post), w_out=f(w_out))
    in_maps = []
    for core in range(8):
        m = dict(shared)
        m["xp"] = x_prompt[core * 4:(core + 1) * 4]
        m["xs"] = x_sample[core]
        m["ck"] = cache_k[core].reshape(2, 256, 128)
        m["cv"] = cache_v[core].reshape(2, 256, 128)
        m["sd"] = state_delta[core]
        m["cvec"] = np.ascontiguousarray(np.stack([c_ctx, c[core]], 0))
        in_maps.append(m)
    return in_maps


def kernel(**inputs):
    in_maps = make_in_maps(**inputs)
    if "nc" not in _CACHE:
        _CACHE["nc"] = build_program(False)[0]
    nc = _CACHE["nc"]
    res = run_bass_kernel_spmd(nc, in_maps, core_ids=list(range(8)))
    R = res.results
    yp = np.concatenate([R[i]["yp"] for i in range(8)], 0)
    ys = np.stack([R[i]["ys"] for i in range(8)], 0)
    nk = np.concatenate([R[i]["nk"] for i in range(8)], 0).reshape(32, 2, 256, 2, 64)
    nv = np.concatenate([R[i]["nv"] for i in range(8)], 0).reshape(32, 2, 256, 2, 64)
    ns = np.concatenate([R[i]["ns"] for i in range(8)], 0)
    return (yp.astype(np.float32), ys.astype(np.float32), nk.astype(np.float32), nv.astype(np.float32),
            ns.astype(np.float32))
```
